# Optimizing a Trainium2 kernel written in Bass

```python
import jax
import jax.numpy as jnp
from jax import lax
import numpy as np

D_MODEL = 1024
BATCH = 16
SEQ = 256
DEPTH = 1
DEC_BATCH = 2
DEC_SEQ = 4096
PAST_LEN = 512

GRID_W = 64
D_MIX = D_MODEL
D_GMLP = D_MIX // 2
D_MLSTM = D_MIX - D_GMLP
GMLP_GROUPS = 4
GMLP_GW = D_GMLP // GMLP_GROUPS
GMLP_CHUNK = 128
MLSTM_HEADS = 4
MLSTM_HD = D_MLSTM // MLSTM_HEADS
MLSTM_CHUNK = 128
CONV_W = 3
N_DIR = 2
N_GATE_COLS = N_DIR * 2 * MLSTM_HEADS
D_FF = -(-(8 * D_MODEL) // (3 * 256)) * 256

OFF_U = 0
OFF_V = OFF_U + D_GMLP
OFF_Q = OFF_V + D_GMLP
OFF_K = OFF_Q + D_MLSTM
OFF_VV = OFF_K + D_MLSTM
OFF_O = OFF_VV + D_MLSTM
OFF_G = OFF_O + D_MLSTM
D_IN = OFF_G + N_GATE_COLS

EPS = 1e-6
NEG = -1e30

kernel_name = 'hymba_gmlp_mlstm_diffusion_step'


def rmsnorm(x, g):
    xf = x.astype(jnp.float32)
    y = xf * lax.rsqrt(jnp.mean(xf * xf, axis=-1, keepdims=True) + EPS)
    return (y * g.astype(jnp.float32)).astype(x.dtype)


def short_conv(x, w, b, rows):
    B, T, C = x.shape
    xs = x if rows is None else x.reshape(B * rows, GRID_W, C)
    n = xs.shape[1]
    pad = CONV_W // 2
    xp = jnp.pad(xs, ((0, 0), (pad, pad), (0, 0)))
    y = b + sum(xp[:, j:j + n] * w[j] for j in range(CONV_W))
    return y.reshape(B, T, C)


def chunk_gmlp(u, v, w_s, b_s, g_v):
    B, T, _ = u.shape
    nc = T // GMLP_CHUNK
    u = jax.nn.gelu(u)
    vg = rmsnorm(jax.nn.gelu(v).reshape(B, T, GMLP_GROUPS, GMLP_GW), g_v)
    vg = vg.reshape(B, nc, GMLP_CHUNK, GMLP_GROUPS, GMLP_GW)
    mixed = jnp.einsum('gts,bcsgd->bctgd', w_s, vg) + jnp.swapaxes(b_s, 0, 1)[:, :, None]
    return u * mixed.reshape(B, T, D_GMLP)


def mlstm_scan(q, k, v, ig, lf, C0, n0, m0):
    B, T, H, HD = q.shape
    L = MLSTM_CHUNK
    nc = T // L

    def to_chunks(a):
        return jnp.swapaxes(a.reshape((B, nc, L) + a.shape[2:]), 0, 1)

    tril = jnp.tril(jnp.ones((L, L), dtype=bool))

    def step(carry, xs):
        C, n, m = carry
        qc, kc, vc, igc, lfc = xs
        b = jnp.swapaxes(jnp.cumsum(lfc, axis=1), 1, 2)
        i_ = jnp.swapaxes(igc, 1, 2)
        logw = jnp.where(tril, b[..., :, None] - b[..., None, :] + i_[..., None, :], NEG)
        a = b + m[..., None]
        mj = jnp.maximum(a, jnp.max(logw, axis=-1))
        w = jnp.exp(logw - mj[..., None])
        inter = jnp.exp(a - mj)
        s = jnp.einsum('bjhd,bshd->bhjs', qc, kc) * w
        num = (jnp.einsum('bhjs,bshd->bhjd', s, vc)
               + inter[..., None] * jnp.einsum('bjhk,bhkv->bhjv', qc, C))
        den = jnp.sum(s, axis=-1) + inter * jnp.einsum('bjhk,bhk->bhj', qc, n)
        h = num / jnp.maximum(jnp.abs(den), jnp.exp(-mj))[..., None]
        bL = b[..., -1]
        g = bL[..., None] - b + i_
        m_new = jnp.maximum(bL + m, jnp.max(g, axis=-1))
        wc = jnp.exp(g - m_new[..., None])
        decay = jnp.exp(bL + m - m_new)
        C_new = decay[..., None, None] * C + jnp.einsum('bhs,bshk,bshv->bhkv', wc, kc, vc)
        n_new = decay[..., None] * n + jnp.einsum('bhs,bshk->bhk', wc, kc)
        return (C_new, n_new, m_new), jnp.swapaxes(h, 1, 2)

    xs = (to_chunks(q), to_chunks(k), to_chunks(v), to_chunks(ig), to_chunks(lf))
    (C, n, m), hs = lax.scan(step, (C0, n0, m0), xs)
    h = jnp.swapaxes(hs, 0, 1).reshape(B, T, H, HD)
    return h, (C, n, m)


def mlstm_mixer(qk, v, o, gates, conv_w, conv_b, g_h, C0, n0, m0, rows):
    B, T, _ = v.shape
    f32 = jnp.float32
    qk = jax.nn.silu(short_conv(qk, conv_w, conv_b, rows))

    def heads(a):
        return a.astype(f32).reshape(B, T, MLSTM_HEADS, MLSTM_HD)

    qh = heads(qk[..., :D_MLSTM])
    kh = heads(qk[..., D_MLSTM:]) * (MLSTM_HD ** -0.5)
    vh = heads(v)
    g = gates.astype(f32).reshape(B, T, N_DIR, 2, MLSTM_HEADS)
    ig = g[:, :, :, 0]
    lf = jax.nn.log_sigmoid(g[:, :, :, 1])
    C0 = C0.astype(f32)
    n0 = n0.astype(f32)
    m0 = m0.astype(f32)
    h_f, st_f = mlstm_scan(qh, kh, vh, ig[:, :, 0], lf[:, :, 0], C0[:, 0], n0[:, 0], m0[:, 0])

    def rev(a):
        return jnp.flip(a, axis=1)

    h_b, st_b = mlstm_scan(rev(qh), rev(kh), rev(vh), rev(ig[:, :, 1]), rev(lf[:, :, 1]),
                           C0[:, 1], n0[:, 1], m0[:, 1])
    h = rmsnorm(h_f + rev(h_b), g_h).reshape(B, T, D_MLSTM).astype(v.dtype)
    out = h * jax.nn.sigmoid(o)
    state = tuple(jnp.stack([sf, sb], axis=1) for sf, sb in zip(st_f, st_b))
    return out, state


def trunk_layer(x, mod, C0, n0, m0, rows, p):
    sh1, sc1, ga1, sh2, sc2, ga2 = jnp.split(mod[:, None, :], 6, axis=-1)
    h = rmsnorm(x, p['g_norm1']) * (1 + sc1) + sh1
    z = h @ p['w_in']
    a_out = chunk_gmlp(z[..., OFF_U:OFF_V], z[..., OFF_V:OFF_Q], p['w_s'], p['b_s'], p['g_v'])
    b_out, state = mlstm_mixer(z[..., OFF_Q:OFF_VV], z[..., OFF_VV:OFF_O], z[..., OFF_O:OFF_G],
                               z[..., OFF_G:] + p['b_gate'], p['conv_w'], p['conv_b'], p['g_h'],
                               C0, n0, m0, rows)
    x = x + ga1 * (jnp.concatenate([a_out, b_out], axis=-1) @ p['w_out'])
    h = rmsnorm(x, p['g_norm2']) * (1 + sc2) + sh2
    x = x + ga2 * ((jax.nn.silu(h @ p['w1']) * (h @ p['w3'])) @ p['w2'])
    return x, state


def setup_inputs(seed: int = 0) -> dict:
    key = jax.random.key(seed)
    ks = jax.random.split(key, 26)
    f32 = jnp.float32
    L = DEPTH
    H = MLSTM_HEADS
    HD = MLSTM_HD

    def nrm(k, shape, s):
        return jax.random.normal(k, shape, f32) * s

    gate_offset = jnp.tile(jnp.repeat(jnp.array([0.0, 3.0], f32), H), N_DIR)
    return {
        'x_prompt': nrm(ks[0], (BATCH, SEQ, D_MODEL), 1.0),
        'x_sample': nrm(ks[1], (DEC_BATCH, DEC_SEQ, D_MODEL), 1.0),
        'state_C': nrm(ks[2], (DEC_BATCH, L, N_DIR, H, HD, HD), 0.1),
        'state_n': nrm(ks[3], (DEC_BATCH, L, N_DIR, H, HD), 0.5),
        'state_m': nrm(ks[4], (DEC_BATCH, L, N_DIR, H), 0.5),
        'c': nrm(ks[5], (DEC_BATCH, D_MODEL), 1.0),
        'c_ctx': nrm(ks[6], (D_MODEL,), 1.0),
        'w_ada': nrm(ks[7], (L, D_MODEL, 6 * D_MODEL), 0.5 * D_MODEL ** -0.5),
        'b_ada': nrm(ks[8], (L, 6 * D_MODEL), 0.02),
        'g_norm1': 1.0 + nrm(ks[9], (L, D_MODEL), 0.02),
        'w_in': nrm(ks[10], (L, D_MODEL, D_IN), D_MODEL ** -0.5),
        'b_gate': gate_offset + nrm(ks[11], (L, N_GATE_COLS), 0.1),
        'w_s': nrm(ks[12], (L, GMLP_GROUPS, GMLP_CHUNK, GMLP_CHUNK), GMLP_CHUNK ** -0.5),
        'b_s': 1.0 + nrm(ks[13], (L, GMLP_GROUPS, GMLP_CHUNK), 0.1),
        'g_v': 1.0 + nrm(ks[14], (L, GMLP_GROUPS, GMLP_GW), 0.02),
        'conv_w': nrm(ks[15], (L, CONV_W, 2 * D_MLSTM), CONV_W ** -0.5),
        'conv_b': nrm(ks[16], (L, 2 * D_MLSTM), 0.02),
        'g_h': 1.0 + nrm(ks[17], (L, H, HD), 0.02),
        'w_out': nrm(ks[18], (L, D_MIX, D_MODEL), D_MIX ** -0.5),
        'g_norm2': 1.0 + nrm(ks[19], (L, D_MODEL), 0.02),
        'w1': nrm(ks[20], (L, D_MODEL, D_FF), D_MODEL ** -0.5),
        'w3': nrm(ks[21], (L, D_MODEL, D_FF), D_MODEL ** -0.5),
        'w2': nrm(ks[22], (L, D_FF, D_MODEL), D_FF ** -0.5),
        'g_final': 1.0 + nrm(ks[23], (D_MODEL,), 0.02),
    }


def reference(x_prompt, x_sample, state_C, state_n, state_m, c, c_ctx, w_ada, b_ada, g_norm1,
              w_in, b_gate, w_s, b_s, g_v, conv_w, conv_b, g_h, w_out, g_norm2, w1, w3, w2,
              g_final):
    f32 = jnp.float32
    B = x_prompt.shape[0]
    rows = x_sample.shape[1] // GRID_W
    zC = jnp.zeros((B, N_DIR, MLSTM_HEADS, MLSTM_HD, MLSTM_HD), f32)
    zn = jnp.zeros((B, N_DIR, MLSTM_HEADS, MLSTM_HD), f32)
    zm = jnp.zeros((B, N_DIR, MLSTM_HEADS), f32)
    xp = x_prompt
    xs = x_sample
    new_C, new_n, new_m = [], [], []
    for l in range(DEPTH):
        p = {'g_norm1': g_norm1[l], 'w_in': w_in[l], 'b_gate': b_gate[l], 'w_s': w_s[l],
             'b_s': b_s[l], 'g_v': g_v[l], 'conv_w': conv_w[l], 'conv_b': conv_b[l],
             'g_h': g_h[l], 'w_out': w_out[l], 'g_norm2': g_norm2[l], 'w1': w1[l],
             'w3': w3[l], 'w2': w2[l]}
        mod_ctx = (jax.nn.silu(c_ctx) @ w_ada[l] + b_ada[l])[None]
        xp, (Cp, npr, mp) = trunk_layer(xp, mod_ctx, zC, zn, zm, None, p)
        new_C.append(Cp)
        new_n.append(npr)
        new_m.append(mp)
        mod_lat = jax.nn.silu(c) @ w_ada[l] + b_ada[l]
        xs, _ = trunk_layer(xs, mod_lat, state_C[:, l], state_n[:, l], state_m[:, l], rows, p)
    y_prompt = rmsnorm(xp, g_final)
    y_sample = rmsnorm(xs, g_final)
    new_C = jnp.stack(new_C, axis=1)
    new_n = jnp.stack(new_n, axis=1)
    new_m = jnp.stack(new_m, axis=1)
    return (y_prompt, y_sample, new_C, new_n, new_m)
```

```python
import numpy as np
from contextlib import ExitStack
import concourse.bass as bass
import concourse.mybir as mybir
from concourse.bass_utils import run_bass_kernel_spmd

F32 = mybir.dt.float32
BF16 = mybir.dt.bfloat16
AF = mybir.ActivationFunctionType
ALU = mybir.AluOpType
AX = mybir.AxisListType

D = 1024
DIN = 3088
DFF = 2816
NFF = DFF // 128
OFF_U, OFF_V, OFF_Q, OFF_K, OFF_VV, OFF_O, OFF_G = 0, 512, 1024, 1536, 2048, 2560, 3072
EPS = 1e-6
NEG = -1e30
LNS = float(-0.5 * np.log(128.0))
NT_P = 4
NT_S = 8
NT = NT_P + NT_S
NFOR = 24
import os
STAGE = float(os.environ.get("KSTAGE", "99"))


class StageExit(Exception):
    pass


DEAD = [False]
STRICT = False


def stage(n):
    if STAGE <= n:
        DEAD[0] = True


class Eng:
    def __init__(self, h, name):
        self.h = h
        self.name = name
        self.sem = None
        self.n = 0
        self.waited = {}


class Buf:
    def __init__(self, name, t):
        self.name = name
        self.t = t
        self.w = {}
        self.r = {}
        self.sem = None
        self.nd = 0
        self.psum = False

    def __getitem__(self, idx):
        return self.t[idx]


class K:
    def __init__(self, nc, es):
        self.nc = nc
        self.es = es
        self.PE = Eng(nc.tensor, "pe")
        self.ACT = Eng(nc.scalar, "act")
        self.DVE = Eng(nc.vector, "dve")
        self.POOL = Eng(nc.gpsimd, "pool")
        self.SP = Eng(nc.sync, "sp")
        self.engs = [self.PE, self.ACT, self.DVE, self.POOL, self.SP]
        for e in self.engs:
            e.sem = es.enter_context(nc.semaphore("s_" + e.name))
        self.nsem = 5
        self.final = []
        self.bufs = []

    def sb(self, es, name, shape, dt):
        b = Buf(name, es.enter_context(self.nc.sbuf_tensor(name, shape, dt)))
        self.bufs.append(b)
        return b

    def ps(self, es, name, shape, dt):
        b = Buf(name, es.enter_context(self.nc.psum_tensor(name, shape, dt)))
        b.psum = True
        self.bufs.append(b)
        return b

    def _need(self, e, toks):
        for key, (sem, val) in toks:
            if e.waited.get(id(sem), 0) >= val:
                continue
            e.h.wait_ge(sem, val)
            e.waited[id(sem)] = val

    def _deps(self, e, R, W):
        toks = []
        for b in R:
            for key, tok in b.w.items():
                if key == e.name and e is self.PE:
                    continue
                toks.append((key, tok))
            if b.psum:
                for key, tok in b.r.items():
                    if key != e.name:
                        toks.append((key, tok))
        for b in W:
            for key, tok in b.w.items():
                if key == e.name and (e is self.PE or not STRICT):
                    continue
                toks.append((key, tok))
            for key, tok in b.r.items():
                if key == e.name and (e is self.PE or not STRICT):
                    continue
                toks.append((key, tok))
        return toks

    def op(self, e, fn, R=(), W=()):
        if DEAD[0]:
            return None
        self._need(e, self._deps(e, R, W))
        inst = fn()
        inst.then_inc(e.sem, 1)
        e.n += 1
        tok = (e.sem, e.n)
        for b in R:
            b.r[e.name] = tok
        for b in W:
            b.w[e.name] = tok
        return tok

    def group(self, fns, R=(), W=()):
        e = self.PE
        if DEAD[0]:
            return None
        self._need(e, self._deps(e, R, W))
        inst = None
        for fn in fns:
            inst = fn()
        inst.then_inc(e.sem, 1)
        e.n += 1
        tok = (e.sem, e.n)
        for b in R:
            b.r[e.name] = tok
        for b in W:
            b.w[e.name] = tok
        return tok

    def dma(self, q, out, in_, R=(), W=(), final=False, **kw):
        owner = W[0] if len(W) else R[0]
        if DEAD[0]:
            return None
        if owner.sem is None:
            owner.sem = self.es.enter_context(self.nc.semaphore("d_" + owner.name))
            self.nsem += 1
        toks = []
        okey = "dma:" + owner.name
        for b in R:
            toks += list(b.w.items())
        for b in W:
            toks += [kv for kv in b.w.items() if kv[0] != okey] + list(b.r.items())
        self._need(q, toks)
        inst = q.h.dma_start(out=out, in_=in_, **kw)
        inst.then_inc(owner.sem, 16)
        owner.nd += 1
        tok = (owner.sem, 16 * owner.nd)
        key = "dma:" + owner.name
        for b in R:
            b.r[key] = tok
        for b in W:
            b.w[key] = tok
        if final:
            self.final.append((key, tok))
        return tok

    def view(self, name, ap, toks=None):
        b = Buf(name, ap)
        if toks:
            b.w.update(toks)
            b.r.update(toks)
        self.bufs.append(b)
        return b

    def all_tokens(self, bufs):
        toks = {}
        for e in self.engs:
            if e.n:
                toks[e.name] = (e.sem, e.n)
        for b in bufs:
            for d in (b.w, b.r):
                for key, tok in d.items():
                    if key.startswith("dma:"):
                        toks[key] = tok
        return toks

    def finish(self):
        self._need(self.SP, self.final)
        toks = [(e.name, (e.sem, e.n)) for e in self.engs if e.n and e is not self.SP]
        self._need(self.SP, toks)


def build_nc(dbg_names=()):
    nc = bass.Bass("TRN2", target_bir_lowering=False)
    DEAD[0] = False

    def din(name, shape):
        return nc.dram_tensor(name, list(shape), F32, kind="ExternalInput").ap()

    def dout(name, shape):
        return nc.dram_tensor(name, list(shape), F32, kind="ExternalOutput").ap()

    xp = din("xp", [NT_P * 128, D])
    xs = din("xs", [NT_S * 128, D])
    xf = din("xf", [NFOR * 128, D])
    cvec = din("cvec", [2, D])
    stC = din("stC", [2, 4, 128, 128])
    stn = din("stn", [2, 4, 128])
    stm = din("stm", [1, 8])
    flg = din("flg", [1, 6])
    consts = din("consts", [128, 512])
    w_ada = din("w_ada", [D, 6 * D])
    b_ada = din("b_ada", [6 * D])
    g_norm1 = din("g_norm1", [D])
    w_in = din("w_in", [D, DIN])
    b_gate = din("b_gate", [1, 16])
    w_s = din("w_s", [4, 128, 128])
    b_s = din("b_s", [1, 512])
    g_v = din("g_v", [1, 512])
    conv_w = din("conv_w", [3, 1024])
    conv_b = din("conv_b", [1024])
    g_h = din("g_h", [1, 512])
    w_out = din("w_out", [D, D])
    g_norm2 = din("g_norm2", [D])
    w1 = din("w1", [D, DFF])
    w3 = din("w3", [D, DFF])
    w2 = din("w2", [DFF, D])
    g_final = din("g_final", [1, D])

    yp = dout("yp", [NT_P * 128, D])
    ys = dout("ys", [NT_S * 128, D])
    nC = dout("nC", [2, 2, 4, 128, 128])
    nn = dout("nn", [2, 2, 4, 128])
    nm = dout("nm", [2, 8])
    dbg = {}

    es0 = ExitStack()
    with es0:
        k = K(nc, es0)
        PE, ACT, DVE, POOL, SP = k.PE, k.ACT, k.DVE, k.POOL, k.SP
        pe, act, dve, pool, sp = nc.tensor, nc.scalar, nc.vector, nc.gpsimd, nc.sync

        AOT = k.sb(es0, "AOT", [128, NT, 4, 128], BF16)
        BOT = k.sb(es0, "BOT", [128, NT, 4, 128], BF16)
        CF = k.sb(es0, "CF", [128, 512], F32)
        CB16 = k.sb(es0, "CB16", [128, 512], BF16)
        MODP = k.sb(es0, "MODP", [128, 48, 2], F32)
        S1 = k.sb(es0, "S1", [128, 8, 2], F32)
        S2 = k.sb(es0, "S2", [128, 8, 2], F32)
        G1 = k.sb(es0, "G1", [128, 8], F32)
        G2 = k.sb(es0, "G2", [128, 8], F32)
        GAB = k.sb(es0, "GAB", [128, 2, D], F32)
        SS = k.sb(es0, "SS", [128, 64], F32)
        RS = k.sb(es0, "RS", [128, 64], F32)
        MHALF = k.sb(es0, "MHALF", [128, 32], F32)
        OH = k.sb(es0, "OH", [8, 8, 128], F32)
        UTT = k.sb(es0, "UTT", [64, 128], F32)
        JUNK = k.sb(es0, "JUNK", [128, D], BF16)
        XN = [k.sb(es0, f"XN{i}", [128, D], BF16) for i in range(4)]
        PS = [k.ps(es0, f"PS{i}", [128, 512], F32) for i in range(8)]
        XWALL = k.sb(es0, "XWALL", [128, 12352], F32)
        HT = [k.sb(es0, f"HT{i}", [128, 8, 512], BF16) for i in range(2)]

        IDF = lambda n=128: CF[0:n, 0:n]
        TRIF = CF[:, 128:256]
        TRIB = CF[:, 256:384]
        ONESF = CF[:, 384:512]
        IDB = CB16[:, 0:128]
        MASK = [CB16[:, 128:256], CB16[:, 256:384]]
        ONESB = CB16[:, 384:512]

        k.dma(SP, CF[:], consts, W=[CF])
        k.dma(POOL, CB16[:], consts, W=[CB16])
        k.op(DVE, lambda: dve.memset(MHALF[:], -0.5), W=[MHALF])

        def rstd_of(ss_ap, out_ap, n, inv):
            k.op(POOL, lambda: pool.tensor_scalar(out=out_ap, in0=ss_ap, scalar1=inv, scalar2=EPS,
                                                  op0=ALU.mult, op1=ALU.add), R=[SS], W=[RS])
            k.op(POOL, lambda: pool.tensor_tensor(out=out_ap, in0=out_ap, in1=MHALF[:, 0:n], op=ALU.pow),
                 R=[RS, MHALF], W=[RS])

        try:
            es1 = ExitStack()
            with es1:
                WIN = k.view("WIN", XWALL.t[:].bitcast(BF16).rearrange("p (k n) -> p k n", n=DIN))
                XS = [k.view(f"XS{i}", GAB.t[:, i, :]) for i in range(2)]
                QT = k.sb(es1, "QT", [128, 4, NT_S * 128], BF16)
                KT = k.sb(es1, "KT", [128, 4, NT_S * 128], BF16)
                KTOK = k.sb(es1, "KTOK", [128, NT_S, 512], BF16)
                VV = k.sb(es1, "VV", [128, NT_S, 4, 132], BF16)
                TH = k.sb(es1, "TH", [128, NT_S, 512], BF16)
                HS = k.sb(es1, "HS", [128, NT_S, 512], F32)
                GUT = k.sb(es1, "GUT", [128, 4, 512], BF16)
                GV = k.sb(es1, "GV", [128, 512], F32)
                VGS = [k.sb(es1, f"VG{i}", [128, 512], BF16) for i in range(2)]
                PRE = [k.sb(es1, f"PRE{i}", [128, 512], F32) for i in range(3)]
                GT = k.sb(es1, "GT", [128, NT_S, 16], F32)
                SPT = k.sb(es1, "SPT", [128, NT_S * 8], F32)
                UT = k.sb(es1, "UT", [128, NT_S * 8], F32)
                CBT = k.sb(es1, "CBT", [128, NT_S * 8], F32)
                EE = k.sb(es1, "EE", [128, NT_S * 8], F32)
                THR = k.sb(es1, "THR", [128, NT_S * 8], F32)
                UMB = k.sb(es1, "UMB", [128, NT_S * 8], F32)
                BLB = k.sb(es1, "BLB", [128, NT_S * 8], F32)
                RB = k.sb(es1, "RB", [128, NT_S * 8], F32)
                MPV = k.sb(es1, "MPV", [128, NT_S * 8], F32)
                DCY = k.sb(es1, "DCY", [128, NT_S * 8], F32)
                MCUR = k.sb(es1, "MCUR", [128, 8], F32)
                UMX = k.sb(es1, "UMX", [64, 1], F32)
                DG = k.sb(es1, "DG", [64, 64], F32)
                CST = [[k.sb(es1, f"CST{d}{h}", [128, 132], F32) for h in range(4)] for d in range(2)]
                CBF = [k.sb(es1, f"CBF{i}", [128, 132], BF16) for i in range(8)]
                VP = [k.sb(es1, f"VP{i}", [128, 132], BF16) for i in range(8)]
                ST = [k.sb(es1, f"ST{i}", [128, 128], BF16) for i in range(8)]
                DD = k.sb(es1, "DD", [128, 32], F32)
                WST = k.sb(es1, "WST", [128, 4, 128], BF16)
                BSR = k.sb(es1, "BSR", [1, 512], BF16)
                GVB = k.sb(es1, "GVB", [128, 512], F32)
                GHB = k.sb(es1, "GHB", [128, 512], F32)
                BGB = k.sb(es1, "BGB", [128, 16], F32)
                CW = k.sb(es1, "CW", [128, 8, 3], F32)
                CBI = k.sb(es1, "CBI", [128, 8], F32)
                BADA = k.sb(es1, "BADA", [128, 48], F32)
                CT = k.sb(es1, "CT", [128, 8, 2], F32)
                SCT = k.sb(es1, "SCT", [128, 8, 2], BF16)
                WAB = [AOT, BOT]
                M0B = k.sb(es1, "M0B", [128, 8], F32)
                FLB = k.sb(es1, "FLB", [128, 6], F32)
                MISC = k.sb(es1, "MISC", [128, 16], F32)
                HTH = {id(HT[i]): [k.view(f"HT{i}lo", HT[i].t[:, 0:4, :]), k.view(f"HT{i}hi", HT[i].t[:, 4:8, :])]
                       for i in range(2)}
                KTv = [k.view(f"KT{i}", KT.t[:, :, i * 512:(i + 1) * 512]) for i in range(2)]
                QTv = [k.view(f"QT{i}", QT.t[:, :, i * 512:(i + 1) * 512]) for i in range(2)]
                KTOKv = [k.view(f"KTOKh{i}", KTOK.t[:, i * 4:(i + 1) * 4, :]) for i in range(2)]
                VVv = [k.view(f"VVh{i}", VV.t[:, i * 4:(i + 1) * 4, :, :]) for i in range(2)]
                GTv = [k.view(f"GTh{i}", GT.t[:, i * 4:(i + 1) * 4, :]) for i in range(2)]
                NEGT = k.sb(es1, "NEGT", [128, 4], F32)
                NEGF = k.sb(es1, "NEGF", [128, 6], F32)
                MLOC = k.sb(es1, "MLOC", [128, 3, 8], F32)
                BSEG = k.sb(es1, "BSEG", [128, 3, 8], F32)
                MFO = k.sb(es1, "MFO", [128, 8], F32)
                OFS = k.sb(es1, "OFS", [128, 64], F32)
                RSEG = k.sb(es1, "RSEG", [128, 8], F32)
                FT = k.sb(es1, "FT", [128, 16], F32)
                _botf = BOT.t[:, 4:12, :, :].rearrange("p t c d -> p (t c d)").bitcast(F32)
                _hsf = HS.t[:].rearrange("p t c -> p (t c)")
                SC = [[[None] * 4 for d in range(2)] for p in range(3)]
                SCX = {}
                for _p in range(3):
                    for _d in range(2):
                        for _h in range(4):
                            _n = _p * 8 + _d * 4 + _h
                            if _n < 15:
                                _b = k.view(f"SC{_p}{_d}{_h}", _botf[:, _n * 130:(_n + 1) * 130])
                                SCX[id(_b)] = [BOT]
                            else:
                                _o = 2112 + (_n - 15) * 130
                                _b = k.view(f"SC{_p}{_d}{_h}", _hsf[:, _o:_o + 130])
                                SCX[id(_b)] = None
                            SC[_p][_d][_h] = _b

                with nc.allow_non_contiguous_dma(reason="small param relayout"):
                    for v in range(2):
                        k.dma(SP, CT[:, :, v], cvec[v, :].rearrange("(k p) -> p k", p=128), W=[CT])
                    k.dma(SP, BADA[:], b_ada.rearrange("(c p) -> p c", p=128), W=[BADA])
                    k.dma(SP, G1[:], g_norm1.rearrange("(k p) -> p k", p=128), W=[G1])
                    k.dma(SP, G2[:], g_norm2.rearrange("(k p) -> p k", p=128), W=[G2])
                    for j in range(3):
                        k.dma(SP, CW[:, :, j], conv_w[j, :].rearrange("(k p) -> p k", p=128), W=[CW])
                    k.dma(SP, CBI[:], conv_b.rearrange("(k p) -> p k", p=128), W=[CBI])

                k.dma(SP, GVB[:], g_v.partition_broadcast(128), W=[GVB])
                k.dma(SP, GHB[:], g_h.partition_broadcast(128), W=[GHB])
                k.dma(SP, BGB[:], b_gate.partition_broadcast(128), W=[BGB])
                k.dma(SP, M0B[:], stm.partition_broadcast(128), W=[M0B])
                k.dma(SP, FLB[:], flg.partition_broadcast(128), W=[FLB])
                k.dma(POOL, BSR[:], b_s, W=[BSR])
                for g in range(4):
                    k.dma(SP, PRE[0][:, g * 128:(g + 1) * 128], w_s[g, :, :], W=[PRE[0]])
                k.group([lambda g=g: pe.transpose(PS[7][:, g * 128:(g + 1) * 128], PRE[0][:, g * 128:(g + 1) * 128], IDF())
                         for g in range(4)], R=[PRE[0], CF], W=[PS[7]])
                k.op(DVE, lambda: dve.tensor_copy(out=WST[:], in_=PS[7][:].rearrange("p (g t) -> p g t", t=128)),
                     R=[PS[7]], W=[WST])
                for kk in range(8):
                    k.op(DVE, lambda kk=kk: dve.tensor_scalar(out=OH[0:8, kk, :], in0=CF[0:8, 384:512],
                                                              scalar1=CF[0:8, kk:kk + 1], scalar2=None, op0=ALU.mult),
                         R=[CF], W=[OH])
                k.op(DVE, lambda: dve.tensor_scalar(out=GHB[:], in0=GHB[:], scalar1=0.5, scalar2=None, op0=ALU.mult),
                     R=[GHB], W=[GHB])
                for d in range(2):
                    for h in range(4):
                        k.op(POOL, lambda d=d, h=h: pool.memset(CST[d][h][:], 0.0), W=[CST[d][h]])
                k.op(POOL, lambda: pool.memset(VV[:, :, :, 128:132], 1.0), W=VVv)

                k.op(ACT, lambda: act.activation(out=SCT[:], in_=CT[:], func=AF.Silu), R=[CT], W=[SCT])
                wa_view = w_ada.rearrange("(k p) n -> p k n", p=128)

                def mod_block(cb):
                    slot = WAB[cb % 2]
                    sv = slot[:, 4:12, :, :].rearrange("p t c d -> p t (c d)")
                    k.dma(POOL, sv, wa_view[:, :, cb * 512:(cb + 1) * 512], W=[slot])
                    for i in range(4):
                        ch = cb * 4 + i
                        k.group([lambda kk=kk, i=i, ch=ch: pe.matmul(PS[0][:, ch * 2:ch * 2 + 2],
                                                                      lhsT=sv[:, kk, i * 128:(i + 1) * 128],
                                                                      rhs=SCT[:, kk, :], start=(kk == 0), stop=(kk == 7))
                                 for kk in range(8)], R=[slot, SCT], W=[PS[0]])

                def mod_load(cb):
                    sv = AOT[:, 4:12, :, :].rearrange("p t c d -> p t (c d)")
                    k.dma(POOL, sv, wa_view[:, :, cb * 512:(cb + 1) * 512], W=[AOT])

                def mod_compute(cb):
                    sv = AOT[:, 4:12, :, :].rearrange("p t c d -> p t (c d)")
                    pbm = PS[6]
                    for i in range(4):
                        k.group([lambda kk=kk, i=i: pe.matmul(pbm[:, i * 2:i * 2 + 2], lhsT=sv[:, kk, i * 128:(i + 1) * 128],
                                                              rhs=SCT[:, kk, :], start=(kk == 0), stop=(kk == 7))
                                 for kk in range(8)], R=[AOT, SCT], W=[pbm])
                    for v in range(2):
                        k.op(DVE, lambda v=v: dve.tensor_tensor(
                            out=MODP[:, cb * 4:cb * 4 + 4, v], in0=pbm[:, 0:8].rearrange("p (c v) -> p c v", v=2)[:, :, v],
                            in1=BADA[:, cb * 4:cb * 4 + 4], op=ALU.add), R=[pbm, BADA], W=[MODP])
                    if cb + 1 < 12:
                        mod_load(cb + 1)
                    if cb == 11:
                        for v in range(2):
                            k.op(DVE, lambda v=v: dve.scalar_tensor_tensor(out=S2[:, :, v], in0=MODP[:, 32:40, v], scalar=1.0,
                                                                           in1=G2[:], op0=ALU.add, op1=ALU.mult),
                                 R=[MODP, G2], W=[S2])

                def mod_finish(c0, c1):
                    for v in range(2):
                        k.op(DVE, lambda v=v: dve.tensor_tensor(
                            out=MODP[:, c0:c1, v], in0=PS[0][:, 2 * c0:2 * c1].rearrange("p (c v) -> p c v", v=2)[:, :, v],
                            in1=BADA[:, c0:c1], op=ALU.add), R=[PS[0], BADA], W=[MODP])

                for cb in range(4):
                    mod_block(cb)
                mod_finish(0, 16)
                for v in range(2):
                    k.op(DVE, lambda v=v: dve.scalar_tensor_tensor(out=S1[:, :, v], in0=MODP[:, 8:16, v], scalar=1.0,
                                                                   in1=G1[:], op0=ALU.add, op1=ALU.mult),
                         R=[MODP, G1], W=[S1])

                stage(1)
                win_view = w_in.rearrange("(k p) n -> p k n", p=128)
                for kk in range(8):
                    pass
                WING = {}
                for (g0, g1) in ((OFF_U, OFF_V), (OFF_Q, OFF_K), (OFF_K, OFF_VV), (OFF_VV, OFF_O), (OFF_G, DIN),
                                 (OFF_O, OFF_G), (OFF_V, OFF_Q)):
                    gb = k.view(f"WIN_{g0}", WIN.t[:, :, g0:g1])
                    for c0 in range(g0, g1, 128):
                        WING[c0] = gb
                    k.dma(POOL, WIN[:, :, g0:g1], win_view[:, :, g0:g1], W=[gb])

                ps_rot = {"proj": 0, "small": 0, "tr": 0}

                def bank(kind):
                    if kind == "tr":
                        b = PS[7 - ps_rot["tr"] % 2]
                    elif kind == "proj":
                        b = PS[2 + ps_rot["proj"] % 4]
                    else:
                        b = PS[6 + ps_rot["small"] % 2]
                    ps_rot[kind] += 1
                    return b

                xn_rot = [0]

                def norm_pre(xbuf, x_ap, ssl):
                    k.op(ACT, lambda: act.activation(out=JUNK[:], in_=x_ap, func=AF.Square, accum_out=SS[:, ssl:ssl + 1]),
                         R=[xbuf], W=[JUNK, SS])
                    rstd_of(SS[:, ssl:ssl + 1], RS[:, ssl:ssl + 1], 1, 1.0 / D)
                    xn = XN[xn_rot[0] % 4]
                    xn_rot[0] += 1
                    k.op(DVE, lambda: dve.tensor_scalar(out=xn[:], in0=x_ap, scalar1=RS[:, ssl:ssl + 1], scalar2=None,
                                                        op0=ALU.mult), R=[xbuf, RS], W=[xn])
                    return xn

                def norm_post(xn, Sx, SHoff, v, ht, col0):
                    pbs = [PS[0], PS[1]]
                    pvs = [pbs[0][:].bitcast(BF16), pbs[1][:].bitcast(BF16)]
                    for hb in range(2):
                        k.group([lambda kk=kk, hb=hb: pe.transpose(pvs[hb][:, (kk % 4) * 128:(kk % 4 + 1) * 128],
                                                                  xn[:, kk * 128:(kk + 1) * 128], IDB)
                                 for kk in range(hb * 4, hb * 4 + 4)], R=[xn, CB16], W=[pbs[hb]])
                    for j in range(4):
                        kk = j
                        k.op(DVE, lambda kk=kk, j=j: dve.tensor_scalar(out=ht[:, kk, col0:col0 + 128],
                                                                       in0=pvs[0][:, j * 128:(j + 1) * 128],
                                                                       scalar1=Sx[:, kk, v:v + 1],
                                                                       scalar2=MODP[:, SHoff + kk, v:v + 1],
                                                                       op0=ALU.mult, op1=ALU.add),
                             R=[pbs[0], Sx, MODP], W=[HTH[id(ht)][0]])
                        kk = 4 + j
                        k.op(ACT, lambda kk=kk, j=j: act.activation(out=ht[:, kk, col0:col0 + 128],
                                                                    in_=pvs[1][:, j * 128:(j + 1) * 128], func=AF.Identity,
                                                                    scale=Sx[:, kk, v:v + 1],
                                                                    bias=MODP[:, SHoff + kk, v:v + 1]),
                             R=[pbs[1], Sx, MODP], W=[HTH[id(ht)][1]])

                lagq = []

                def lag_flush():
                    while lagq:
                        lagq.pop(0)()

                def lag_push(fn):
                    lag_flush()
                    lagq.append(fn)

                def proj_fm(ht, ntok, col, evac):
                    pb = bank("proj")
                    k.group([lambda kk=kk: pe.matmul(pb[:, 0:ntok], lhsT=WIN[:, kk, col:col + 128], rhs=ht[:, kk, 0:ntok],
                                                     start=(kk == 0), stop=(kk == 7)) for kk in range(8)],
                            R=[WING[col]] + HTH[id(ht)], W=[pb])
                    lag_push(lambda: evac(pb))

                def proj_tm(ht, tcol, col, ncol, evac, kind="proj"):
                    pb = bank(kind)
                    k.group([lambda kk=kk: pe.matmul(pb[:, 0:ncol], lhsT=ht[:, kk, tcol:tcol + 128],
                                                     rhs=WIN[:, kk, col:col + ncol],
                                                     start=(kk == 0), stop=(kk == 7)) for kk in range(8)],
                            R=[WING[col]] + HTH[id(ht)], W=[pb])
                    lag_push(lambda: evac(pb))

                pre_rot = [0]

                def conv_silu(pb, ch, ntok, seqlen, out_ap, outbuf):
                    pr = PRE[pre_rot[0] % 3]
                    pre_rot[0] += 1
                    k.op(ACT, lambda: act.activation(out=pr[:, 0:ntok], in_=pb[:, 0:ntok], func=AF.Identity,
                                                     scale=CW[:, ch, 1:2], bias=CBI[:, ch:ch + 1]),
                         R=[pb, CW, CBI], W=[pr])
                    pv = pb[:, 0:ntok].rearrange("p (s t) -> p s t", t=seqlen)
                    rv = pr[:, 0:ntok].rearrange("p (s t) -> p s t", t=seqlen)

                    def stage_b():
                        k.op(DVE, lambda: dve.scalar_tensor_tensor(out=rv[:, :, 1:seqlen], in0=pv[:, :, 0:seqlen - 1],
                                                                   scalar=CW[:, ch, 0:1], in1=rv[:, :, 1:seqlen],
                                                                   op0=ALU.mult, op1=ALU.add), R=[pb, CW, pr], W=[pr])
                        k.op(DVE, lambda: dve.scalar_tensor_tensor(out=rv[:, :, 0:seqlen - 1], in0=pv[:, :, 1:seqlen],
                                                                   scalar=CW[:, ch, 2:3], in1=rv[:, :, 0:seqlen - 1],
                                                                   op0=ALU.mult, op1=ALU.add), R=[pb, CW, pr], W=[pr])

                    def stage_c():
                        k.op(ACT, lambda: act.activation(out=out_ap, in_=pr[:, 0:ntok], func=AF.Silu), R=[pr], W=[outbuf])

                    if conv_c:
                        conv_c.pop(0)()
                    if conv_b:
                        fb, fc = conv_b.pop(0)
                        fb()
                        conv_c.append(fc)
                    conv_b.append((stage_b, stage_c))

                conv_b = []
                conv_c = []

                def conv_flush():
                    while conv_b:
                        fb, fc = conv_b.pop(0)
                        fb()
                        conv_c.append(fc)
                    while conv_c:
                        conv_c.pop(0)()

                ktok_rot = [0]

                def k_to_tok_tile(ut):
                    lag_flush()
                    conv_flush()
                    pb = bank("small")
                    pbv = pb[:].bitcast(BF16)
                    k.group([lambda h=h: pe.transpose(pbv[:, h * 128:(h + 1) * 128], KT[:, h, ut * 128:(ut + 1) * 128], IDB)
                             for h in range(4)], R=[KTv[ut // 4], CB16], W=[pb])
                    ktok_rot[0] += 1
                    if ktok_rot[0] % 2 == 0:
                        k.op(DVE, lambda: dve.tensor_copy(out=KTOK[:, ut, :], in_=pbv[:, 0:512]), R=[pb], W=[KTOKv[ut // 4]])
                    else:
                        k.op(ACT, lambda: act.activation(out=KTOK[:, ut, :], in_=pbv[:, 0:512], func=AF.Copy), R=[pb], W=[KTOKv[ut // 4]])

                def gates_evac(pb, ut):
                    k.op(DVE, lambda: dve.tensor_tensor(out=GT[:, ut, :], in0=pb[:, 0:16], in1=BGB[:], op=ALU.add),
                         R=[pb, BGB], W=[GTv[ut // 4]])

                pa_state = {"xc": 0, "n": 0}

                def mk_blk(tiles, v, seqlen, own, u0, after=None):
                    blk = dict(tiles=tiles, v=v, seqlen=seqlen, own=own, u0=u0, after=after, idx=pa_state["n"])
                    blk["ht"] = HT[pa_state["n"] % 2]
                    pa_state["n"] += 1
                    return blk

                def pa_pre(blk, i):
                    t = blk["tiles"][i]
                    xb = XS[pa_state["xc"] % 2]
                    pa_state["xc"] += 1
                    if blk["own"]:
                        src = xp[t * 128:(t + 1) * 128, :] if t < NT_P else xs[(t - NT_P) * 128:(t - NT_P + 1) * 128, :]
                    else:
                        src = xf[t * 128:(t + 1) * 128, :]
                    k.dma(SP, xb[:], src, W=[xb])
                    blk.setdefault("xn", {})[i] = norm_pre(xb, xb[:], (blk["idx"] % 2) * 4 + i)

                def pa_post(blk, i):
                    norm_post(blk["xn"][i], S1, 0, blk["v"], blk["ht"], i * 128)

                def pa_items(blk):
                    tiles, v, seqlen, own, u0, ht = blk["tiles"], blk["v"], blk["seqlen"], blk["own"], blk["u0"], blk["ht"]
                    nt = len(tiles)
                    ntok = nt * 128
                    items = []
                    if own:
                        for c in range(4):
                            items.append(lambda c=c: proj_fm(ht, ntok, OFF_U + c * 128,
                                         lambda pb: k.op(ACT, lambda: act.activation(out=GUT[:, c, 0:ntok], in_=pb[:, 0:ntok],
                                                                                     func=AF.Gelu_apprx_tanh), R=[pb], W=[GUT])))
                        for c in range(4):
                            items.append(lambda c=c: proj_fm(ht, ntok, OFF_Q + c * 128,
                                         lambda pb: conv_silu(pb, c, ntok, seqlen, QT[:, c, u0 * 128:u0 * 128 + ntok], QTv[u0 // 4])))
                    for c in range(4):
                        items.append(lambda c=c: proj_fm(ht, ntok, OFF_K + c * 128,
                                     lambda pb: conv_silu(pb, 4 + c, ntok, seqlen, KT[:, c, u0 * 128:u0 * 128 + ntok], KTv[u0 // 4])))

                    def fm_tail_flush():
                        lag_flush()
                        conv_flush()
                    items.append(fm_tail_flush)

                    def tile_item(i, t):
                        ut = u0 + i
                        if i % 2 == 0:
                            proj_tm(ht, i * 128, OFF_VV, 512,
                                    lambda pb: k.op(ACT, lambda: act.activation(
                                        out=VV[:, ut, :, 0:128], in_=pb[:].rearrange("p (h d) -> p h d", d=128), func=AF.Copy),
                                        R=[pb], W=[VVv[ut // 4]]))
                        else:
                            proj_tm(ht, i * 128, OFF_VV, 512,
                                    lambda pb: k.op(DVE, lambda: dve.tensor_copy(
                                        out=VV[:, ut, :, 0:128], in_=pb[:].rearrange("p (h d) -> p h d", d=128)),
                                        R=[pb], W=[VVv[ut // 4]]))
                        proj_tm(ht, i * 128, OFF_G, 16, lambda pb: gates_evac(pb, ut), kind="small")

                    def tile_item_own(i, t):
                        ut = u0 + i
                        proj_tm(ht, i * 128, OFF_O, 512,
                                lambda pb: k.op(ACT, lambda: act.activation(out=TH[:, ut, :], in_=pb[:], func=AF.Tanh,
                                                                            scale=0.5), R=[pb], W=[TH]))
                        proj_tm(ht, i * 128, OFF_V, 512,
                                lambda pb: k.op(ACT, lambda: act.activation(out=GV[:], in_=pb[:], func=AF.Gelu_apprx_tanh),
                                                R=[pb], W=[GV]))
                        lag_flush()
                        for g in range(4):
                            k.op(ACT, lambda g=g: act.activation(out=JUNK[:, 0:128], in_=GV[:, g * 128:(g + 1) * 128],
                                                                 func=AF.Square, accum_out=SS[:, 8 + g:9 + g]),
                                 R=[GV], W=[JUNK, SS])
                        rstd_of(SS[:, 8:12], RS[:, 8:12], 4, 1.0 / 128)
                        vg = VGS[i % 2]
                        for g in range(4):
                            k.op(DVE, lambda g=g: dve.scalar_tensor_tensor(
                                out=vg[:, g * 128:(g + 1) * 128], in0=GV[:, g * 128:(g + 1) * 128],
                                scalar=RS[:, 8 + g:9 + g], in1=GVB[:, g * 128:(g + 1) * 128],
                                op0=ALU.mult, op1=ALU.mult), R=[GV, RS, GVB], W=[vg])

                    def tile_item_own_b(i, t):
                        vg = VGS[i % 2]
                        pb = bank("proj")
                        fns = []
                        for g in range(4):
                            fns.append(lambda g=g: pe.matmul(pb[:, g * 128:(g + 1) * 128], lhsT=vg[:, g * 128:(g + 1) * 128],
                                                             rhs=WST[:, g, :], start=True, stop=False))
                            fns.append(lambda g=g: pe.matmul(pb[:, g * 128:(g + 1) * 128], lhsT=ONESB[0:1, :],
                                                             rhs=BSR[0:1, g * 128:(g + 1) * 128], start=False, stop=True))
                        k.group(fns, R=[vg, WST, CB16, BSR], W=[pb])
                        k.op(DVE, lambda: dve.tensor_tensor(
                            out=AOT[:, t, :, :], in0=pb[:].rearrange("p (g t) -> p g t", t=128),
                            in1=GUT[:, :, i * 128:(i + 1) * 128], op=ALU.mult), R=[pb, GUT], W=[AOT])

                    def full_flush():
                        lag_flush()
                        conv_flush()
                    if blk.get("pre_a") is not None:
                        items.insert(0, blk["pre_a"])
                    if blk.get("mid_a") is not None:
                        items.append(full_flush)
                        items.append(blk["mid_a"])
                    for i, t in enumerate(tiles):
                        items.append(lambda i=i, t=t: tile_item(i, t))
                        if own:
                            items.append(lambda i=i, t=t: tile_item_own(i, t))
                            if i > 0:
                                items.append(lambda i=i: tile_item_own_b(i - 1, tiles[i - 1]))
                    if own:
                        items.append(lambda: tile_item_own_b(nt - 1, tiles[nt - 1]))
                    if blk.get("mid_b") is not None:
                        items.append(full_flush)
                        items.append(blk["mid_b"])
                    for i, t in enumerate(tiles):
                        items.append(lambda i=i: k_to_tok_tile(u0 + i))
                    return items

                def run_blocks(blks):
                    for n_, b_ in enumerate(blks):
                        b_["ht"] = HT[n_ % 2]
                        b_["idx"] = n_
                    for i in range(4):
                        pa_pre(blks[0], i)
                    for i in range(4):
                        pa_post(blks[0], i)
                    for n, blk in enumerate(blks):
                        items = pa_items(blk)
                        nxt = blks[n + 1] if n + 1 < len(blks) else None
                        L = len(items)
                        marks = {max(0, (L * (q + 1)) // 8): q for q in range(4)}
                        if nxt is not None:
                            for i in range(4):
                                pa_pre(nxt, i)
                        mods = blk.get("mods")
                        extras = list(blk.get("extras") or [])
                        for idx, it in enumerate(items):
                            it()
                            if extras:
                                lag_flush()
                                extras.pop(0)()
                            if nxt is not None and idx in marks:
                                pa_post(nxt, marks[idx])
                            if mods is not None and idx == L // 4:
                                lag_flush()
                                mod_compute(mods[0])
                            if mods is not None and idx == (3 * L) // 4:
                                lag_flush()
                                mod_compute(mods[1])
                        lag_flush()
                        conv_flush()
                        while extras:
                            extras.pop(0)()
                        if blk["after"] is not None:
                            blk["after"]()

                def gates_unit(ntl, seqs, m0_aps, prefix_only=False):
                    n8 = ntl * 8
                    gv4 = GT[:, 0:ntl, :].rearrange("p t (d j h) -> p t d j h", d=2, j=2)
                    SPT3 = SPT[:, 0:n8].rearrange("p (t e) -> p t e", e=8)
                    for d in range(2):
                        k.op(ACT, lambda d=d: act.activation(out=SPT3[:, :, d * 4:(d + 1) * 4], in_=gv4[:, :, d, 1, :],
                                                             func=AF.Exp, scale=-1.0), R=GTv, W=[SPT])
                    k.op(ACT, lambda: act.activation(out=SPT[:, 0:n8], in_=SPT[:, 0:n8], func=AF.Ln, bias=1.0),
                         R=[SPT], W=[SPT])
                    pcb = PS[0]
                    fns = []
                    for t in range(ntl):
                        fns.append(lambda t=t: pe.matmul(pcb[:, t * 8:t * 8 + 4], lhsT=TRIF, rhs=SPT[:, t * 8:t * 8 + 4],
                                                         start=True, stop=True))
                        fns.append(lambda t=t: pe.matmul(pcb[:, t * 8 + 4:t * 8 + 8], lhsT=TRIB, rhs=SPT[:, t * 8 + 4:t * 8 + 8],
                                                         start=True, stop=True))
                    k.group(fns, R=[CF, SPT], W=[pcb])
                    k.op(DVE, lambda: dve.tensor_copy(out=CBT[:, 0:n8], in_=pcb[:, 0:n8]), R=[pcb], W=[CBT])
                    for d in range(2):
                        k.op(DVE, lambda d=d: dve.tensor_tensor(
                            out=UT[:, 0:n8].rearrange("p (t e) -> p t e", e=8)[:, :, d * 4:(d + 1) * 4],
                            in0=CBT[:, 0:n8].rearrange("p (t e) -> p t e", e=8)[:, :, d * 4:(d + 1) * 4],
                            in1=gv4[:, :, d, 0, :], op=ALU.add), R=[CBT] + GTv, W=[UT])
                    ptr = PS[1]
                    k.group([lambda: pe.transpose(ptr[0:n8, 0:128], UT[:, 0:n8], IDF())], R=[UT, CF], W=[ptr])
                    k.op(DVE, lambda: dve.tensor_reduce(out=UMX[0:n8, :], in_=ptr[0:n8, 0:128], axis=AX.X, op=ALU.max),
                         R=[ptr], W=[UMX])
                    k.op(DVE, lambda: dve.tensor_scalar(out=DG[0:n8, 0:n8], in0=CF[0:n8, 0:n8], scalar1=UMX[0:n8, 0:1],
                                                        scalar2=None, op0=ALU.mult), R=[CF, UMX], W=[DG])
                    pbb = PS[1]
                    k.group([lambda: pe.matmul(pbb[:, 128:128 + n8], lhsT=ONESF[0:n8, :], rhs=DG[0:n8, 0:n8],
                                               start=True, stop=True)], R=[CF, DG], W=[pbb])
                    k.op(DVE, lambda: dve.tensor_copy(out=UMB[:, 0:n8], in_=pbb[:, 128:128 + n8]), R=[pbb], W=[UMB])
                    k.group([lambda: pe.matmul(pbb[:, 256:256 + n8], lhsT=ONESF,
                                               rhs=SPT[:, 0:n8], start=True, stop=True)],
                            R=[CF, SPT], W=[pbb])
                    k.op(DVE, lambda: dve.tensor_copy(out=BLB[:, 0:n8], in_=pbb[:, 256:256 + n8]), R=[pbb], W=[BLB])
                    if prefix_only:
                        return
                    for si, seq in enumerate(seqs):
                        for d in range(2):
                            order = seq if d == 0 else seq[::-1]
                            m0 = m0_aps[si][d]
                            if m0 is None:
                                k.op(DVE, lambda d=d: dve.memset(MCUR[:, d * 4:(d + 1) * 4], 0.0), W=[MCUR])
                            else:
                                k.op(DVE, lambda d=d, m0=m0: dve.tensor_copy(out=MCUR[:, d * 4:(d + 1) * 4], in_=m0[0]),
                                     R=[m0[1]], W=[MCUR])
                            for t in order:
                                sl = slice(t * 8 + d * 4, t * 8 + d * 4 + 4)
                                k.op(DVE, lambda sl=sl, d=d: dve.tensor_copy(out=MPV[:, sl], in_=MCUR[:, d * 4:(d + 1) * 4]),
                                     R=[MCUR], W=[MPV])
                                k.op(DVE, lambda sl=sl, d=d: dve.tensor_tensor(out=RB[:, sl], in0=MCUR[:, d * 4:(d + 1) * 4],
                                                                               in1=UMB[:, sl], op=ALU.max),
                                     R=[MCUR, UMB], W=[RB])
                                k.op(DVE, lambda sl=sl, d=d: dve.tensor_tensor(out=MCUR[:, d * 4:(d + 1) * 4], in0=RB[:, sl],
                                                                               in1=BLB[:, sl], op=ALU.subtract),
                                     R=[RB, BLB], W=[MCUR])
                            seq_end(si, d)
                    k.op(DVE, lambda: dve.tensor_tensor(out=EE[:, 0:n8], in0=UT[:, 0:n8], in1=RB[:, 0:n8], op=ALU.subtract),
                         R=[UT, RB], W=[EE])
                    k.op(ACT, lambda: act.activation(out=EE[:, 0:n8], in_=EE[:, 0:n8], func=AF.Exp, bias=MISC[:, 0:1]),
                         R=[EE, MISC], W=[EE])
                    k.op(DVE, lambda: dve.tensor_tensor(out=THR[:, 0:n8], in0=CBT[:, 0:n8], in1=RB[:, 0:n8], op=ALU.subtract),
                         R=[CBT, RB], W=[THR])
                    k.op(ACT, lambda: act.activation(out=THR[:, 0:n8], in_=THR[:, 0:n8], func=AF.Exp), R=[THR], W=[THR])
                    k.op(DVE, lambda: dve.tensor_tensor(out=DCY[:, 0:n8], in0=MPV[:, 0:n8], in1=RB[:, 0:n8], op=ALU.subtract),
                         R=[MPV, RB], W=[DCY])
                    k.op(ACT, lambda: act.activation(out=DCY[:, 0:n8], in_=DCY[:, 0:n8], func=AF.Exp), R=[DCY], W=[DCY])

                seq_end_cb = [None]

                def seq_end(si, d):
                    if seq_end_cb[0] is not None:
                        seq_end_cb[0](si, d)

                k.op(DVE, lambda: dve.memset(MISC[:, 0:1], LNS), W=[MISC])

                NCH = 8
                HSV = [k.view(f"HSV{t}", HS.t[:, t, :]) for t in range(NT_S)]
                DDV = [k.view(f"DDV{i}", DD.t[:, i * 4:(i + 1) * 4]) for i in range(NCH)]
                hs_written = {}

                def mlstm_wave(chains, banks=None, slot0=0):
                    n = len(chains)
                    info = []
                    for i, (ut, d, h) in enumerate(chains):
                        j = slot0 + i
                        info.append(dict(ut=ut, d=d, h=h, col=ut * 8 + d * 4 + h, pb=(banks[i] if banks else PS[i]), vp=VP[j],
                                         cb=CBF[j], st=ST[j], dd=DDV[j], cst=CST[d][h], qs=slice(ut * 128, (ut + 1) * 128)))
                    for i, c in enumerate(info):
                        k.op(ACT, lambda c=c: act.activation(out=c["vp"][:, 0:130], in_=VV[:, c["ut"], c["h"], 0:130],
                                                             func=AF.Identity, scale=EE[:, c["col"]:c["col"] + 1]),
                             R=[VVv[c["ut"] // 4], EE], W=[c["vp"]])
                        if False:
                            pass
                        else:
                            k.op(ACT, lambda c=c: act.activation(out=c["cb"][:, 0:130], in_=c["cst"][:, 0:130],
                                                                 func=AF.Identity, scale=DCY[:, c["col"]:c["col"] + 1]),
                                 R=[c["cst"], DCY], W=[c["cb"]])
                    for c in info:
                        k.group([lambda c=c: pe.matmul(c["pb"][:, 0:128], lhsT=KT[:, c["h"], c["qs"]], rhs=QT[:, c["h"], c["qs"]],
                                                       start=True, stop=True)], R=[KTv[c["ut"] // 4], QTv[c["ut"] // 4]], W=[c["pb"]])
                    for c in info:
                        k.op(DVE, lambda c=c: dve.tensor_tensor(out=c["st"][:], in0=c["pb"][:, 0:128], in1=MASK[c["d"]],
                                                                op=ALU.mult), R=[c["pb"], CB16], W=[c["st"]])
                    for c in info:
                        k.group([lambda c=c: pe.matmul(c["pb"][:, 128:258], lhsT=c["st"][:], rhs=c["vp"][:, 0:130],
                                                       start=True, stop=False),
                                 lambda c=c: pe.matmul(c["pb"][:, 128:258], lhsT=QT[:, c["h"], c["qs"]], rhs=c["cb"][:, 0:130],
                                                       start=False, stop=True),
                                 lambda c=c: pe.matmul(c["pb"][:, 260:390], lhsT=KTOK[:, c["ut"], c["h"] * 128:(c["h"] + 1) * 128],
                                                       rhs=c["vp"][:, 0:130], start=True, stop=True)],
                                R=[c["st"], c["vp"], QTv[c["ut"] // 4], c["cb"], KTOKv[c["ut"] // 4]], W=[c["pb"]])
                    for c in info:
                        dd = c["dd"]
                        k.op(DVE, lambda c=c, dd=dd: dve.tensor_scalar(out=dd[:, 2:3], in0=c["pb"][:, 256:257], scalar1=-1.0,
                                                                       scalar2=None, op0=ALU.mult), R=[c["pb"]], W=[dd])
                        k.op(DVE, lambda c=c, dd=dd: dve.scalar_tensor_tensor(out=dd[:, 0:1], in0=c["pb"][:, 256:257],
                                                                              scalar=THR[:, c["col"]:c["col"] + 1],
                                                                              in1=dd[:, 2:3], op0=ALU.max, op1=ALU.max),
                             R=[c["pb"], THR, dd], W=[dd])
                        k.op(DVE, lambda dd=dd: dve.reciprocal(out=dd[:, 1:2], in_=dd[:, 0:1]), R=[dd], W=[dd])
                    for c in info:
                        dd = c["dd"]
                        hsb = HSV[c["ut"]]
                        hsl = slice(c["h"] * 128, (c["h"] + 1) * 128)
                        key = (c["ut"], c["h"])
                        if key not in hs_written:
                            hs_written[key] = True
                            k.op(ACT, lambda c=c, dd=dd, hsb=hsb, hsl=hsl: act.activation(out=hsb[:, hsl], in_=c["pb"][:, 128:256],
                                                                                         func=AF.Identity, scale=dd[:, 1:2]),
                                 R=[c["pb"], dd], W=[hsb])
                        else:
                            k.op(DVE, lambda c=c, dd=dd, hsb=hsb, hsl=hsl: dve.scalar_tensor_tensor(
                                out=hsb[:, hsl], in0=c["pb"][:, 128:256], scalar=dd[:, 1:2], in1=hsb[:, hsl],
                                op0=ALU.mult, op1=ALU.add), R=[c["pb"], dd, hsb], W=[hsb])
                    for c in info:
                        k.op(DVE, lambda c=c: dve.scalar_tensor_tensor(out=c["cst"][:, 0:130], in0=c["cst"][:, 0:130],
                                                                       scalar=DCY[:, c["col"]:c["col"] + 1],
                                                                       in1=c["pb"][:, 260:390], op0=ALU.mult, op1=ALU.add),
                             R=[c["cst"], DCY, c["pb"]], W=[c["cst"]])

                def mlstm_post_unit(uts, gts):
                    ntl = len(uts)
                    for ut in uts:
                        for h in range(4):
                            k.op(ACT, lambda h=h, ut=ut: act.activation(out=JUNK[:, 0:128], in_=HSV[ut][:, h * 128:(h + 1) * 128],
                                                                        func=AF.Square,
                                                                        accum_out=SS[:, 16 + ut * 4 + h:17 + ut * 4 + h]),
                                 R=[HSV[ut]], W=[JUNK, SS])
                        rstd_of(SS[:, 16 + 4 * ut:20 + 4 * ut], RS[:, 16 + 4 * ut:20 + 4 * ut], 4, 1.0 / 128)
                    for ut, gt in zip(uts, gts):
                        for h in range(4):
                            k.op(DVE, lambda h=h, ut=ut: dve.scalar_tensor_tensor(
                                out=HSV[ut][:, h * 128:(h + 1) * 128], in0=HSV[ut][:, h * 128:(h + 1) * 128],
                                scalar=RS[:, 16 + ut * 4 + h:17 + ut * 4 + h], in1=GHB[:, h * 128:(h + 1) * 128],
                                op0=ALU.mult, op1=ALU.mult), R=[HSV[ut], RS, GHB], W=[HSV[ut]])
                        xn = XN[xn_rot[0] % 2]
                        xn_rot[0] += 1
                        k.op(POOL, lambda ut=ut, xn=xn: pool.scalar_tensor_tensor(out=xn[:, 0:512], in0=TH[:, ut, :], scalar=1.0,
                                                                                   in1=HSV[ut][:], op0=ALU.add, op1=ALU.mult),
                             R=[TH, HSV[ut]], W=[xn]) if False else k.op(
                            DVE, lambda ut=ut, xn=xn: dve.scalar_tensor_tensor(out=xn[:, 0:512], in0=TH[:, ut, :], scalar=1.0,
                                                                               in1=HSV[ut][:], op0=ALU.add, op1=ALU.mult),
                            R=[TH, HSV[ut]], W=[xn])
                        pb = bank("tr")
                        pbv = pb[:].bitcast(BF16)
                        k.group([lambda c=c, xn=xn, pbv=pbv: pe.transpose(pbv[:, c * 128:(c + 1) * 128],
                                                                          xn[:, c * 128:(c + 1) * 128], IDB)
                                 for c in range(4)], R=[xn, CB16], W=[pb])
                        k.op(ACT, lambda gt=gt, pbv=pbv: act.activation(out=BOT[:, gt, :, :],
                                                                        in_=pbv[:, 0:512].rearrange("p (c t) -> p c t", t=128),
                                                                        func=AF.Copy), R=[pb], W=[BOT])

                prompt_blk = mk_blk([0, 1, 2, 3], 0, 256, True, 0)

                def prompt_seq_end(si, d):
                    k.dma(SP, nm[si:si + 1, d * 4:(d + 1) * 4], MCUR[0:1, d * 4:(d + 1) * 4], R=[MCUR], final=True)

                def prompt_gates():
                    seq_end_cb[0] = prompt_seq_end
                    gates_unit(4, [[0, 1], [2, 3]], [[None, None], [None, None]])
                    seq_end_cb[0] = None

                HW_BANKS = [PS[0], PS[1], PS[6], PS[7]]
                prompt_extras = [prompt_gates]
                for si, seq in enumerate([[0, 1], [2, 3]]):
                    def zero_states():
                        for d in range(2):
                            for h in range(4):
                                k.op(POOL, lambda d=d, h=h: pool.memset(CST[d][h][:], 0.0), W=[CST[d][h]])
                    prompt_extras.append(zero_states)
                    for w in range(2):
                        prompt_extras.append(lambda seq=seq, w=w: mlstm_wave([(seq[w], 0, h) for h in range(4)],
                                                                            banks=HW_BANKS, slot0=0))
                        prompt_extras.append(lambda seq=seq, w=w: mlstm_wave([(seq[1 - w], 1, h) for h in range(4)],
                                                                            banks=HW_BANKS, slot0=4))

                    def store_states(si=si):
                        with nc.allow_non_contiguous_dma(reason="state column"):
                            for d in range(2):
                                for h in range(4):
                                    k.dma(SP, nC[si, d, h, :, :], CST[d][h][:, 0:128], R=[CST[d][h]], final=True)
                                    k.dma(SP, nn[si, d, h, :].rearrange("(p o) -> p o", o=1), CST[d][h][:, 128:129],
                                          R=[CST[d][h]], final=True)
                    prompt_extras.append(store_states)
                prompt_extras.append(lambda: mlstm_post_unit([0, 1, 2, 3], [0, 1, 2, 3]))

                mod_load(4)

                stage(5)
                k.op(DVE, lambda: dve.memset(NEGT[:], NEG), W=[NEGT])
                k.op(DVE, lambda: dve.tensor_scalar(out=NEGF[:], in0=FLB[:], scalar1=-1.0, scalar2=-NEG,
                                                    op0=ALU.add, op1=ALU.mult), R=[FLB], W=[NEGF])
                def summary_pre(p):
                    gates_unit(8, None, None, prefix_only=True)
                    summary_a0(p)

                def summary_a0(p):
                    B3 = BLB[:, 0:64].rearrange("p (t e) -> p t e", e=8)
                    O3 = OFS[:, 0:64].rearrange("p (t e) -> p t e", e=8)
                    k.op(DVE, lambda: dve.memset(OFS[:], 0.0), W=[OFS])
                    for t in range(1, 8):
                        k.op(DVE, lambda t=t: dve.tensor_tensor(out=O3[:, t, 0:4], in0=O3[:, t - 1, 0:4], in1=B3[:, t - 1, 0:4],
                                                                op=ALU.add), R=[OFS, BLB], W=[OFS])
                    for t in range(6, -1, -1):
                        k.op(DVE, lambda t=t: dve.tensor_tensor(out=O3[:, t, 4:8], in0=O3[:, t + 1, 4:8], in1=B3[:, t + 1, 4:8],
                                                                op=ALU.add), R=[OFS, BLB], W=[OFS])
                    k.op(DVE, lambda p=p: dve.tensor_reduce(out=BSEG[:, p - 1, :],
                                                            in_=BLB[:, 0:64].rearrange("p (t e) -> p e t", e=8),
                                                            axis=AX.X, op=ALU.add), R=[BLB], W=[BSEG])
                    k.op(DVE, lambda: dve.tensor_tensor(out=RB[:, 0:64], in0=UMB[:, 0:64], in1=OFS[:, 0:64], op=ALU.add),
                         R=[UMB, OFS], W=[RB])
                    k.op(DVE, lambda: dve.tensor_reduce(out=RSEG[:], in_=RB[:, 0:64].rearrange("p (t e) -> p e t", e=8),
                                                        axis=AX.X, op=ALU.max), R=[RB], W=[RSEG])
                    k.op(DVE, lambda p=p: dve.tensor_tensor(out=MLOC[:, p - 1, :], in0=RSEG[:], in1=BSEG[:, p - 1, :],
                                                            op=ALU.subtract), R=[RSEG, BSEG], W=[MLOC])
                    k.op(DVE, lambda: dve.tensor_tensor(out=EE[:, 0:64], in0=UT[:, 0:64], in1=OFS[:, 0:64], op=ALU.add),
                         R=[UT, OFS], W=[EE])
                    for t in range(8):
                        k.op(DVE, lambda t=t: dve.tensor_tensor(out=EE[:, t * 8:t * 8 + 8], in0=EE[:, t * 8:t * 8 + 8],
                                                                in1=RSEG[:], op=ALU.subtract), R=[EE, RSEG], W=[EE])
                    k.op(ACT, lambda: act.activation(out=EE[:, 0:64], in_=EE[:, 0:64], func=AF.Exp, bias=MISC[:, 0:1]),
                         R=[EE, MISC], W=[EE])
                    summary_vpb(0)

                def summary_a(p):
                    summary_mm(p, 0)
                    summary_vpb(1)

                VPB = HS.t[:].rearrange("p t c -> p (t c)").bitcast(BF16)[:, 0:8 * 4 * 132].rearrange(
                    "p (t h c) -> p t h c", t=8, h=4)

                def summary_vpb(d):
                    for t in range(8):
                        k.op(DVE, lambda t=t, d=d: dve.tensor_tensor(
                            out=VPB[:, t, :, 0:130], in0=VV[:, t, :, 0:130],
                            in1=EE[:, t * 8 + d * 4:t * 8 + d * 4 + 4].unsqueeze(2).to_broadcast([128, 4, 130]),
                            op=ALU.mult), R=VVv + [EE], W=HSV)

                def summary_mm(p, d):
                    pbs = [PS[2 + d * 2], PS[3 + d * 2]]
                    for h in range(4):
                        pb = pbs[h // 2]
                        c0 = (h % 2) * 132
                        k.group([lambda t=t, h=h, pb=pb, c0=c0: pe.matmul(pb[:, c0:c0 + 130],
                                                                         lhsT=KTOK[:, t, h * 128:(h + 1) * 128],
                                                                         rhs=VPB[:, t, h, 0:130],
                                                                         start=(t == 0), stop=(t == 7)) for t in range(8)],
                                R=KTOKv + HSV, W=[pb])
                        scb = SC[p - 1][d][h]
                        k.op(ACT, lambda scb=scb, pb=pb, c0=c0: act.activation(out=scb[:], in_=pb[:, c0:c0 + 130],
                                                                               func=AF.Copy),
                             R=[pb], W=[scb] + (SCX[id(scb)] or HSV))

                def summary_b(p):
                    summary_mm(p, 1)

                blks = []
                for p in (1, 2, 3):
                    base = (p - 1) * 8
                    ba = mk_blk([base + i for i in range(4)], 1, 64, False, 0)
                    bb = mk_blk([base + 4 + i for i in range(4)], 1, 64, False, 4)
                    blks += [bb, ba] if p == 1 else [ba, bb]
                blks.append(mk_blk([4, 5, 6, 7], 1, 64, True, 0))
                blks.append(mk_blk([8, 9, 10, 11], 1, 64, True, 4))
                for p in (1, 2, 3):
                    blks[2 * p]["pre_a"] = (lambda p=p: summary_pre(p))
                    blks[2 * p]["mid_a"] = (lambda p=p: summary_a(p))
                    blks[2 * p]["mid_b"] = (lambda p=p: summary_b(p))
                for j in range(4):
                    blks[j]["mods"] = [4 + 2 * j, 5 + 2 * j]
                def fold_states():
                    for d in range(2):
                        for h in range(4):
                            k.dma(SP, CST[d][h][:, 0:128], stC[d, h, :, :], W=[CST[d][h]])
                            with nc.allow_non_contiguous_dma(reason="state column"):
                                k.dma(SP, CST[d][h][:, 128:129], stn[d, h, :].rearrange("(p o) -> p o", o=1), W=[CST[d][h]])
                    stage(5.6)
                    k.op(DVE, lambda: dve.tensor_copy(out=MFO[:], in_=M0B[:]), R=[M0B], W=[MFO])
                    for d in range(2):
                        for p in ((1, 2, 3) if d == 0 else (3, 2, 1)):
                            fi = d * 3 + p - 1
                            dsl = slice(d * 4, (d + 1) * 4)
                            k.op(DVE, lambda: dve.scalar_tensor_tensor(out=FT[:, 0:4], in0=BSEG[:, p - 1, dsl],
                                                                       scalar=FLB[:, fi:fi + 1], in1=MFO[:, dsl],
                                                                       op0=ALU.mult, op1=ALU.subtract), R=[BSEG, FLB, MFO], W=[FT])
                            k.op(DVE, lambda: dve.tensor_scalar(out=FT[:, 0:4], in0=FT[:, 0:4], scalar1=-1.0, scalar2=None,
                                                                op0=ALU.mult), R=[FT], W=[FT])
                            k.op(DVE, lambda: dve.tensor_scalar(out=FT[:, 4:8], in0=MLOC[:, p - 1, dsl], scalar1=FLB[:, fi:fi + 1],
                                                                scalar2=NEGF[:, fi:fi + 1], op0=ALU.mult, op1=ALU.add),
                                 R=[MLOC, FLB, NEGF], W=[FT])
                            k.op(DVE, lambda: dve.tensor_tensor(out=MFO[:, dsl], in0=FT[:, 0:4], in1=FT[:, 4:8], op=ALU.max),
                                 R=[FT], W=[MFO])
                            for q in range(2):
                                k.op(DVE, lambda q=q: dve.tensor_tensor(out=FT[:, 8 + q * 4:12 + q * 4], in0=FT[:, q * 4:q * 4 + 4],
                                                                        in1=MFO[:, dsl], op=ALU.subtract), R=[FT, MFO], W=[FT])
                            k.op(ACT, lambda: act.activation(out=FT[:, 8:16], in_=FT[:, 8:16], func=AF.Exp), R=[FT], W=[FT])
                            for h in range(4):
                                k.op(DVE, lambda h=h: dve.tensor_scalar(out=CST[d][h][:, 0:130], in0=CST[d][h][:, 0:130],
                                                                        scalar1=FT[:, 8 + h:9 + h], scalar2=None, op0=ALU.mult),
                                     R=[CST[d][h], FT], W=[CST[d][h]])
                                k.op(DVE, lambda h=h: dve.scalar_tensor_tensor(out=CST[d][h][:, 0:130], in0=SC[p - 1][d][h][:],
                                                                               scalar=FT[:, 12 + h:13 + h], in1=CST[d][h][:, 0:130],
                                                                               op0=ALU.mult, op1=ALU.add),
                                     R=[SC[p - 1][d][h], FT, CST[d][h]] + (SCX[id(SC[p - 1][d][h])] or HSV), W=[CST[d][h]])

                blks[7]['mid_b'] = fold_states
                blks[0]["extras"] = prompt_extras
                def gate_bcast(off):
                    for v in range(2):
                        pt = PS[0]
                        k.group([lambda: pe.transpose(pt[0:8, 0:128], MODP[:, off:off + 8, v], IDF())], R=[MODP, CF], W=[pt])
                        k.op(DVE, lambda: dve.tensor_copy(out=UTT[0:8, :], in_=pt[0:8, 0:128]), R=[pt], W=[UTT])
                        for hf in range(2):
                            pbk = PS[1 + hf]
                            fns = []
                            for c in range(4):
                                kk = hf * 4 + c
                                fns.append(lambda kk=kk, c=c: pe.matmul(pbk[:, c * 128:(c + 1) * 128],
                                                                        lhsT=OH[0:8, kk, :],
                                                                        rhs=UTT[0:8, :], start=True, stop=True))
                            k.group(fns, R=[OH, UTT], W=[pbk])
                            k.op(DVE, lambda hf=hf, v=v, pbk=pbk: dve.tensor_copy(out=GAB[:, v, hf * 512:(hf + 1) * 512],
                                                                                 in_=pbk[:]), R=[pbk], W=[GAB])


                run_blocks([prompt_blk] + blks)
                stage(5.5)
                tok_a = k.all_tokens(k.bufs)
                GAB.w.update(tok_a)
                GAB.r.update(tok_a)
                X = [k.view(f"X{t}", XWALL.t[:, t * 1024:(t + 1) * 1024], tok_a) for t in range(NT)]
                WOE = [k.view(f"WOE{i}", HT[i].t[:].rearrange("p k n -> p (k n)").rearrange("p (k n) -> p k n", n=1024),
                              tok_a) for i in range(2)]
                wout_view = w_out.rearrange("(k p) n -> p k n", p=128)
                for kk in range(8):
                    k.dma(POOL, WOE[kk // 4][:, kk % 4, :], wout_view[:, kk, :], W=[WOE[kk // 4]])
                for t in range(NT):
                    src = xp[t * 128:(t + 1) * 128, :] if t < NT_P else xs[(t - NT_P) * 128:(t - NT_P + 1) * 128, :]
                    k.dma(POOL, X[t][:], src, W=[X[t]])
                gate_bcast(16)
                crot = [0]

                def phase_c(t):
                    v = 0 if t < NT_P else 1
                    for hf in range(2):
                        pb = PS[crot[0] % 8]
                        crot[0] += 1
                        fns = []
                        for kc in range(8):
                            src_ = AOT if kc < 4 else BOT
                            fns.append(lambda kc=kc, src_=src_: pe.matmul(pb[:], lhsT=src_[:, t, kc % 4, :],
                                                                          rhs=WOE[kc // 4][:, kc % 4, hf * 512:(hf + 1) * 512],
                                                                          start=(kc == 0), stop=(kc == 7)))
                        k.group(fns, R=[AOT, BOT] + WOE, W=[pb])
                        tmpc = PRE[crot[0] % 2]
                        k.op(DVE, lambda: dve.tensor_tensor(out=tmpc[:], in0=pb[:], in1=GAB[:, v, hf * 512:(hf + 1) * 512],
                                                            op=ALU.mult), R=[pb, GAB], W=[tmpc])
                        eng, hh = (DVE, dve) if hf == 0 else (POOL, pool)
                        k.op(eng, lambda hh=hh: hh.tensor_tensor(out=X[t][:, hf * 512:(hf + 1) * 512],
                                                                 in0=X[t][:, hf * 512:(hf + 1) * 512], in1=tmpc[:],
                                                                 op=ALU.add), R=[X[t], tmpc], W=[X[t]])
                stage(5.7)
                gates_unit(8, [list(range(8))], [[(MFO[:, 0:4], MFO), (MFO[:, 4:8], MFO)]])
                hs_written.clear()
                for w in range(8):
                    mlstm_wave([(w, 0, h) for h in range(4)] + [(7 - w, 1, h) for h in range(4)])
                for t in range(NT_P):
                    phase_c(t)
                mlstm_post_unit(list(range(8)), [4 + u for u in range(8)])
                for t in range(NT_P, NT):
                    phase_c(t)
                gate_bcast(40)

                stage(6)
                era1_tokens = k.all_tokens(k.bufs)

            GAB.w.update(era1_tokens)
            GAB.r.update(era1_tokens)

            esx = ExitStack()
            with esx:
                def nbx(es, name, shape, dt, toks):
                    b = k.sb(es, name, shape, dt)
                    b.w.update(toks)
                    b.r.update(toks)
                    return b
                era_tokens = k.all_tokens(k.bufs)

                stage(7)
                es2 = ExitStack()
                es2.__enter__()

                def nb(name, shape, dt):
                    return nbx(es2, name, shape, dt, era_tokens)
                MB = 6 * 128
                W2 = nb("W2", [128, NFF, D], BF16)
                GTF = nb("GTF", [128, NFF, MB], BF16)
                GFB = nb("GFB", [128, D], F32)
                k.dma(SP, GFB[:], g_final.partition_broadcast(128), W=[GFB])
                aflat = AOT.t[:].rearrange("p t c d -> p (t c d)")
                bflat = BOT.t[:].rearrange("p t c d -> p (t c d)")
                H2 = k.view("H2", aflat.rearrange("p (k n) -> p k n", n=MB), era_tokens)
                W13 = [k.view(f"W13_{i}", bflat[:, i * 2048:(i + 1) * 2048].rearrange("p (w k n) -> p w k n", w=2, k=8),
                              era_tokens) for i in range(3)]
                SA = [nb(f"SA{i}", [128, 384], BF16) for i in range(2)]
                TM2 = nb("TM2", [128, 512], F32)
                OUT = [nb(f"OUT{i}", [128, D], F32) for i in range(2)]
                w1v = w1.rearrange("(k p) n -> p k n", p=128)
                w3v = w3.rearrange("(k p) n -> p k n", p=128)
                w2v = w2.rearrange("(f p) n -> p f n", p=128)
                orot = [0]
                for mb in range(2):
                    tiles = list(range(mb * 6, mb * 6 + 6))
                    for i, t in enumerate(tiles):
                        v = 0 if t < NT_P else 1
                        ssl = 24 + i
                        k.op(ACT, lambda t=t, ssl=ssl: act.activation(out=JUNK[:], in_=X[t][:], func=AF.Square,
                                                                      accum_out=SS[:, ssl:ssl + 1]), R=[X[t]], W=[JUNK, SS])
                        rstd_of(SS[:, ssl:ssl + 1], RS[:, ssl:ssl + 1], 1, 1.0 / D)
                        xn = XN[i % 4]
                        k.op(DVE, lambda t=t, ssl=ssl, xn=xn: dve.tensor_scalar(out=xn[:], in0=X[t][:], scalar1=RS[:, ssl:ssl + 1],
                                                                               scalar2=None, op0=ALU.mult),
                             R=[X[t], RS], W=[xn])
                        pb = PS[i % 2]
                        pbv = pb[:].bitcast(BF16)
                        k.group([lambda kk=kk, xn=xn, pbv=pbv: pe.transpose(pbv[:, kk * 128:(kk + 1) * 128],
                                                                           xn[:, kk * 128:(kk + 1) * 128], IDB)
                                 for kk in range(8)], R=[xn, CB16], W=[pb])
                        for kk in range(8):
                            k.op(DVE, lambda kk=kk, i=i, v=v, pbv=pbv: dve.tensor_scalar(
                                out=H2[:, kk, i * 128:(i + 1) * 128], in0=pbv[:, kk * 128:(kk + 1) * 128],
                                scalar1=S2[:, kk, v:v + 1], scalar2=MODP[:, 24 + kk, v:v + 1], op0=ALU.mult, op1=ALU.add),
                                R=[pb, S2, MODP], W=[H2])
                    for f in range(NFF):
                        slot = W13[f % 3]
                        k.dma(POOL, slot[:, 0, :, :], w1v[:, :, f * 128:(f + 1) * 128], W=[slot])
                        k.dma(POOL, slot[:, 1, :, :], w3v[:, :, f * 128:(f + 1) * 128], W=[slot])
                        if mb == 0:
                            k.dma(POOL, W2[:, f, :], w2v[:, f, :], W=[W2])
                        for hf in range(2):
                            pa = PS[2 + (f * 4 + hf * 2) % 6]
                            pbb = PS[2 + (f * 4 + hf * 2 + 1) % 6]
                            for (pp, wi) in ((pa, 0), (pbb, 1)):
                                k.group([lambda kk=kk, pp=pp, wi=wi: pe.matmul(pp[:, 0:384], lhsT=slot[:, wi, kk, :],
                                                                               rhs=H2[:, kk, hf * 384:(hf + 1) * 384],
                                                                               start=(kk == 0), stop=(kk == 7))
                                         for kk in range(8)], R=[slot, H2], W=[pp])
                            sa = SA[hf]
                            k.op(ACT, lambda pa=pa, sa=sa: act.activation(out=sa[:], in_=pa[:, 0:384], func=AF.Silu),
                                 R=[pa], W=[sa])
                            k.op(DVE, lambda pbb=pbb, sa=sa, f=f, hf=hf: dve.tensor_tensor(
                                out=GTF[:, f, hf * 384:(hf + 1) * 384], in0=pbb[:, 0:384], in1=sa[:], op=ALU.mult),
                                R=[pbb, sa], W=[GTF])
                    for i, t in enumerate(tiles):
                        v = 0 if t < NT_P else 1
                        for hf in range(2):
                            pb = PS[(i * 2 + hf) % 8]
                            k.group([lambda f=f, pb=pb: pe.matmul(pb[:], lhsT=GTF[:, f, i * 128:(i + 1) * 128],
                                                                  rhs=W2[:, f, hf * 512:(hf + 1) * 512],
                                                                  start=(f == 0), stop=(f == NFF - 1)) for f in range(NFF)],
                                    R=[GTF, W2], W=[pb])
                            k.op(DVE, lambda pb=pb, v=v, hf=hf: dve.tensor_tensor(out=TM2[:], in0=pb[:],
                                                                                 in1=GAB[:, v, hf * 512:(hf + 1) * 512],
                                                                                 op=ALU.mult), R=[pb, GAB], W=[TM2])
                            k.op(POOL, lambda t=t, hf=hf: pool.tensor_tensor(out=X[t][:, hf * 512:(hf + 1) * 512],
                                                                             in0=X[t][:, hf * 512:(hf + 1) * 512],
                                                                             in1=TM2[:], op=ALU.add),
                                 R=[X[t], TM2], W=[X[t]])
                        ssl = 32 + i
                        k.op(ACT, lambda t=t, ssl=ssl: act.activation(out=JUNK[:], in_=X[t][:], func=AF.Square,
                                                                      accum_out=SS[:, ssl:ssl + 1]), R=[X[t]], W=[JUNK, SS])
                        rstd_of(SS[:, ssl:ssl + 1], RS[:, ssl:ssl + 1], 1, 1.0 / D)
                        ob = OUT[orot[0] % 2]
                        orot[0] += 1
                        k.op(DVE, lambda t=t, ssl=ssl, ob=ob: dve.scalar_tensor_tensor(
                            out=ob[:], in0=X[t][:], scalar=RS[:, ssl:ssl + 1], in1=GFB[:], op0=ALU.mult, op1=ALU.mult),
                            R=[X[t], RS, GFB], W=[ob])
                        dst = yp[t * 128:(t + 1) * 128, :] if t < NT_P else ys[(t - NT_P) * 128:(t - NT_P + 1) * 128, :]
                        k.dma(SP, dst, ob[:], R=[ob], final=True)

                es2.__exit__(None, None, None)
        except StageExit:
            pass
        k.finish()
    return nc


_NC_CACHE = {}


def make_consts():
    c = np.zeros((128, 512), np.float32)
    c[:, 0:128] = np.eye(128, dtype=np.float32)
    s = np.arange(128)[:, None]
    j = np.arange(128)[None, :]
    c[:, 128:256] = (s <= j).astype(np.float32)
    c[:, 256:384] = (s >= j).astype(np.float32)
    c[:, 384:512] = 1.0
    return c


def kernel(x_prompt, x_sample, state_C, state_n, state_m, c, c_ctx, w_ada, b_ada, g_norm1,
           w_in, b_gate, w_s, b_s, g_v, conv_w, conv_b, g_h, w_out, g_norm2, w1, w3, w2, g_final):
    f = lambda a: np.ascontiguousarray(np.asarray(a, dtype=np.float32))
    x_prompt, x_sample = f(x_prompt), f(x_sample)
    state_C, state_n, state_m, c, c_ctx = f(state_C), f(state_n), f(state_m), f(c), f(c_ctx)
    if "nc" not in _NC_CACHE:
        _NC_CACHE["nc"] = build_nc()
    nc = _NC_CACHE["nc"]
    consts = make_consts()
    shared = {
        "consts": consts, "w_ada": f(w_ada)[0], "b_ada": f(b_ada)[0], "g_norm1": f(g_norm1)[0], "w_in": f(w_in)[0],
        "b_gate": f(b_gate)[0].reshape(1, 16), "w_s": f(w_s)[0], "b_s": f(b_s)[0].reshape(1, 512),
        "g_v": f(g_v)[0].reshape(1, 512), "conv_w": f(conv_w)[0], "conv_b": f(conv_b)[0],
        "g_h": f(g_h)[0].reshape(1, 512), "w_out": f(w_out)[0], "g_norm2": f(g_norm2)[0], "w1": f(w1)[0],
        "w3": f(w3)[0], "w2": f(w2)[0], "g_final": f(g_final).reshape(1, D),
    }
    in_maps = []
    for core in range(8):
        b, j = core // 4, core % 4
        m = dict(shared)
        m["xp"] = x_prompt[2 * core:2 * core + 2].reshape(512, D)
        m["xs"] = x_sample[b, j * 1024:(j + 1) * 1024]
        m["xf"] = np.concatenate([x_sample[b, ((j + p) % 4) * 1024:((j + p) % 4 + 1) * 1024] for p in (1, 2, 3)], 0)
        m["cvec"] = np.stack([c_ctx, c[b]], 0)
        m["stC"] = state_C[b, 0]
        m["stn"] = state_n[b, 0]
        m["stm"] = state_m[b, 0].reshape(1, 8)
        fl = np.zeros((1, 6), np.float32)
        for p in (1, 2, 3):
            fl[0, p - 1] = 1.0 if p >= 4 - j else 0.0
            fl[0, 3 + p - 1] = 1.0 if p <= 3 - j else 0.0
        m["flg"] = fl
        in_maps.append({kk: np.ascontiguousarray(vv) for kk, vv in m.items()})
    res = run_bass_kernel_spmd(nc, in_maps, core_ids=list(range(8)))
    B = x_prompt.shape[0]
    y_prompt = np.zeros((B, 256, D), np.float32)
    y_sample = np.zeros((2, 4096, D), np.float32)
    new_C = np.zeros((B, 1, 2, 4, 128, 128), np.float32)
    new_n = np.zeros((B, 1, 2, 4, 128), np.float32)
    new_m = np.zeros((B, 1, 2, 4), np.float32)
    for core in range(8):
        r = res.results[core]
        b, j = core // 4, core % 4
        y_prompt[2 * core:2 * core + 2] = r["yp"].reshape(2, 256, D)
        y_sample[b, j * 1024:(j + 1) * 1024] = r["ys"]
        new_C[2 * core:2 * core + 2, 0] = r["nC"]
        new_n[2 * core:2 * core + 2, 0] = r["nn"]
        new_m[2 * core:2 * core + 2, 0] = r["nm"].reshape(2, 2, 4)
    return (y_prompt, y_sample, new_C, new_n, new_m)
```

```python
import numpy as np
from contextlib import ExitStack
import concourse.bass as bass
import concourse.mybir as mybir
from concourse.bass_utils import run_bass_kernel_spmd

F32 = mybir.dt.float32
BF16 = mybir.dt.bfloat16
AF = mybir.ActivationFunctionType
ALU = mybir.AluOpType
AX = mybir.AxisListType

D = 1024
DIN = 3088
DFF = 2816
NFF = DFF // 128
OFF_U, OFF_V, OFF_Q, OFF_K, OFF_VV, OFF_O, OFF_G = 0, 512, 1024, 1536, 2048, 2560, 3072
EPS = 1e-6
NEG = -1e30
LNS = float(-0.5 * np.log(128.0))
NT_P = 4
NT_S = 8
NT = NT_P + NT_S
NFOR = 24
import os
STAGE = float(os.environ.get("KSTAGE", "99"))


class StageExit(Exception):
    pass


DEAD = [False]
STRICT = False


def stage(n):
    if STAGE <= n:
        DEAD[0] = True


class Eng:
    def __init__(self, h, name):
        self.h = h
        self.name = name
        self.sem = None
        self.n = 0
        self.waited = {}


class Buf:
    def __init__(self, name, t):
        self.name = name
        self.t = t
        self.w = {}
        self.r = {}
        self.sem = None
        self.nd = 0
        self.psum = False

    def __getitem__(self, idx):
        return self.t[idx]


class K:
    def __init__(self, nc, es):
        self.nc = nc
        self.es = es
        self.PE = Eng(nc.tensor, "pe")
        self.ACT = Eng(nc.scalar, "act")
        self.DVE = Eng(nc.vector, "dve")
        self.POOL = Eng(nc.gpsimd, "pool")
        self.SP = Eng(nc.sync, "sp")
        self.engs = [self.PE, self.ACT, self.DVE, self.POOL, self.SP]
        for e in self.engs:
            e.sem = es.enter_context(nc.semaphore("s_" + e.name))
        self.nsem = 5
        self.final = []
        self.bufs = []

    def sb(self, es, name, shape, dt):
        b = Buf(name, es.enter_context(self.nc.sbuf_tensor(name, shape, dt)))
        self.bufs.append(b)
        return b

    def ps(self, es, name, shape, dt):
        b = Buf(name, es.enter_context(self.nc.psum_tensor(name, shape, dt)))
        b.psum = True
        self.bufs.append(b)
        return b

    def _need(self, e, toks):
        for key, (sem, val) in toks:
            if e.waited.get(id(sem), 0) >= val:
                continue
            e.h.wait_ge(sem, val)
            e.waited[id(sem)] = val

    def _deps(self, e, R, W):
        toks = []
        for b in R:
            for key, tok in b.w.items():
                if key == e.name and e is self.PE:
                    continue
                toks.append((key, tok))
            if b.psum:
                for key, tok in b.r.items():
                    if key != e.name:
                        toks.append((key, tok))
        for b in W:
            for key, tok in b.w.items():
                if key == e.name and (e is self.PE or not STRICT):
                    continue
                toks.append((key, tok))
            for key, tok in b.r.items():
                if key == e.name and (e is self.PE or not STRICT):
                    continue
                toks.append((key, tok))
        return toks

    def op(self, e, fn, R=(), W=()):
        if DEAD[0]:
            return None
        pend = []
        for key, (sem, val) in self._deps(e, R, W):
            if e.waited.get(id(sem), 0) >= val:
                continue
            e.waited[id(sem)] = val
            pend = [p for p in pend if p[0] is not sem] + [(sem, val)]
        for sem, val in pend[:-1]:
            e.h.wait_ge(sem, val)
        inst = fn()
        if pend:
            inst._wait_ge(pend[-1][0], pend[-1][1])
        inst.then_inc(e.sem, 1)
        e.n += 1
        tok = (e.sem, e.n)
        for b in R:
            b.r[e.name] = tok
        for b in W:
            b.w[e.name] = tok
        return tok

    def group(self, fns, R=(), W=()):
        e = self.PE
        if DEAD[0]:
            return None
        self._need(e, self._deps(e, R, W))
        inst = None
        for fn in fns:
            inst = fn()
        inst.then_inc(e.sem, 1)
        e.n += 1
        tok = (e.sem, e.n)
        for b in R:
            b.r[e.name] = tok
        for b in W:
            b.w[e.name] = tok
        return tok

    def dma(self, q, out, in_, R=(), W=(), final=False, **kw):
        owner = W[0] if len(W) else R[0]
        if DEAD[0]:
            return None
        if owner.sem is None:
            owner.sem = self.es.enter_context(self.nc.semaphore("d_" + owner.name))
            self.nsem += 1
        toks = []
        okey = "dma:" + owner.name
        for b in R:
            toks += list(b.w.items())
        for b in W:
            toks += [kv for kv in b.w.items() if kv[0] != okey] + list(b.r.items())
        self._need(q, toks)
        inst = q.h.dma_start(out=out, in_=in_, **kw)
        inst.then_inc(owner.sem, 16)
        owner.nd += 1
        tok = (owner.sem, 16 * owner.nd)
        key = "dma:" + owner.name
        for b in R:
            b.r[key] = tok
        for b in W:
            b.w[key] = tok
        if final:
            self.final.append((key, tok))
        return tok

    def view(self, name, ap, toks=None):
        b = Buf(name, ap)
        if toks:
            b.w.update(toks)
            b.r.update(toks)
        self.bufs.append(b)
        return b

    def all_tokens(self, bufs):
        toks = {}
        for e in self.engs:
            if e.n:
                toks[e.name] = (e.sem, e.n)
        for b in bufs:
            for d in (b.w, b.r):
                for key, tok in d.items():
                    if key.startswith("dma:"):
                        toks[key] = tok
        return toks

    def finish(self):
        self._need(self.SP, self.final)
        toks = [(e.name, (e.sem, e.n)) for e in self.engs if e.n and e is not self.SP]
        self._need(self.SP, toks)


def build_nc(dbg_names=()):
    nc = bass.Bass("TRN2", target_bir_lowering=False)
    DEAD[0] = False

    def din(name, shape):
        return nc.dram_tensor(name, list(shape), F32, kind="ExternalInput").ap()

    def dout(name, shape):
        return nc.dram_tensor(name, list(shape), F32, kind="ExternalOutput").ap()

    xp = din("xp", [NT_P * 128, D])
    xs = din("xs", [NT_S * 128, D])
    xf = din("xf", [NFOR * 128, D])
    cvec = din("cvec", [2, D])
    stC = din("stC", [2, 4, 128, 128])
    stn = din("stn", [2, 4, 128])
    stm = din("stm", [1, 8])
    flg = din("flg", [1, 6])
    consts = din("consts", [128, 512])
    w_ada = din("w_ada", [D, 6 * D])
    b_ada = din("b_ada", [6 * D])
    g_norm1 = din("g_norm1", [D])
    w_in = din("w_in", [D, DIN])
    b_gate = din("b_gate", [1, 16])
    w_s = din("w_s", [4, 128, 128])
    b_s = din("b_s", [1, 512])
    g_v = din("g_v", [1, 512])
    conv_w = din("conv_w", [3, 1024])
    conv_b = din("conv_b", [1024])
    g_h = din("g_h", [1, 512])
    w_out = din("w_out", [D, D])
    g_norm2 = din("g_norm2", [D])
    w1 = din("w1", [D, DFF])
    w3 = din("w3", [D, DFF])
    w2 = din("w2", [DFF, D])
    g_final = din("g_final", [1, D])

    yp = dout("yp", [NT_P * 128, D])
    ys = dout("ys", [NT_S * 128, D])
    nC = dout("nC", [2, 2, 4, 128, 128])
    nn = dout("nn", [2, 2, 4, 128])
    nm = dout("nm", [2, 8])
    dbg = {}

    es0 = ExitStack()
    with es0:
        k = K(nc, es0)
        PE, ACT, DVE, POOL, SP = k.PE, k.ACT, k.DVE, k.POOL, k.SP
        pe, act, dve, pool, sp = nc.tensor, nc.scalar, nc.vector, nc.gpsimd, nc.sync

        AOT = k.sb(es0, "AOT", [128, NT, 4, 128], BF16)
        BOT = k.sb(es0, "BOT", [128, NT, 4, 128], BF16)
        CF = k.sb(es0, "CF", [128, 512], F32)
        CB16 = k.sb(es0, "CB16", [128, 512], BF16)
        MODP = k.sb(es0, "MODP", [128, 48, 2], F32)
        S1 = k.sb(es0, "S1", [128, 8, 2], F32)
        S2 = k.sb(es0, "S2", [128, 8, 2], F32)
        G1 = k.sb(es0, "G1", [128, 8], F32)
        G2 = k.sb(es0, "G2", [128, 8], F32)
        GAB = k.sb(es0, "GAB", [128, 2, D], F32)
        SS = k.sb(es0, "SS", [128, 64], F32)
        RS = k.sb(es0, "RS", [128, 64], F32)
        MHALF = k.sb(es0, "MHALF", [128, 32], F32)
        OH = k.sb(es0, "OH", [8, 8, 128], F32)
        UTT = k.sb(es0, "UTT", [64, 128], F32)
        JUNK = k.sb(es0, "JUNK", [128, D], BF16)
        XN = [k.sb(es0, f"XN{i}", [128, D], BF16) for i in range(4)]
        PS = [k.ps(es0, f"PS{i}", [128, 512], F32) for i in range(8)]

        IDF = lambda n=128: CF[0:n, 0:n]
        TRIF = CF[:, 128:256]
        TRIB = CF[:, 256:384]
        ONESF = CF[:, 384:512]
        IDB = CB16[:, 0:128]
        MASK = [CB16[:, 128:256], CB16[:, 256:384]]
        ONESB = CB16[:, 384:512]

        k.dma(SP, CF[:], consts, W=[CF])
        k.dma(POOL, CB16[:], consts, W=[CB16])
        k.op(DVE, lambda: dve.memset(MHALF[:], -0.5), W=[MHALF])

        def rstd_of(ss_ap, out_ap, n, inv):
            k.op(POOL, lambda: pool.tensor_scalar(out=out_ap, in0=ss_ap, scalar1=inv, scalar2=EPS,
                                                  op0=ALU.mult, op1=ALU.add), R=[SS], W=[RS])
            k.op(POOL, lambda: pool.tensor_tensor(out=out_ap, in0=out_ap, in1=MHALF[:, 0:n], op=ALU.pow),
                 R=[RS, MHALF], W=[RS])

        try:
            es1 = ExitStack()
            with es1:
                WIN = k.sb(es1, "WIN", [128, 8, DIN], BF16)
                HT = [k.sb(es1, f"HT{i}", [128, 8, 512], BF16) for i in range(2)]
                XS = [k.view(f"XS{i}", GAB.t[:, i, :]) for i in range(2)]
                QT = k.sb(es1, "QT", [128, 4, NT_S * 128], BF16)
                KT = k.sb(es1, "KT", [128, 4, NT_S * 128], BF16)
                KTOK = k.sb(es1, "KTOK", [128, NT_S, 512], BF16)
                VV = k.sb(es1, "VV", [128, NT_S, 4, 132], BF16)
                TH = k.sb(es1, "TH", [128, NT_S, 512], BF16)
                HS = k.sb(es1, "HS", [128, NT_S, 512], F32)
                GUT = k.sb(es1, "GUT", [128, 4, 512], BF16)
                GV = k.sb(es1, "GV", [128, 512], F32)
                VGS = [k.sb(es1, f"VG{i}", [128, 512], BF16) for i in range(2)]
                PRE = [k.sb(es1, f"PRE{i}", [128, 512], F32) for i in range(3)]
                GT = k.sb(es1, "GT", [128, NT_S, 16], F32)
                SPT = k.sb(es1, "SPT", [128, NT_S * 8], F32)
                UT = k.sb(es1, "UT", [128, NT_S * 8], F32)
                CBT = k.sb(es1, "CBT", [128, NT_S * 8], F32)
                EE = k.sb(es1, "EE", [128, NT_S * 8], F32)
                THR = k.sb(es1, "THR", [128, NT_S * 8], F32)
                UMB = k.sb(es1, "UMB", [128, NT_S * 8], F32)
                BLB = k.sb(es1, "BLB", [128, NT_S * 8], F32)
                RB = k.sb(es1, "RB", [128, NT_S * 8], F32)
                MPV = k.sb(es1, "MPV", [128, NT_S * 8], F32)
                DCY = k.sb(es1, "DCY", [128, NT_S * 8], F32)
                MCUR = k.sb(es1, "MCUR", [128, 8], F32)
                UMX = k.sb(es1, "UMX", [64, 1], F32)
                DG = k.sb(es1, "DG", [64, 64], F32)
                CST = [[k.sb(es1, f"CST{d}{h}", [128, 132], F32) for h in range(4)] for d in range(2)]
                CBF = [k.sb(es1, f"CBF{i}", [128, 132], BF16) for i in range(8)]
                VP = [k.sb(es1, f"VP{i}", [128, 132], BF16) for i in range(8)]
                ST = [k.sb(es1, f"ST{i}", [128, 128], BF16) for i in range(8)]
                DD = k.sb(es1, "DD", [128, 32], F32)
                WST = k.sb(es1, "WST", [128, 4, 128], BF16)
                BSR = k.sb(es1, "BSR", [1, 512], BF16)
                GVB = k.sb(es1, "GVB", [128, 512], F32)
                GHB = k.sb(es1, "GHB", [128, 512], F32)
                BGB = k.sb(es1, "BGB", [128, 16], F32)
                CW = k.sb(es1, "CW", [128, 8, 3], F32)
                CBI = k.sb(es1, "CBI", [128, 8], F32)
                BADA = k.sb(es1, "BADA", [128, 48], F32)
                CT = k.sb(es1, "CT", [128, 8, 2], F32)
                SCT = k.sb(es1, "SCT", [128, 8, 2], BF16)
                WAB = [AOT, BOT]
                M0B = k.sb(es1, "M0B", [128, 8], F32)
                FLB = k.sb(es1, "FLB", [128, 6], F32)
                MISC = k.sb(es1, "MISC", [128, 16], F32)
                HTH = {id(HT[i]): [k.view(f"HT{i}lo", HT[i].t[:, 0:4, :]), k.view(f"HT{i}hi", HT[i].t[:, 4:8, :])]
                       for i in range(2)}
                KTv = [k.view(f"KT{i}", KT.t[:, :, i * 512:(i + 1) * 512]) for i in range(2)]
                QTv = [k.view(f"QT{i}", QT.t[:, :, i * 512:(i + 1) * 512]) for i in range(2)]
                KTOKv = [k.view(f"KTOKh{i}", KTOK.t[:, i * 4:(i + 1) * 4, :]) for i in range(2)]
                VVv = [k.view(f"VVh{i}", VV.t[:, i * 4:(i + 1) * 4, :, :]) for i in range(2)]
                GTv = [k.view(f"GTh{i}", GT.t[:, i * 4:(i + 1) * 4, :]) for i in range(2)]
                NEGT = k.sb(es1, "NEGT", [128, 4], F32)
                NEGF = k.sb(es1, "NEGF", [128, 6], F32)
                MLOC = k.sb(es1, "MLOC", [128, 3, 8], F32)
                BSEG = k.sb(es1, "BSEG", [128, 3, 8], F32)
                MFO = k.sb(es1, "MFO", [128, 8], F32)
                OFS = k.sb(es1, "OFS", [128, 64], F32)
                RSEG = k.sb(es1, "RSEG", [128, 8], F32)
                FT = k.sb(es1, "FT", [128, 16], F32)
                _botf = BOT.t[:, 4:12, :, :].rearrange("p t c d -> p (t c d)").bitcast(F32)
                _hsf = HS.t[:].rearrange("p t c -> p (t c)")
                SC = [[[None] * 4 for d in range(2)] for p in range(3)]
                SCX = {}
                for _p in range(3):
                    for _d in range(2):
                        for _h in range(4):
                            _n = _p * 8 + _d * 4 + _h
                            if _n < 15:
                                _b = k.view(f"SC{_p}{_d}{_h}", _botf[:, _n * 130:(_n + 1) * 130])
                                SCX[id(_b)] = [BOT]
                            else:
                                _o = 2112 + (_n - 15) * 130
                                _b = k.view(f"SC{_p}{_d}{_h}", _hsf[:, _o:_o + 130])
                                SCX[id(_b)] = None
                            SC[_p][_d][_h] = _b

                with nc.allow_non_contiguous_dma(reason="small param relayout"):
                    for v in range(2):
                        k.dma(SP, CT[:, :, v], cvec[v, :].rearrange("(k p) -> p k", p=128), W=[CT])
                    k.dma(SP, BADA[:], b_ada.rearrange("(c p) -> p c", p=128), W=[BADA])
                    k.dma(SP, G1[:], g_norm1.rearrange("(k p) -> p k", p=128), W=[G1])
                    k.dma(SP, G2[:], g_norm2.rearrange("(k p) -> p k", p=128), W=[G2])
                    for j in range(3):
                        k.dma(SP, CW[:, :, j], conv_w[j, :].rearrange("(k p) -> p k", p=128), W=[CW])
                    k.dma(SP, CBI[:], conv_b.rearrange("(k p) -> p k", p=128), W=[CBI])

                k.dma(SP, GVB[:], g_v.partition_broadcast(128), W=[GVB])
                k.dma(SP, GHB[:], g_h.partition_broadcast(128), W=[GHB])
                k.dma(SP, BGB[:], b_gate.partition_broadcast(128), W=[BGB])
                k.dma(SP, M0B[:], stm.partition_broadcast(128), W=[M0B])
                k.dma(SP, FLB[:], flg.partition_broadcast(128), W=[FLB])
                k.dma(POOL, BSR[:], b_s, W=[BSR])
                for g in range(4):
                    k.dma(SP, PRE[0][:, g * 128:(g + 1) * 128], w_s[g, :, :], W=[PRE[0]])
                k.group([lambda g=g: pe.transpose(PS[7][:, g * 128:(g + 1) * 128], PRE[0][:, g * 128:(g + 1) * 128], IDF())
                         for g in range(4)], R=[PRE[0], CF], W=[PS[7]])
                k.op(DVE, lambda: dve.tensor_copy(out=WST[:], in_=PS[7][:].rearrange("p (g t) -> p g t", t=128)),
                     R=[PS[7]], W=[WST])
                for kk in range(8):
                    k.op(DVE, lambda kk=kk: dve.tensor_scalar(out=OH[0:8, kk, :], in0=CF[0:8, 384:512],
                                                              scalar1=CF[0:8, kk:kk + 1], scalar2=None, op0=ALU.mult),
                         R=[CF], W=[OH])
                k.op(DVE, lambda: dve.tensor_scalar(out=GHB[:], in0=GHB[:], scalar1=0.5, scalar2=None, op0=ALU.mult),
                     R=[GHB], W=[GHB])
                for d in range(2):
                    for h in range(4):
                        k.op(POOL, lambda d=d, h=h: pool.memset(CST[d][h][:], 0.0), W=[CST[d][h]])
                k.op(POOL, lambda: pool.memset(VV[:, :, :, 128:132], 1.0), W=VVv)

                k.op(ACT, lambda: act.activation(out=SCT[:], in_=CT[:], func=AF.Silu), R=[CT], W=[SCT])
                wa_view = w_ada.rearrange("(k p) n -> p k n", p=128)

                def mod_block(cb):
                    slot = WAB[cb % 2]
                    sv = slot[:, 4:12, :, :].rearrange("p t c d -> p t (c d)")
                    k.dma(POOL, sv, wa_view[:, :, cb * 512:(cb + 1) * 512], W=[slot])
                    for i in range(4):
                        ch = cb * 4 + i
                        k.group([lambda kk=kk, i=i, ch=ch: pe.matmul(PS[0][:, ch * 2:ch * 2 + 2],
                                                                      lhsT=sv[:, kk, i * 128:(i + 1) * 128],
                                                                      rhs=SCT[:, kk, :], start=(kk == 0), stop=(kk == 7))
                                 for kk in range(8)], R=[slot, SCT], W=[PS[0]])

                def mod_load(cb):
                    sv = AOT[:, 4:12, :, :].rearrange("p t c d -> p t (c d)")
                    k.dma(POOL, sv, wa_view[:, :, cb * 512:(cb + 1) * 512], W=[AOT])

                def mod_compute(cb):
                    sv = AOT[:, 4:12, :, :].rearrange("p t c d -> p t (c d)")
                    pbm = PS[6]
                    for i in range(4):
                        k.group([lambda kk=kk, i=i: pe.matmul(pbm[:, i * 2:i * 2 + 2], lhsT=sv[:, kk, i * 128:(i + 1) * 128],
                                                              rhs=SCT[:, kk, :], start=(kk == 0), stop=(kk == 7))
                                 for kk in range(8)], R=[AOT, SCT], W=[pbm])
                    for v in range(2):
                        k.op(DVE, lambda v=v: dve.tensor_tensor(
                            out=MODP[:, cb * 4:cb * 4 + 4, v], in0=pbm[:, 0:8].rearrange("p (c v) -> p c v", v=2)[:, :, v],
                            in1=BADA[:, cb * 4:cb * 4 + 4], op=ALU.add), R=[pbm, BADA], W=[MODP])
                    if cb + 1 < 12:
                        mod_load(cb + 1)
                    if cb == 11:
                        for v in range(2):
                            k.op(DVE, lambda v=v: dve.scalar_tensor_tensor(out=S2[:, :, v], in0=MODP[:, 32:40, v], scalar=1.0,
                                                                           in1=G2[:], op0=ALU.add, op1=ALU.mult),
                                 R=[MODP, G2], W=[S2])

                def mod_finish(c0, c1):
                    for v in range(2):
                        k.op(DVE, lambda v=v: dve.tensor_tensor(
                            out=MODP[:, c0:c1, v], in0=PS[0][:, 2 * c0:2 * c1].rearrange("p (c v) -> p c v", v=2)[:, :, v],
                            in1=BADA[:, c0:c1], op=ALU.add), R=[PS[0], BADA], W=[MODP])

                for cb in range(4):
                    mod_block(cb)
                mod_finish(0, 16)
                for v in range(2):
                    k.op(DVE, lambda v=v: dve.scalar_tensor_tensor(out=S1[:, :, v], in0=MODP[:, 8:16, v], scalar=1.0,
                                                                   in1=G1[:], op0=ALU.add, op1=ALU.mult),
                         R=[MODP, G1], W=[S1])

                stage(1)
                win_view = w_in.rearrange("(k p) n -> p k n", p=128)
                for kk in range(8):
                    pass
                WING = {}
                for (g0, g1) in ((OFF_U, OFF_V), (OFF_Q, OFF_K), (OFF_K, OFF_VV), (OFF_VV, OFF_O), (OFF_G, DIN),
                                 (OFF_O, OFF_G), (OFF_V, OFF_Q)):
                    gb = k.view(f"WIN_{g0}", WIN.t[:, :, g0:g1])
                    for c0 in range(g0, g1, 128):
                        WING[c0] = gb
                    k.dma(POOL, WIN[:, :, g0:g1], win_view[:, :, g0:g1], W=[gb])

                ps_rot = {"proj": 0, "small": 0, "tr": 0}

                def bank(kind):
                    if kind == "tr":
                        b = PS[7 - ps_rot["tr"] % 2]
                    elif kind == "proj":
                        b = PS[2 + ps_rot["proj"] % 4]
                    else:
                        b = PS[6 + ps_rot["small"] % 2]
                    ps_rot[kind] += 1
                    return b

                xn_rot = [0]

                def norm_pre(xbuf, x_ap, ssl):
                    k.op(ACT, lambda: act.activation(out=JUNK[:], in_=x_ap, func=AF.Square, accum_out=SS[:, ssl:ssl + 1]),
                         R=[xbuf], W=[JUNK, SS])
                    rstd_of(SS[:, ssl:ssl + 1], RS[:, ssl:ssl + 1], 1, 1.0 / D)
                    xn = XN[xn_rot[0] % 4]
                    xn_rot[0] += 1
                    k.op(DVE, lambda: dve.tensor_scalar(out=xn[:], in0=x_ap, scalar1=RS[:, ssl:ssl + 1], scalar2=None,
                                                        op0=ALU.mult), R=[xbuf, RS], W=[xn])
                    return xn

                def norm_post(xn, Sx, SHoff, v, ht, col0):
                    pbs = [PS[0], PS[1]]
                    pvs = [pbs[0][:].bitcast(BF16), pbs[1][:].bitcast(BF16)]
                    for hb in range(2):
                        k.group([lambda kk=kk, hb=hb: pe.transpose(pvs[hb][:, (kk % 4) * 128:(kk % 4 + 1) * 128],
                                                                  xn[:, kk * 128:(kk + 1) * 128], IDB)
                                 for kk in range(hb * 4, hb * 4 + 4)], R=[xn, CB16], W=[pbs[hb]])
                    for j in range(4):
                        kk = j
                        k.op(DVE, lambda kk=kk, j=j: dve.tensor_scalar(out=ht[:, kk, col0:col0 + 128],
                                                                       in0=pvs[0][:, j * 128:(j + 1) * 128],
                                                                       scalar1=Sx[:, kk, v:v + 1],
                                                                       scalar2=MODP[:, SHoff + kk, v:v + 1],
                                                                       op0=ALU.mult, op1=ALU.add),
                             R=[pbs[0], Sx, MODP], W=[HTH[id(ht)][0]])
                        kk = 4 + j
                        k.op(ACT, lambda kk=kk, j=j: act.activation(out=ht[:, kk, col0:col0 + 128],
                                                                    in_=pvs[1][:, j * 128:(j + 1) * 128], func=AF.Identity,
                                                                    scale=Sx[:, kk, v:v + 1],
                                                                    bias=MODP[:, SHoff + kk, v:v + 1]),
                             R=[pbs[1], Sx, MODP], W=[HTH[id(ht)][1]])

                lagq = []

                def lag_flush():
                    while lagq:
                        lagq.pop(0)()

                def lag_push(fn):
                    lag_flush()
                    lagq.append(fn)

                def proj_fm(ht, ntok, col, evac):
                    pb = bank("proj")
                    k.group([lambda kk=kk: pe.matmul(pb[:, 0:ntok], lhsT=WIN[:, kk, col:col + 128], rhs=ht[:, kk, 0:ntok],
                                                     start=(kk == 0), stop=(kk == 7)) for kk in range(8)],
                            R=[WING[col]] + HTH[id(ht)], W=[pb])
                    lag_push(lambda: evac(pb))

                def proj_tm(ht, tcol, col, ncol, evac, kind="proj"):
                    pb = bank(kind)
                    k.group([lambda kk=kk: pe.matmul(pb[:, 0:ncol], lhsT=ht[:, kk, tcol:tcol + 128],
                                                     rhs=WIN[:, kk, col:col + ncol],
                                                     start=(kk == 0), stop=(kk == 7)) for kk in range(8)],
                            R=[WING[col]] + HTH[id(ht)], W=[pb])
                    lag_push(lambda: evac(pb))

                pre_rot = [0]

                def conv_silu(pb, ch, ntok, seqlen, out_ap, outbuf):
                    pr = PRE[pre_rot[0] % 3]
                    pre_rot[0] += 1
                    k.op(ACT, lambda: act.activation(out=pr[:, 0:ntok], in_=pb[:, 0:ntok], func=AF.Identity,
                                                     scale=CW[:, ch, 1:2], bias=CBI[:, ch:ch + 1]),
                         R=[pb, CW, CBI], W=[pr])
                    pv = pb[:, 0:ntok].rearrange("p (s t) -> p s t", t=seqlen)
                    rv = pr[:, 0:ntok].rearrange("p (s t) -> p s t", t=seqlen)

                    def stage_b():
                        k.op(DVE, lambda: dve.scalar_tensor_tensor(out=rv[:, :, 1:seqlen], in0=pv[:, :, 0:seqlen - 1],
                                                                   scalar=CW[:, ch, 0:1], in1=rv[:, :, 1:seqlen],
                                                                   op0=ALU.mult, op1=ALU.add), R=[pb, CW, pr], W=[pr])
                        k.op(DVE, lambda: dve.scalar_tensor_tensor(out=rv[:, :, 0:seqlen - 1], in0=pv[:, :, 1:seqlen],
                                                                   scalar=CW[:, ch, 2:3], in1=rv[:, :, 0:seqlen - 1],
                                                                   op0=ALU.mult, op1=ALU.add), R=[pb, CW, pr], W=[pr])

                    def stage_c():
                        k.op(ACT, lambda: act.activation(out=out_ap, in_=pr[:, 0:ntok], func=AF.Silu), R=[pr], W=[outbuf])

                    if conv_c:
                        conv_c.pop(0)()
                    if conv_b:
                        fb, fc = conv_b.pop(0)
                        fb()
                        conv_c.append(fc)
                    conv_b.append((stage_b, stage_c))

                conv_b = []
                conv_c = []

                def conv_flush():
                    while conv_b:
                        fb, fc = conv_b.pop(0)
                        fb()
                        conv_c.append(fc)
                    while conv_c:
                        conv_c.pop(0)()

                ktok_rot = [0]

                def k_to_tok_tile(ut):
                    lag_flush()
                    conv_flush()
                    pb = bank("small")
                    pbv = pb[:].bitcast(BF16)
                    k.group([lambda h=h: pe.transpose(pbv[:, h * 128:(h + 1) * 128], KT[:, h, ut * 128:(ut + 1) * 128], IDB)
                             for h in range(4)], R=[KTv[ut // 4], CB16], W=[pb])
                    ktok_rot[0] += 1
                    if ktok_rot[0] % 2 == 0:
                        k.op(DVE, lambda: dve.tensor_copy(out=KTOK[:, ut, :], in_=pbv[:, 0:512]), R=[pb], W=[KTOKv[ut // 4]])
                    else:
                        k.op(ACT, lambda: act.activation(out=KTOK[:, ut, :], in_=pbv[:, 0:512], func=AF.Copy), R=[pb], W=[KTOKv[ut // 4]])

                def gates_evac(pb, ut):
                    k.op(DVE, lambda: dve.tensor_tensor(out=GT[:, ut, :], in0=pb[:, 0:16], in1=BGB[:], op=ALU.add),
                         R=[pb, BGB], W=[GTv[ut // 4]])

                pa_state = {"xc": 0, "n": 0}

                def mk_blk(tiles, v, seqlen, own, u0, after=None):
                    blk = dict(tiles=tiles, v=v, seqlen=seqlen, own=own, u0=u0, after=after, idx=pa_state["n"])
                    blk["ht"] = HT[pa_state["n"] % 2]
                    pa_state["n"] += 1
                    return blk

                def pa_pre(blk, i):
                    t = blk["tiles"][i]
                    xb = XS[pa_state["xc"] % 2]
                    pa_state["xc"] += 1
                    if blk["own"]:
                        src = xp[t * 128:(t + 1) * 128, :] if t < NT_P else xs[(t - NT_P) * 128:(t - NT_P + 1) * 128, :]
                    else:
                        src = xf[t * 128:(t + 1) * 128, :]
                    k.dma(SP, xb[:], src, W=[xb])
                    blk.setdefault("xn", {})[i] = norm_pre(xb, xb[:], (blk["idx"] % 2) * 4 + i)

                def pa_post(blk, i):
                    norm_post(blk["xn"][i], S1, 0, blk["v"], blk["ht"], i * 128)

                def pa_items(blk):
                    tiles, v, seqlen, own, u0, ht = blk["tiles"], blk["v"], blk["seqlen"], blk["own"], blk["u0"], blk["ht"]
                    nt = len(tiles)
                    ntok = nt * 128
                    items = []
                    if own:
                        for c in range(4):
                            items.append(lambda c=c: proj_fm(ht, ntok, OFF_U + c * 128,
                                         lambda pb: k.op(ACT, lambda: act.activation(out=GUT[:, c, 0:ntok], in_=pb[:, 0:ntok],
                                                                                     func=AF.Gelu_apprx_tanh), R=[pb], W=[GUT])))
                        for c in range(4):
                            items.append(lambda c=c: proj_fm(ht, ntok, OFF_Q + c * 128,
                                         lambda pb: conv_silu(pb, c, ntok, seqlen, QT[:, c, u0 * 128:u0 * 128 + ntok], QTv[u0 // 4])))
                    for c in range(4):
                        items.append(lambda c=c: proj_fm(ht, ntok, OFF_K + c * 128,
                                     lambda pb: conv_silu(pb, 4 + c, ntok, seqlen, KT[:, c, u0 * 128:u0 * 128 + ntok], KTv[u0 // 4])))

                    def fm_tail_flush():
                        lag_flush()
                        conv_flush()
                    items.append(fm_tail_flush)

                    def tile_item(i, t):
                        ut = u0 + i
                        if i % 2 == 0:
                            proj_tm(ht, i * 128, OFF_VV, 512,
                                    lambda pb: k.op(ACT, lambda: act.activation(
                                        out=VV[:, ut, :, 0:128], in_=pb[:].rearrange("p (h d) -> p h d", d=128), func=AF.Copy),
                                        R=[pb], W=[VVv[ut // 4]]))
                        else:
                            proj_tm(ht, i * 128, OFF_VV, 512,
                                    lambda pb: k.op(DVE, lambda: dve.tensor_copy(
                                        out=VV[:, ut, :, 0:128], in_=pb[:].rearrange("p (h d) -> p h d", d=128)),
                                        R=[pb], W=[VVv[ut // 4]]))
                        proj_tm(ht, i * 128, OFF_G, 16, lambda pb: gates_evac(pb, ut), kind="small")

                    def tile_item_own(i, t):
                        ut = u0 + i
                        proj_tm(ht, i * 128, OFF_O, 512,
                                lambda pb: k.op(ACT, lambda: act.activation(out=TH[:, ut, :], in_=pb[:], func=AF.Tanh,
                                                                            scale=0.5), R=[pb], W=[TH]))
                        proj_tm(ht, i * 128, OFF_V, 512,
                                lambda pb: k.op(ACT, lambda: act.activation(out=GV[:], in_=pb[:], func=AF.Gelu_apprx_tanh),
                                                R=[pb], W=[GV]))
                        lag_flush()
                        for g in range(4):
                            k.op(ACT, lambda g=g: act.activation(out=JUNK[:, 0:128], in_=GV[:, g * 128:(g + 1) * 128],
                                                                 func=AF.Square, accum_out=SS[:, 8 + g:9 + g]),
                                 R=[GV], W=[JUNK, SS])
                        rstd_of(SS[:, 8:12], RS[:, 8:12], 4, 1.0 / 128)
                        vg = VGS[i % 2]
                        for g in range(4):
                            k.op(DVE, lambda g=g: dve.scalar_tensor_tensor(
                                out=vg[:, g * 128:(g + 1) * 128], in0=GV[:, g * 128:(g + 1) * 128],
                                scalar=RS[:, 8 + g:9 + g], in1=GVB[:, g * 128:(g + 1) * 128],
                                op0=ALU.mult, op1=ALU.mult), R=[GV, RS, GVB], W=[vg])

                    def tile_item_own_b(i, t):
                        vg = VGS[i % 2]
                        pb = bank("proj")
                        fns = []
                        for g in range(4):
                            fns.append(lambda g=g: pe.matmul(pb[:, g * 128:(g + 1) * 128], lhsT=vg[:, g * 128:(g + 1) * 128],
                                                             rhs=WST[:, g, :], start=True, stop=False))
                            fns.append(lambda g=g: pe.matmul(pb[:, g * 128:(g + 1) * 128], lhsT=ONESB[0:1, :],
                                                             rhs=BSR[0:1, g * 128:(g + 1) * 128], start=False, stop=True))
                        k.group(fns, R=[vg, WST, CB16, BSR], W=[pb])
                        k.op(DVE, lambda: dve.tensor_tensor(
                            out=AOT[:, t, :, :], in0=pb[:].rearrange("p (g t) -> p g t", t=128),
                            in1=GUT[:, :, i * 128:(i + 1) * 128], op=ALU.mult), R=[pb, GUT], W=[AOT])

                    def full_flush():
                        lag_flush()
                        conv_flush()
                    if blk.get("pre_a") is not None:
                        items.insert(0, blk["pre_a"])
                    if blk.get("mid_a") is not None:
                        items.append(full_flush)
                        items.append(blk["mid_a"])
                    for i, t in enumerate(tiles):
                        items.append(lambda i=i, t=t: tile_item(i, t))
                        if own:
                            items.append(lambda i=i, t=t: tile_item_own(i, t))
                            if i > 0:
                                items.append(lambda i=i: tile_item_own_b(i - 1, tiles[i - 1]))
                    if own:
                        items.append(lambda: tile_item_own_b(nt - 1, tiles[nt - 1]))
                    if blk.get("mid_b") is not None:
                        items.append(full_flush)
                        items.append(blk["mid_b"])
                    for i, t in enumerate(tiles):
                        items.append(lambda i=i: k_to_tok_tile(u0 + i))
                    return items

                def run_blocks(blks):
                    for n_, b_ in enumerate(blks):
                        b_["ht"] = HT[n_ % 2]
                        b_["idx"] = n_
                    for i in range(4):
                        pa_pre(blks[0], i)
                    for i in range(4):
                        pa_post(blks[0], i)
                    for n, blk in enumerate(blks):
                        items = pa_items(blk)
                        nxt = blks[n + 1] if n + 1 < len(blks) else None
                        L = len(items)
                        marks = {max(0, (L * (q + 1)) // 8): q for q in range(4)}
                        if nxt is not None:
                            for i in range(4):
                                pa_pre(nxt, i)
                        mods = blk.get("mods")
                        extras = list(blk.get("extras") or [])
                        for idx, it in enumerate(items):
                            it()
                            if extras:
                                lag_flush()
                                extras.pop(0)()
                            if nxt is not None and idx in marks:
                                pa_post(nxt, marks[idx])
                            if mods is not None and idx == L // 4:
                                lag_flush()
                                mod_compute(mods[0])
                            if mods is not None and idx == (3 * L) // 4:
                                lag_flush()
                                mod_compute(mods[1])
                        lag_flush()
                        conv_flush()
                        while extras:
                            extras.pop(0)()
                        if blk["after"] is not None:
                            blk["after"]()

                def gates_unit(ntl, seqs, m0_aps, prefix_only=False):
                    n8 = ntl * 8
                    gv4 = GT[:, 0:ntl, :].rearrange("p t (d j h) -> p t d j h", d=2, j=2)
                    SPT3 = SPT[:, 0:n8].rearrange("p (t e) -> p t e", e=8)
                    for d in range(2):
                        k.op(ACT, lambda d=d: act.activation(out=SPT3[:, :, d * 4:(d + 1) * 4], in_=gv4[:, :, d, 1, :],
                                                             func=AF.Exp, scale=-1.0), R=GTv, W=[SPT])
                    k.op(ACT, lambda: act.activation(out=SPT[:, 0:n8], in_=SPT[:, 0:n8], func=AF.Ln, bias=1.0),
                         R=[SPT], W=[SPT])
                    pcb = PS[0]
                    fns = []
                    for t in range(ntl):
                        fns.append(lambda t=t: pe.matmul(pcb[:, t * 8:t * 8 + 4], lhsT=TRIF, rhs=SPT[:, t * 8:t * 8 + 4],
                                                         start=True, stop=True))
                        fns.append(lambda t=t: pe.matmul(pcb[:, t * 8 + 4:t * 8 + 8], lhsT=TRIB, rhs=SPT[:, t * 8 + 4:t * 8 + 8],
                                                         start=True, stop=True))
                    k.group(fns, R=[CF, SPT], W=[pcb])
                    k.op(DVE, lambda: dve.tensor_copy(out=CBT[:, 0:n8], in_=pcb[:, 0:n8]), R=[pcb], W=[CBT])
                    for d in range(2):
                        k.op(DVE, lambda d=d: dve.tensor_tensor(
                            out=UT[:, 0:n8].rearrange("p (t e) -> p t e", e=8)[:, :, d * 4:(d + 1) * 4],
                            in0=CBT[:, 0:n8].rearrange("p (t e) -> p t e", e=8)[:, :, d * 4:(d + 1) * 4],
                            in1=gv4[:, :, d, 0, :], op=ALU.add), R=[CBT] + GTv, W=[UT])
                    ptr = PS[1]
                    k.group([lambda: pe.transpose(ptr[0:n8, 0:128], UT[:, 0:n8], IDF())], R=[UT, CF], W=[ptr])
                    k.op(DVE, lambda: dve.tensor_reduce(out=UMX[0:n8, :], in_=ptr[0:n8, 0:128], axis=AX.X, op=ALU.max),
                         R=[ptr], W=[UMX])
                    k.op(DVE, lambda: dve.tensor_scalar(out=DG[0:n8, 0:n8], in0=CF[0:n8, 0:n8], scalar1=UMX[0:n8, 0:1],
                                                        scalar2=None, op0=ALU.mult), R=[CF, UMX], W=[DG])
                    pbb = PS[1]
                    k.group([lambda: pe.matmul(pbb[:, 128:128 + n8], lhsT=ONESF[0:n8, :], rhs=DG[0:n8, 0:n8],
                                               start=True, stop=True)], R=[CF, DG], W=[pbb])
                    k.op(DVE, lambda: dve.tensor_copy(out=UMB[:, 0:n8], in_=pbb[:, 128:128 + n8]), R=[pbb], W=[UMB])
                    k.group([lambda: pe.matmul(pbb[:, 256:256 + n8], lhsT=ONESF,
                                               rhs=SPT[:, 0:n8], start=True, stop=True)],
                            R=[CF, SPT], W=[pbb])
                    k.op(DVE, lambda: dve.tensor_copy(out=BLB[:, 0:n8], in_=pbb[:, 256:256 + n8]), R=[pbb], W=[BLB])
                    if prefix_only:
                        return
                    for si, seq in enumerate(seqs):
                        for d in range(2):
                            order = seq if d == 0 else seq[::-1]
                            m0 = m0_aps[si][d]
                            if m0 is None:
                                k.op(DVE, lambda d=d: dve.memset(MCUR[:, d * 4:(d + 1) * 4], 0.0), W=[MCUR])
                            else:
                                k.op(DVE, lambda d=d, m0=m0: dve.tensor_copy(out=MCUR[:, d * 4:(d + 1) * 4], in_=m0[0]),
                                     R=[m0[1]], W=[MCUR])
                            for t in order:
                                sl = slice(t * 8 + d * 4, t * 8 + d * 4 + 4)
                                k.op(DVE, lambda sl=sl, d=d: dve.tensor_copy(out=MPV[:, sl], in_=MCUR[:, d * 4:(d + 1) * 4]),
                                     R=[MCUR], W=[MPV])
                                k.op(DVE, lambda sl=sl, d=d: dve.tensor_tensor(out=RB[:, sl], in0=MCUR[:, d * 4:(d + 1) * 4],
                                                                               in1=UMB[:, sl], op=ALU.max),
                                     R=[MCUR, UMB], W=[RB])
                                k.op(DVE, lambda sl=sl, d=d: dve.tensor_tensor(out=MCUR[:, d * 4:(d + 1) * 4], in0=RB[:, sl],
                                                                               in1=BLB[:, sl], op=ALU.subtract),
                                     R=[RB, BLB], W=[MCUR])
                            seq_end(si, d)
                    k.op(DVE, lambda: dve.tensor_tensor(out=EE[:, 0:n8], in0=UT[:, 0:n8], in1=RB[:, 0:n8], op=ALU.subtract),
                         R=[UT, RB], W=[EE])
                    k.op(ACT, lambda: act.activation(out=EE[:, 0:n8], in_=EE[:, 0:n8], func=AF.Exp, bias=MISC[:, 0:1]),
                         R=[EE, MISC], W=[EE])
                    k.op(DVE, lambda: dve.tensor_tensor(out=THR[:, 0:n8], in0=CBT[:, 0:n8], in1=RB[:, 0:n8], op=ALU.subtract),
                         R=[CBT, RB], W=[THR])
                    k.op(ACT, lambda: act.activation(out=THR[:, 0:n8], in_=THR[:, 0:n8], func=AF.Exp), R=[THR], W=[THR])
                    k.op(DVE, lambda: dve.tensor_tensor(out=DCY[:, 0:n8], in0=MPV[:, 0:n8], in1=RB[:, 0:n8], op=ALU.subtract),
                         R=[MPV, RB], W=[DCY])
                    k.op(ACT, lambda: act.activation(out=DCY[:, 0:n8], in_=DCY[:, 0:n8], func=AF.Exp), R=[DCY], W=[DCY])

                seq_end_cb = [None]

                def seq_end(si, d):
                    if seq_end_cb[0] is not None:
                        seq_end_cb[0](si, d)

                k.op(DVE, lambda: dve.memset(MISC[:, 0:1], LNS), W=[MISC])

                NCH = 8
                HSV = [k.view(f"HSV{t}", HS.t[:, t, :]) for t in range(NT_S)]
                DDV = [k.view(f"DDV{i}", DD.t[:, i * 4:(i + 1) * 4]) for i in range(NCH)]
                hs_written = {}

                def mlstm_wave(chains, banks=None, slot0=0):
                    n = len(chains)
                    info = []
                    for i, (ut, d, h) in enumerate(chains):
                        j = slot0 + i
                        info.append(dict(ut=ut, d=d, h=h, col=ut * 8 + d * 4 + h, pb=(banks[i] if banks else PS[i]), vp=VP[j],
                                         cb=CBF[j], st=ST[j], dd=DDV[j], cst=CST[d][h], qs=slice(ut * 128, (ut + 1) * 128)))
                    for i, c in enumerate(info):
                        k.op(ACT, lambda c=c: act.activation(out=c["vp"][:, 0:130], in_=VV[:, c["ut"], c["h"], 0:130],
                                                             func=AF.Identity, scale=EE[:, c["col"]:c["col"] + 1]),
                             R=[VVv[c["ut"] // 4], EE], W=[c["vp"]])
                        if False:
                            pass
                        else:
                            k.op(ACT, lambda c=c: act.activation(out=c["cb"][:, 0:130], in_=c["cst"][:, 0:130],
                                                                 func=AF.Identity, scale=DCY[:, c["col"]:c["col"] + 1]),
                                 R=[c["cst"], DCY], W=[c["cb"]])
                    for c in info:
                        k.group([lambda c=c: pe.matmul(c["pb"][:, 0:128], lhsT=KT[:, c["h"], c["qs"]], rhs=QT[:, c["h"], c["qs"]],
                                                       start=True, stop=True)], R=[KTv[c["ut"] // 4], QTv[c["ut"] // 4]], W=[c["pb"]])
                    for c in info:
                        k.op(DVE, lambda c=c: dve.tensor_tensor(out=c["st"][:], in0=c["pb"][:, 0:128], in1=MASK[c["d"]],
                                                                op=ALU.mult), R=[c["pb"], CB16], W=[c["st"]])
                    for c in info:
                        k.group([lambda c=c: pe.matmul(c["pb"][:, 128:258], lhsT=c["st"][:], rhs=c["vp"][:, 0:130],
                                                       start=True, stop=False),
                                 lambda c=c: pe.matmul(c["pb"][:, 128:258], lhsT=QT[:, c["h"], c["qs"]], rhs=c["cb"][:, 0:130],
                                                       start=False, stop=True),
                                 lambda c=c: pe.matmul(c["pb"][:, 260:390], lhsT=KTOK[:, c["ut"], c["h"] * 128:(c["h"] + 1) * 128],
                                                       rhs=c["vp"][:, 0:130], start=True, stop=True)],
                                R=[c["st"], c["vp"], QTv[c["ut"] // 4], c["cb"], KTOKv[c["ut"] // 4]], W=[c["pb"]])
                    for c in info:
                        dd = c["dd"]
                        k.op(DVE, lambda c=c, dd=dd: dve.tensor_scalar(out=dd[:, 2:3], in0=c["pb"][:, 256:257], scalar1=-1.0,
                                                                       scalar2=None, op0=ALU.mult), R=[c["pb"]], W=[dd])
                        k.op(DVE, lambda c=c, dd=dd: dve.scalar_tensor_tensor(out=dd[:, 0:1], in0=c["pb"][:, 256:257],
                                                                              scalar=THR[:, c["col"]:c["col"] + 1],
                                                                              in1=dd[:, 2:3], op0=ALU.max, op1=ALU.max),
                             R=[c["pb"], THR, dd], W=[dd])
                        k.op(DVE, lambda dd=dd: dve.reciprocal(out=dd[:, 1:2], in_=dd[:, 0:1]), R=[dd], W=[dd])
                    for c in info:
                        dd = c["dd"]
                        hsb = HSV[c["ut"]]
                        hsl = slice(c["h"] * 128, (c["h"] + 1) * 128)
                        key = (c["ut"], c["h"])
                        if key not in hs_written:
                            hs_written[key] = True
                            k.op(ACT, lambda c=c, dd=dd, hsb=hsb, hsl=hsl: act.activation(out=hsb[:, hsl], in_=c["pb"][:, 128:256],
                                                                                         func=AF.Identity, scale=dd[:, 1:2]),
                                 R=[c["pb"], dd], W=[hsb])
                        else:
                            k.op(DVE, lambda c=c, dd=dd, hsb=hsb, hsl=hsl: dve.scalar_tensor_tensor(
                                out=hsb[:, hsl], in0=c["pb"][:, 128:256], scalar=dd[:, 1:2], in1=hsb[:, hsl],
                                op0=ALU.mult, op1=ALU.add), R=[c["pb"], dd, hsb], W=[hsb])
                    for c in info:
                        k.op(DVE, lambda c=c: dve.scalar_tensor_tensor(out=c["cst"][:, 0:130], in0=c["cst"][:, 0:130],
                                                                       scalar=DCY[:, c["col"]:c["col"] + 1],
                                                                       in1=c["pb"][:, 260:390], op0=ALU.mult, op1=ALU.add),
                             R=[c["cst"], DCY, c["pb"]], W=[c["cst"]])

                def mlstm_post_unit(uts, gts):
                    ntl = len(uts)
                    for ut in uts:
                        for h in range(4):
                            k.op(ACT, lambda h=h, ut=ut: act.activation(out=JUNK[:, 0:128], in_=HSV[ut][:, h * 128:(h + 1) * 128],
                                                                        func=AF.Square,
                                                                        accum_out=SS[:, 16 + ut * 4 + h:17 + ut * 4 + h]),
                                 R=[HSV[ut]], W=[JUNK, SS])
                        rstd_of(SS[:, 16 + 4 * ut:20 + 4 * ut], RS[:, 16 + 4 * ut:20 + 4 * ut], 4, 1.0 / 128)
                    for ut, gt in zip(uts, gts):
                        for h in range(4):
                            k.op(DVE, lambda h=h, ut=ut: dve.scalar_tensor_tensor(
                                out=HSV[ut][:, h * 128:(h + 1) * 128], in0=HSV[ut][:, h * 128:(h + 1) * 128],
                                scalar=RS[:, 16 + ut * 4 + h:17 + ut * 4 + h], in1=GHB[:, h * 128:(h + 1) * 128],
                                op0=ALU.mult, op1=ALU.mult), R=[HSV[ut], RS, GHB], W=[HSV[ut]])
                        xn = XN[xn_rot[0] % 2]
                        xn_rot[0] += 1
                        k.op(POOL, lambda ut=ut, xn=xn: pool.scalar_tensor_tensor(out=xn[:, 0:512], in0=TH[:, ut, :], scalar=1.0,
                                                                                   in1=HSV[ut][:], op0=ALU.add, op1=ALU.mult),
                             R=[TH, HSV[ut]], W=[xn]) if False else k.op(
                            DVE, lambda ut=ut, xn=xn: dve.scalar_tensor_tensor(out=xn[:, 0:512], in0=TH[:, ut, :], scalar=1.0,
                                                                               in1=HSV[ut][:], op0=ALU.add, op1=ALU.mult),
                            R=[TH, HSV[ut]], W=[xn])
                        pb = bank("tr")
                        pbv = pb[:].bitcast(BF16)
                        k.group([lambda c=c, xn=xn, pbv=pbv: pe.transpose(pbv[:, c * 128:(c + 1) * 128],
                                                                          xn[:, c * 128:(c + 1) * 128], IDB)
                                 for c in range(4)], R=[xn, CB16], W=[pb])
                        k.op(ACT, lambda gt=gt, pbv=pbv: act.activation(out=BOT[:, gt, :, :],
                                                                        in_=pbv[:, 0:512].rearrange("p (c t) -> p c t", t=128),
                                                                        func=AF.Copy), R=[pb], W=[BOT])

                prompt_blk = mk_blk([0, 1, 2, 3], 0, 256, True, 0)

                def prompt_seq_end(si, d):
                    k.dma(SP, nm[si:si + 1, d * 4:(d + 1) * 4], MCUR[0:1, d * 4:(d + 1) * 4], R=[MCUR], final=True)

                def prompt_gates():
                    seq_end_cb[0] = prompt_seq_end
                    gates_unit(4, [[0, 1], [2, 3]], [[None, None], [None, None]])
                    seq_end_cb[0] = None

                HW_BANKS = [PS[0], PS[1], PS[6], PS[7]]
                prompt_extras = [prompt_gates]
                for si, seq in enumerate([[0, 1], [2, 3]]):
                    def zero_states():
                        for d in range(2):
                            for h in range(4):
                                k.op(POOL, lambda d=d, h=h: pool.memset(CST[d][h][:], 0.0), W=[CST[d][h]])
                    prompt_extras.append(zero_states)
                    for w in range(2):
                        prompt_extras.append(lambda seq=seq, w=w: mlstm_wave([(seq[w], 0, h) for h in range(4)],
                                                                            banks=HW_BANKS, slot0=0))
                        prompt_extras.append(lambda seq=seq, w=w: mlstm_wave([(seq[1 - w], 1, h) for h in range(4)],
                                                                            banks=HW_BANKS, slot0=4))

                    def store_states(si=si):
                        with nc.allow_non_contiguous_dma(reason="state column"):
                            for d in range(2):
                                for h in range(4):
                                    k.dma(SP, nC[si, d, h, :, :], CST[d][h][:, 0:128], R=[CST[d][h]], final=True)
                                    k.dma(SP, nn[si, d, h, :].rearrange("(p o) -> p o", o=1), CST[d][h][:, 128:129],
                                          R=[CST[d][h]], final=True)
                    prompt_extras.append(store_states)
                prompt_extras.append(lambda: mlstm_post_unit([0, 1, 2, 3], [0, 1, 2, 3]))

                mod_load(4)

                stage(5)
                k.op(DVE, lambda: dve.memset(NEGT[:], NEG), W=[NEGT])
                k.op(DVE, lambda: dve.tensor_scalar(out=NEGF[:], in0=FLB[:], scalar1=-1.0, scalar2=-NEG,
                                                    op0=ALU.add, op1=ALU.mult), R=[FLB], W=[NEGF])
                def summary_pre(p):
                    gates_unit(8, None, None, prefix_only=True)
                    summary_a0(p)

                def summary_a0(p):
                    B3 = BLB[:, 0:64].rearrange("p (t e) -> p t e", e=8)
                    O3 = OFS[:, 0:64].rearrange("p (t e) -> p t e", e=8)
                    k.op(DVE, lambda: dve.memset(OFS[:], 0.0), W=[OFS])
                    for t in range(1, 8):
                        k.op(DVE, lambda t=t: dve.tensor_tensor(out=O3[:, t, 0:4], in0=O3[:, t - 1, 0:4], in1=B3[:, t - 1, 0:4],
                                                                op=ALU.add), R=[OFS, BLB], W=[OFS])
                    for t in range(6, -1, -1):
                        k.op(DVE, lambda t=t: dve.tensor_tensor(out=O3[:, t, 4:8], in0=O3[:, t + 1, 4:8], in1=B3[:, t + 1, 4:8],
                                                                op=ALU.add), R=[OFS, BLB], W=[OFS])
                    k.op(DVE, lambda p=p: dve.tensor_reduce(out=BSEG[:, p - 1, :],
                                                            in_=BLB[:, 0:64].rearrange("p (t e) -> p e t", e=8),
                                                            axis=AX.X, op=ALU.add), R=[BLB], W=[BSEG])
                    k.op(DVE, lambda: dve.tensor_tensor(out=RB[:, 0:64], in0=UMB[:, 0:64], in1=OFS[:, 0:64], op=ALU.add),
                         R=[UMB, OFS], W=[RB])
                    k.op(DVE, lambda: dve.tensor_reduce(out=RSEG[:], in_=RB[:, 0:64].rearrange("p (t e) -> p e t", e=8),
                                                        axis=AX.X, op=ALU.max), R=[RB], W=[RSEG])
                    k.op(DVE, lambda p=p: dve.tensor_tensor(out=MLOC[:, p - 1, :], in0=RSEG[:], in1=BSEG[:, p - 1, :],
                                                            op=ALU.subtract), R=[RSEG, BSEG], W=[MLOC])
                    k.op(DVE, lambda: dve.tensor_tensor(out=EE[:, 0:64], in0=UT[:, 0:64], in1=OFS[:, 0:64], op=ALU.add),
                         R=[UT, OFS], W=[EE])
                    for t in range(8):
                        k.op(DVE, lambda t=t: dve.tensor_tensor(out=EE[:, t * 8:t * 8 + 8], in0=EE[:, t * 8:t * 8 + 8],
                                                                in1=RSEG[:], op=ALU.subtract), R=[EE, RSEG], W=[EE])
                    k.op(ACT, lambda: act.activation(out=EE[:, 0:64], in_=EE[:, 0:64], func=AF.Exp, bias=MISC[:, 0:1]),
                         R=[EE, MISC], W=[EE])
                    summary_vpb(0)

                def summary_a(p):
                    summary_mm(p, 0)
                    summary_vpb(1)

                VPB = HS.t[:].rearrange("p t c -> p (t c)").bitcast(BF16)[:, 0:8 * 4 * 132].rearrange(
                    "p (t h c) -> p t h c", t=8, h=4)

                def summary_vpb(d):
                    for t in range(8):
                        k.op(DVE, lambda t=t, d=d: dve.tensor_tensor(
                            out=VPB[:, t, :, 0:130], in0=VV[:, t, :, 0:130],
                            in1=EE[:, t * 8 + d * 4:t * 8 + d * 4 + 4].unsqueeze(2).to_broadcast([128, 4, 130]),
                            op=ALU.mult), R=VVv + [EE], W=HSV)

                def summary_mm(p, d):
                    pbs = [PS[2 + d * 2], PS[3 + d * 2]]
                    for h in range(4):
                        pb = pbs[h // 2]
                        c0 = (h % 2) * 132
                        k.group([lambda t=t, h=h, pb=pb, c0=c0: pe.matmul(pb[:, c0:c0 + 130],
                                                                         lhsT=KTOK[:, t, h * 128:(h + 1) * 128],
                                                                         rhs=VPB[:, t, h, 0:130],
                                                                         start=(t == 0), stop=(t == 7)) for t in range(8)],
                                R=KTOKv + HSV, W=[pb])
                        scb = SC[p - 1][d][h]
                        k.op(ACT, lambda scb=scb, pb=pb, c0=c0: act.activation(out=scb[:], in_=pb[:, c0:c0 + 130],
                                                                               func=AF.Copy),
                             R=[pb], W=[scb] + (SCX[id(scb)] or HSV))

                def summary_b(p):
                    summary_mm(p, 1)

                blks = []
                for p in (1, 2, 3):
                    base = (p - 1) * 8
                    ba = mk_blk([base + i for i in range(4)], 1, 64, False, 0)
                    bb = mk_blk([base + 4 + i for i in range(4)], 1, 64, False, 4)
                    blks += [bb, ba] if p == 1 else [ba, bb]
                blks.append(mk_blk([4, 5, 6, 7], 1, 64, True, 0))
                blks.append(mk_blk([8, 9, 10, 11], 1, 64, True, 4))
                for p in (1, 2, 3):
                    blks[2 * p]["pre_a"] = (lambda p=p: summary_pre(p))
                    blks[2 * p]["mid_a"] = (lambda p=p: summary_a(p))
                    blks[2 * p]["mid_b"] = (lambda p=p: summary_b(p))
                for j in range(4):
                    blks[j]["mods"] = [4 + 2 * j, 5 + 2 * j]
                def fold_states():
                    for d in range(2):
                        for h in range(4):
                            k.dma(SP, CST[d][h][:, 0:128], stC[d, h, :, :], W=[CST[d][h]])
                            with nc.allow_non_contiguous_dma(reason="state column"):
                                k.dma(SP, CST[d][h][:, 128:129], stn[d, h, :].rearrange("(p o) -> p o", o=1), W=[CST[d][h]])
                    stage(5.6)
                    k.op(DVE, lambda: dve.tensor_copy(out=MFO[:], in_=M0B[:]), R=[M0B], W=[MFO])
                    for d in range(2):
                        for p in ((1, 2, 3) if d == 0 else (3, 2, 1)):
                            fi = d * 3 + p - 1
                            dsl = slice(d * 4, (d + 1) * 4)
                            k.op(DVE, lambda: dve.scalar_tensor_tensor(out=FT[:, 0:4], in0=BSEG[:, p - 1, dsl],
                                                                       scalar=FLB[:, fi:fi + 1], in1=MFO[:, dsl],
                                                                       op0=ALU.mult, op1=ALU.subtract), R=[BSEG, FLB, MFO], W=[FT])
                            k.op(DVE, lambda: dve.tensor_scalar(out=FT[:, 0:4], in0=FT[:, 0:4], scalar1=-1.0, scalar2=None,
                                                                op0=ALU.mult), R=[FT], W=[FT])
                            k.op(DVE, lambda: dve.tensor_scalar(out=FT[:, 4:8], in0=MLOC[:, p - 1, dsl], scalar1=FLB[:, fi:fi + 1],
                                                                scalar2=NEGF[:, fi:fi + 1], op0=ALU.mult, op1=ALU.add),
                                 R=[MLOC, FLB, NEGF], W=[FT])
                            k.op(DVE, lambda: dve.tensor_tensor(out=MFO[:, dsl], in0=FT[:, 0:4], in1=FT[:, 4:8], op=ALU.max),
                                 R=[FT], W=[MFO])
                            for q in range(2):
                                k.op(DVE, lambda q=q: dve.tensor_tensor(out=FT[:, 8 + q * 4:12 + q * 4], in0=FT[:, q * 4:q * 4 + 4],
                                                                        in1=MFO[:, dsl], op=ALU.subtract), R=[FT, MFO], W=[FT])
                            k.op(ACT, lambda: act.activation(out=FT[:, 8:16], in_=FT[:, 8:16], func=AF.Exp), R=[FT], W=[FT])
                            for h in range(4):
                                k.op(DVE, lambda h=h: dve.tensor_scalar(out=CST[d][h][:, 0:130], in0=CST[d][h][:, 0:130],
                                                                        scalar1=FT[:, 8 + h:9 + h], scalar2=None, op0=ALU.mult),
                                     R=[CST[d][h], FT], W=[CST[d][h]])
                                k.op(DVE, lambda h=h: dve.scalar_tensor_tensor(out=CST[d][h][:, 0:130], in0=SC[p - 1][d][h][:],
                                                                               scalar=FT[:, 12 + h:13 + h], in1=CST[d][h][:, 0:130],
                                                                               op0=ALU.mult, op1=ALU.add),
                                     R=[SC[p - 1][d][h], FT, CST[d][h]] + (SCX[id(SC[p - 1][d][h])] or HSV), W=[CST[d][h]])

                blks[7]['mid_b'] = fold_states
                blks[0]["extras"] = prompt_extras
                run_blocks([prompt_blk] + blks)
                stage(5.5)
                stage(5.7)
                gates_unit(8, [list(range(8))], [[(MFO[:, 0:4], MFO), (MFO[:, 4:8], MFO)]])
                hs_written.clear()
                for w in range(8):
                    mlstm_wave([(w, 0, h) for h in range(4)] + [(7 - w, 1, h) for h in range(4)])
                mlstm_post_unit(list(range(8)), [4 + u for u in range(8)])

                stage(6)
                era1_tokens = k.all_tokens(k.bufs)

            GAB.w.update(era1_tokens)
            GAB.r.update(era1_tokens)

            def gate_bcast(off):
                for v in range(2):
                    pt = PS[0]
                    k.group([lambda: pe.transpose(pt[0:8, 0:128], MODP[:, off:off + 8, v], IDF())], R=[MODP, CF], W=[pt])
                    k.op(DVE, lambda: dve.tensor_copy(out=UTT[0:8, :], in_=pt[0:8, 0:128]), R=[pt], W=[UTT])
                    for hf in range(2):
                        pbk = PS[1 + hf]
                        fns = []
                        for c in range(4):
                            kk = hf * 4 + c
                            fns.append(lambda kk=kk, c=c: pe.matmul(pbk[:, c * 128:(c + 1) * 128],
                                                                    lhsT=OH[0:8, kk, :],
                                                                    rhs=UTT[0:8, :], start=True, stop=True))
                        k.group(fns, R=[OH, UTT], W=[pbk])
                        k.op(DVE, lambda hf=hf, v=v, pbk=pbk: dve.tensor_copy(out=GAB[:, v, hf * 512:(hf + 1) * 512],
                                                                             in_=pbk[:]), R=[pbk], W=[GAB])

            esx = ExitStack()
            with esx:
                def nbx(es, name, shape, dt, toks):
                    b = k.sb(es, name, shape, dt)
                    b.w.update(toks)
                    b.r.update(toks)
                    return b
                X = [nbx(esx, f"X{t}", [128, D], F32, era1_tokens) for t in range(NT)]
                esc = ExitStack()
                with esc:
                    WOUT = nbx(esc, "WOUT", [128, 8, D], BF16, era1_tokens)
                    TMPCS = [nbx(esc, f"TMPC{i}", [128, 512], F32, era1_tokens) for i in range(2)]
                    wout_view = w_out.rearrange("(k p) n -> p k n", p=128)
                    for kk in range(8):
                        k.dma(POOL, WOUT[:, kk, :], wout_view[:, kk, :], W=[WOUT])
                    for t in range(NT):
                        src = xp[t * 128:(t + 1) * 128, :] if t < NT_P else xs[(t - NT_P) * 128:(t - NT_P + 1) * 128, :]
                        k.dma(POOL, X[t][:], src, W=[X[t]])
                    gate_bcast(16)
                    crot = [0]
                    for t in range(NT):
                        v = 0 if t < NT_P else 1
                        for hf in range(2):
                            pb = PS[crot[0] % 8]
                            crot[0] += 1
                            fns = []
                            for kc in range(8):
                                src = AOT if kc < 4 else BOT
                                fns.append(lambda kc=kc, src=src: pe.matmul(pb[:], lhsT=src[:, t, kc % 4, :],
                                                                            rhs=WOUT[:, kc, hf * 512:(hf + 1) * 512],
                                                                            start=(kc == 0), stop=(kc == 7)))
                            k.group(fns, R=[AOT, BOT, WOUT], W=[pb])
                            TMPC = TMPCS[crot[0] % 2]
                            k.op(DVE, lambda: dve.tensor_tensor(out=TMPC[:], in0=pb[:], in1=GAB[:, v, hf * 512:(hf + 1) * 512],
                                                                op=ALU.mult), R=[pb, GAB], W=[TMPC])
                            eng, hh = (DVE, dve) if hf == 0 else (POOL, pool)
                            k.op(eng, lambda hh=hh: hh.tensor_tensor(out=X[t][:, hf * 512:(hf + 1) * 512],
                                                                     in0=X[t][:, hf * 512:(hf + 1) * 512], in1=TMPC[:],
                                                                     op=ALU.add),
                                 R=[X[t], TMPC], W=[X[t]])
                    gate_bcast(40)
                    era_tokens = k.all_tokens(k.bufs)

                stage(7)
                es2 = ExitStack()
                es2.__enter__()

                def nb(name, shape, dt):
                    return nbx(es2, name, shape, dt, era_tokens)
                MB = 6 * 128
                W2 = nb("W2", [128, NFF, D], BF16)
                GTF = nb("GTF", [128, NFF, MB], BF16)
                GFB = nb("GFB", [128, D], F32)
                k.dma(SP, GFB[:], g_final.partition_broadcast(128), W=[GFB])
                aflat = AOT.t[:].rearrange("p t c d -> p (t c d)")
                bflat = BOT.t[:].rearrange("p t c d -> p (t c d)")
                H2 = k.view("H2", aflat.rearrange("p (k n) -> p k n", n=MB), era_tokens)
                W13 = [k.view(f"W13_{i}", bflat[:, i * 2048:(i + 1) * 2048].rearrange("p (w k n) -> p w k n", w=2, k=8),
                              era_tokens) for i in range(3)]
                SA = [nb(f"SA{i}", [128, 384], BF16) for i in range(2)]
                TM2 = nb("TM2", [128, 512], F32)
                OUT = [nb(f"OUT{i}", [128, D], F32) for i in range(2)]
                w1v = w1.rearrange("(k p) n -> p k n", p=128)
                w3v = w3.rearrange("(k p) n -> p k n", p=128)
                w2v = w2.rearrange("(f p) n -> p f n", p=128)
                orot = [0]
                for mb in range(2):
                    tiles = list(range(mb * 6, mb * 6 + 6))
                    for i, t in enumerate(tiles):
                        v = 0 if t < NT_P else 1
                        ssl = 24 + i
                        k.op(ACT, lambda t=t, ssl=ssl: act.activation(out=JUNK[:], in_=X[t][:], func=AF.Square,
                                                                      accum_out=SS[:, ssl:ssl + 1]), R=[X[t]], W=[JUNK, SS])
                        rstd_of(SS[:, ssl:ssl + 1], RS[:, ssl:ssl + 1], 1, 1.0 / D)
                        xn = XN[i % 4]
                        k.op(DVE, lambda t=t, ssl=ssl, xn=xn: dve.tensor_scalar(out=xn[:], in0=X[t][:], scalar1=RS[:, ssl:ssl + 1],
                                                                               scalar2=None, op0=ALU.mult),
                             R=[X[t], RS], W=[xn])
                        pb = PS[i % 2]
                        pbv = pb[:].bitcast(BF16)
                        k.group([lambda kk=kk, xn=xn, pbv=pbv: pe.transpose(pbv[:, kk * 128:(kk + 1) * 128],
                                                                           xn[:, kk * 128:(kk + 1) * 128], IDB)
                                 for kk in range(8)], R=[xn, CB16], W=[pb])
                        for kk in range(8):
                            k.op(DVE, lambda kk=kk, i=i, v=v, pbv=pbv: dve.tensor_scalar(
                                out=H2[:, kk, i * 128:(i + 1) * 128], in0=pbv[:, kk * 128:(kk + 1) * 128],
                                scalar1=S2[:, kk, v:v + 1], scalar2=MODP[:, 24 + kk, v:v + 1], op0=ALU.mult, op1=ALU.add),
                                R=[pb, S2, MODP], W=[H2])
                    for f in range(NFF):
                        slot = W13[f % 3]
                        k.dma(POOL, slot[:, 0, :, :], w1v[:, :, f * 128:(f + 1) * 128], W=[slot])
                        k.dma(POOL, slot[:, 1, :, :], w3v[:, :, f * 128:(f + 1) * 128], W=[slot])
                        if mb == 0:
                            k.dma(POOL, W2[:, f, :], w2v[:, f, :], W=[W2])
                        for hf in range(2):
                            pa = PS[2 + (f * 4 + hf * 2) % 6]
                            pbb = PS[2 + (f * 4 + hf * 2 + 1) % 6]
                            for (pp, wi) in ((pa, 0), (pbb, 1)):
                                k.group([lambda kk=kk, pp=pp, wi=wi: pe.matmul(pp[:, 0:384], lhsT=slot[:, wi, kk, :],
                                                                               rhs=H2[:, kk, hf * 384:(hf + 1) * 384],
                                                                               start=(kk == 0), stop=(kk == 7))
                                         for kk in range(8)], R=[slot, H2], W=[pp])
                            sa = SA[hf]
                            k.op(ACT, lambda pa=pa, sa=sa: act.activation(out=sa[:], in_=pa[:, 0:384], func=AF.Silu),
                                 R=[pa], W=[sa])
                            k.op(DVE, lambda pbb=pbb, sa=sa, f=f, hf=hf: dve.tensor_tensor(
                                out=GTF[:, f, hf * 384:(hf + 1) * 384], in0=pbb[:, 0:384], in1=sa[:], op=ALU.mult),
                                R=[pbb, sa], W=[GTF])
                    for i, t in enumerate(tiles):
                        v = 0 if t < NT_P else 1
                        for hf in range(2):
                            pb = PS[(i * 2 + hf) % 8]
                            k.group([lambda f=f, pb=pb: pe.matmul(pb[:], lhsT=GTF[:, f, i * 128:(i + 1) * 128],
                                                                  rhs=W2[:, f, hf * 512:(hf + 1) * 512],
                                                                  start=(f == 0), stop=(f == NFF - 1)) for f in range(NFF)],
                                    R=[GTF, W2], W=[pb])
                            k.op(DVE, lambda pb=pb, v=v, hf=hf: dve.tensor_tensor(out=TM2[:], in0=pb[:],
                                                                                 in1=GAB[:, v, hf * 512:(hf + 1) * 512],
                                                                                 op=ALU.mult), R=[pb, GAB], W=[TM2])
                            k.op(POOL, lambda t=t, hf=hf: pool.tensor_tensor(out=X[t][:, hf * 512:(hf + 1) * 512],
                                                                             in0=X[t][:, hf * 512:(hf + 1) * 512],
                                                                             in1=TM2[:], op=ALU.add),
                                 R=[X[t], TM2], W=[X[t]])
                        ssl = 32 + i
                        k.op(ACT, lambda t=t, ssl=ssl: act.activation(out=JUNK[:], in_=X[t][:], func=AF.Square,
                                                                      accum_out=SS[:, ssl:ssl + 1]), R=[X[t]], W=[JUNK, SS])
                        rstd_of(SS[:, ssl:ssl + 1], RS[:, ssl:ssl + 1], 1, 1.0 / D)
                        ob = OUT[orot[0] % 2]
                        orot[0] += 1
                        k.op(DVE, lambda t=t, ssl=ssl, ob=ob: dve.scalar_tensor_tensor(
                            out=ob[:], in0=X[t][:], scalar=RS[:, ssl:ssl + 1], in1=GFB[:], op0=ALU.mult, op1=ALU.mult),
                            R=[X[t], RS, GFB], W=[ob])
                        dst = yp[t * 128:(t + 1) * 128, :] if t < NT_P else ys[(t - NT_P) * 128:(t - NT_P + 1) * 128, :]
                        k.dma(SP, dst, ob[:], R=[ob], final=True)

                es2.__exit__(None, None, None)
        except StageExit:
            pass
        k.finish()
    return nc


_NC_CACHE = {}


def make_consts():
    c = np.zeros((128, 512), np.float32)
    c[:, 0:128] = np.eye(128, dtype=np.float32)
    s = np.arange(128)[:, None]
    j = np.arange(128)[None, :]
    c[:, 128:256] = (s <= j).astype(np.float32)
    c[:, 256:384] = (s >= j).astype(np.float32)
    c[:, 384:512] = 1.0
    return c


def kernel(x_prompt, x_sample, state_C, state_n, state_m, c, c_ctx, w_ada, b_ada, g_norm1,
           w_in, b_gate, w_s, b_s, g_v, conv_w, conv_b, g_h, w_out, g_norm2, w1, w3, w2, g_final):
    f = lambda a: np.ascontiguousarray(np.asarray(a, dtype=np.float32))
    x_prompt, x_sample = f(x_prompt), f(x_sample)
    state_C, state_n, state_m, c, c_ctx = f(state_C), f(state_n), f(state_m), f(c), f(c_ctx)
    if "nc" not in _NC_CACHE:
        _NC_CACHE["nc"] = build_nc()
    nc = _NC_CACHE["nc"]
    consts = make_consts()
    shared = {
        "consts": consts, "w_ada": f(w_ada)[0], "b_ada": f(b_ada)[0], "g_norm1": f(g_norm1)[0], "w_in": f(w_in)[0],
        "b_gate": f(b_gate)[0].reshape(1, 16), "w_s": f(w_s)[0], "b_s": f(b_s)[0].reshape(1, 512),
        "g_v": f(g_v)[0].reshape(1, 512), "conv_w": f(conv_w)[0], "conv_b": f(conv_b)[0],
        "g_h": f(g_h)[0].reshape(1, 512), "w_out": f(w_out)[0], "g_norm2": f(g_norm2)[0], "w1": f(w1)[0],
        "w3": f(w3)[0], "w2": f(w2)[0], "g_final": f(g_final).reshape(1, D),
    }
    in_maps = []
    for core in range(8):
        b, j = core // 4, core % 4
        m = dict(shared)
        m["xp"] = x_prompt[2 * core:2 * core + 2].reshape(512, D)
        m["xs"] = x_sample[b, j * 1024:(j + 1) * 1024]
        m["xf"] = np.concatenate([x_sample[b, ((j + p) % 4) * 1024:((j + p) % 4 + 1) * 1024] for p in (1, 2, 3)], 0)
        m["cvec"] = np.stack([c_ctx, c[b]], 0)
        m["stC"] = state_C[b, 0]
        m["stn"] = state_n[b, 0]
        m["stm"] = state_m[b, 0].reshape(1, 8)
        fl = np.zeros((1, 6), np.float32)
        for p in (1, 2, 3):
            fl[0, p - 1] = 1.0 if p >= 4 - j else 0.0
            fl[0, 3 + p - 1] = 1.0 if p <= 3 - j else 0.0
        m["flg"] = fl
        in_maps.append({kk: np.ascontiguousarray(vv) for kk, vv in m.items()})
    res = run_bass_kernel_spmd(nc, in_maps, core_ids=list(range(8)))
    B = x_prompt.shape[0]
    y_prompt = np.zeros((B, 256, D), np.float32)
    y_sample = np.zeros((2, 4096, D), np.float32)
    new_C = np.zeros((B, 1, 2, 4, 128, 128), np.float32)
    new_n = np.zeros((B, 1, 2, 4, 128), np.float32)
    new_m = np.zeros((B, 1, 2, 4), np.float32)
    for core in range(8):
        r = res.results[core]
        b, j = core // 4, core % 4
        y_prompt[2 * core:2 * core + 2] = r["yp"].reshape(2, 256, D)
        y_sample[b, j * 1024:(j + 1) * 1024] = r["ys"]
        new_C[2 * core:2 * core + 2, 0] = r["nC"]
        new_n[2 * core:2 * core + 2, 0] = r["nn"]
        new_m[2 * core:2 * core + 2, 0] = r["nm"].reshape(2, 2, 4)
    return (y_prompt, y_sample, new_C, new_n, new_m)
```

```python
import numpy as np
from contextlib import ExitStack
import concourse.bass as bass
import concourse.mybir as mybir
from concourse.bass_utils import run_bass_kernel_spmd

F32 = mybir.dt.float32
BF16 = mybir.dt.bfloat16
AF = mybir.ActivationFunctionType
ALU = mybir.AluOpType
AX = mybir.AxisListType

D = 1024
DIN = 3088
DFF = 2816
NFF = DFF // 128
OFF_U, OFF_V, OFF_Q, OFF_K, OFF_VV, OFF_O, OFF_G = 0, 512, 1024, 1536, 2048, 2560, 3072
EPS = 1e-6
NEG = -1e30
LNS = float(-0.5 * np.log(128.0))
NT_P = 4
NT_S = 8
NT = NT_P + NT_S
NFOR = 24
import os
STAGE = float(os.environ.get("KSTAGE", "99"))


class StageExit(Exception):
    pass


DEAD = [False]
STRICT = False


def stage(n):
    if STAGE <= n:
        DEAD[0] = True


class Eng:
    def __init__(self, h, name):
        self.h = h
        self.name = name
        self.sem = None
        self.n = 0
        self.waited = {}


class Buf:
    def __init__(self, name, t):
        self.name = name
        self.t = t
        self.w = {}
        self.r = {}
        self.sem = None
        self.nd = 0
        self.psum = False

    def __getitem__(self, idx):
        return self.t[idx]


class K:
    def __init__(self, nc, es):
        self.nc = nc
        self.es = es
        self.PE = Eng(nc.tensor, "pe")
        self.ACT = Eng(nc.scalar, "act")
        self.DVE = Eng(nc.vector, "dve")
        self.POOL = Eng(nc.gpsimd, "pool")
        self.SP = Eng(nc.sync, "sp")
        self.engs = [self.PE, self.ACT, self.DVE, self.POOL, self.SP]
        for e in self.engs:
            e.sem = es.enter_context(nc.semaphore("s_" + e.name))
        self.nsem = 5
        self.final = []
        self.bufs = []

    def sb(self, es, name, shape, dt):
        b = Buf(name, es.enter_context(self.nc.sbuf_tensor(name, shape, dt)))
        self.bufs.append(b)
        return b

    def ps(self, es, name, shape, dt):
        b = Buf(name, es.enter_context(self.nc.psum_tensor(name, shape, dt)))
        b.psum = True
        self.bufs.append(b)
        return b

    def _need(self, e, toks):
        for key, (sem, val) in toks:
            if e.waited.get(id(sem), 0) >= val:
                continue
            e.h.wait_ge(sem, val)
            e.waited[id(sem)] = val

    def _deps(self, e, R, W):
        toks = []
        for b in R:
            for key, tok in b.w.items():
                if key == e.name and e is self.PE:
                    continue
                toks.append((key, tok))
            if b.psum:
                for key, tok in b.r.items():
                    if key != e.name:
                        toks.append((key, tok))
        for b in W:
            for key, tok in b.w.items():
                if key == e.name and (e is self.PE or not STRICT):
                    continue
                toks.append((key, tok))
            for key, tok in b.r.items():
                if key == e.name and (e is self.PE or not STRICT):
                    continue
                toks.append((key, tok))
        return toks

    def op(self, e, fn, R=(), W=()):
        if DEAD[0]:
            return None
        pend = []
        for key, (sem, val) in self._deps(e, R, W):
            if e.waited.get(id(sem), 0) >= val:
                continue
            e.waited[id(sem)] = val
            pend = [p for p in pend if p[0] is not sem] + [(sem, val)]
        for sem, val in pend[:-1]:
            e.h.wait_ge(sem, val)
        inst = fn()
        if pend:
            inst._wait_ge(pend[-1][0], pend[-1][1])
        inst.then_inc(e.sem, 1)
        e.n += 1
        tok = (e.sem, e.n)
        for b in R:
            b.r[e.name] = tok
        for b in W:
            b.w[e.name] = tok
        return tok

    def group(self, fns, R=(), W=()):
        e = self.PE
        if DEAD[0]:
            return None
        pend = []
        for key, (sem, val) in self._deps(e, R, W):
            if e.waited.get(id(sem), 0) >= val:
                continue
            e.waited[id(sem)] = val
            pend = [p for p in pend if p[0] is not sem] + [(sem, val)]
        for sem, val in pend[:-1]:
            e.h.wait_ge(sem, val)
        inst = None
        for n_, fn in enumerate(fns):
            inst = fn()
            if n_ == 0 and pend:
                inst._wait_ge(pend[-1][0], pend[-1][1])
        inst.then_inc(e.sem, 1)
        e.n += 1
        tok = (e.sem, e.n)
        for b in R:
            b.r[e.name] = tok
        for b in W:
            b.w[e.name] = tok
        return tok

    def dma(self, q, out, in_, R=(), W=(), final=False, **kw):
        owner = W[0] if len(W) else R[0]
        if DEAD[0]:
            return None
        if owner.sem is None:
            owner.sem = self.es.enter_context(self.nc.semaphore("d_" + owner.name))
            self.nsem += 1
        toks = []
        okey = "dma:" + owner.name
        for b in R:
            toks += list(b.w.items())
        for b in W:
            toks += [kv for kv in b.w.items() if kv[0] != okey] + list(b.r.items())
        self._need(q, toks)
        inst = q.h.dma_start(out=out, in_=in_, **kw)
        inst.then_inc(owner.sem, 16)
        owner.nd += 1
        tok = (owner.sem, 16 * owner.nd)
        key = "dma:" + owner.name
        for b in R:
            b.r[key] = tok
        for b in W:
            b.w[key] = tok
        if final:
            self.final.append((key, tok))
        return tok

    def view(self, name, ap, toks=None):
        b = Buf(name, ap)
        if toks:
            b.w.update(toks)
            b.r.update(toks)
        self.bufs.append(b)
        return b

    def all_tokens(self, bufs):
        toks = {}
        for e in self.engs:
            if e.n:
                toks[e.name] = (e.sem, e.n)
        for b in bufs:
            for d in (b.w, b.r):
                for key, tok in d.items():
                    if key.startswith("dma:"):
                        toks[key] = tok
        return toks

    def finish(self):
        self._need(self.SP, self.final)
        toks = [(e.name, (e.sem, e.n)) for e in self.engs if e.n and e is not self.SP]
        self._need(self.SP, toks)


def build_nc(dbg_names=()):
    nc = bass.Bass("TRN2", target_bir_lowering=False)
    DEAD[0] = False

    def din(name, shape):
        return nc.dram_tensor(name, list(shape), F32, kind="ExternalInput").ap()

    def dout(name, shape):
        return nc.dram_tensor(name, list(shape), F32, kind="ExternalOutput").ap()

    xp = din("xp", [NT_P * 128, D])
    xs = din("xs", [NT_S * 128, D])
    xf = din("xf", [NFOR * 128, D])
    cvec = din("cvec", [2, D])
    stC = din("stC", [2, 4, 128, 128])
    stn = din("stn", [2, 4, 128])
    stm = din("stm", [1, 8])
    flg = din("flg", [1, 6])
    consts = din("consts", [128, 512])
    w_ada = din("w_ada", [D, 6 * D])
    b_ada = din("b_ada", [6 * D])
    g_norm1 = din("g_norm1", [D])
    w_in = din("w_in", [D, DIN])
    b_gate = din("b_gate", [1, 16])
    w_s = din("w_s", [4, 128, 128])
    b_s = din("b_s", [1, 512])
    g_v = din("g_v", [1, 512])
    conv_w = din("conv_w", [3, 1024])
    conv_b = din("conv_b", [1024])
    g_h = din("g_h", [1, 512])
    w_out = din("w_out", [D, D])
    g_norm2 = din("g_norm2", [D])
    w1 = din("w1", [D, DFF])
    w3 = din("w3", [D, DFF])
    w2 = din("w2", [DFF, D])
    g_final = din("g_final", [1, D])

    yp = dout("yp", [NT_P * 128, D])
    ys = dout("ys", [NT_S * 128, D])
    nC = dout("nC", [2, 2, 4, 128, 128])
    nn = dout("nn", [2, 2, 4, 128])
    nm = dout("nm", [2, 8])
    dbg = {}

    es0 = ExitStack()
    with es0:
        k = K(nc, es0)
        PE, ACT, DVE, POOL, SP = k.PE, k.ACT, k.DVE, k.POOL, k.SP
        pe, act, dve, pool, sp = nc.tensor, nc.scalar, nc.vector, nc.gpsimd, nc.sync

        AOT = k.sb(es0, "AOT", [128, NT, 4, 128], BF16)
        BOT = k.sb(es0, "BOT", [128, NT, 4, 128], BF16)
        CF = k.sb(es0, "CF", [128, 512], F32)
        CB16 = k.sb(es0, "CB16", [128, 512], BF16)
        MODP = k.sb(es0, "MODP", [128, 48, 2], F32)
        S1 = k.sb(es0, "S1", [128, 8, 2], F32)
        S2 = k.sb(es0, "S2", [128, 8, 2], F32)
        G1 = k.sb(es0, "G1", [128, 8], F32)
        G2 = k.sb(es0, "G2", [128, 8], F32)
        GAB = k.sb(es0, "GAB", [128, 2, D], F32)
        SS = k.sb(es0, "SS", [128, 64], F32)
        RS = k.sb(es0, "RS", [128, 64], F32)
        MHALF = k.sb(es0, "MHALF", [128, 32], F32)
        OH = k.sb(es0, "OH", [8, 8, 128], F32)
        UTT = k.sb(es0, "UTT", [64, 128], F32)
        JUNK = k.sb(es0, "JUNK", [128, D], BF16)
        XN = [k.sb(es0, f"XN{i}", [128, D], BF16) for i in range(4)]
        PS = [k.ps(es0, f"PS{i}", [128, 512], F32) for i in range(8)]

        IDF = lambda n=128: CF[0:n, 0:n]
        TRIF = CF[:, 128:256]
        TRIB = CF[:, 256:384]
        ONESF = CF[:, 384:512]
        IDB = CB16[:, 0:128]
        MASK = [CB16[:, 128:256], CB16[:, 256:384]]
        ONESB = CB16[:, 384:512]

        k.dma(SP, CF[:], consts, W=[CF])
        k.dma(POOL, CB16[:], consts, W=[CB16])
        k.op(DVE, lambda: dve.memset(MHALF[:], -0.5), W=[MHALF])

        def rstd_of(ss_ap, out_ap, n, inv):
            k.op(POOL, lambda: pool.tensor_scalar(out=out_ap, in0=ss_ap, scalar1=inv, scalar2=EPS,
                                                  op0=ALU.mult, op1=ALU.add), R=[SS], W=[RS])
            k.op(POOL, lambda: pool.tensor_tensor(out=out_ap, in0=out_ap, in1=MHALF[:, 0:n], op=ALU.pow),
                 R=[RS, MHALF], W=[RS])

        try:
            es1 = ExitStack()
            with es1:
                WIN = k.sb(es1, "WIN", [128, 8, DIN], BF16)
                HT = [k.sb(es1, f"HT{i}", [128, 8, 512], BF16) for i in range(2)]
                XS = [k.view(f"XS{i}", GAB.t[:, i, :]) for i in range(2)]
                QT = k.sb(es1, "QT", [128, 4, NT_S * 128], BF16)
                KT = k.sb(es1, "KT", [128, 4, NT_S * 128], BF16)
                KTOK = k.sb(es1, "KTOK", [128, NT_S, 512], BF16)
                VV = k.sb(es1, "VV", [128, NT_S, 4, 132], BF16)
                TH = k.sb(es1, "TH", [128, NT_S, 512], BF16)
                HS = k.sb(es1, "HS", [128, NT_S, 512], F32)
                GUT = k.sb(es1, "GUT", [128, 4, 512], BF16)
                GV = k.sb(es1, "GV", [128, 512], F32)
                VGS = [k.sb(es1, f"VG{i}", [128, 512], BF16) for i in range(2)]
                PRE = [k.sb(es1, f"PRE{i}", [128, 512], F32) for i in range(3)]
                GT = k.sb(es1, "GT", [128, NT_S, 16], F32)
                SPT = k.sb(es1, "SPT", [128, NT_S * 8], F32)
                UT = k.sb(es1, "UT", [128, NT_S * 8], F32)
                CBT = k.sb(es1, "CBT", [128, NT_S * 8], F32)
                EE = k.sb(es1, "EE", [128, NT_S * 8], F32)
                THR = k.sb(es1, "THR", [128, NT_S * 8], F32)
                UMB = k.sb(es1, "UMB", [128, NT_S * 8], F32)
                BLB = k.sb(es1, "BLB", [128, NT_S * 8], F32)
                RB = k.sb(es1, "RB", [128, NT_S * 8], F32)
                MPV = k.sb(es1, "MPV", [128, NT_S * 8], F32)
                DCY = k.sb(es1, "DCY", [128, NT_S * 8], F32)
                MCUR = k.sb(es1, "MCUR", [128, 8], F32)
                UMX = k.sb(es1, "UMX", [64, 1], F32)
                DG = k.sb(es1, "DG", [64, 64], F32)
                CST = [[k.sb(es1, f"CST{d}{h}", [128, 132], F32) for h in range(4)] for d in range(2)]
                CBF = [k.sb(es1, f"CBF{i}", [128, 132], BF16) for i in range(8)]
                VP = [k.sb(es1, f"VP{i}", [128, 132], BF16) for i in range(8)]
                ST = [k.sb(es1, f"ST{i}", [128, 128], BF16) for i in range(8)]
                DD = k.sb(es1, "DD", [128, 32], F32)
                WST = k.sb(es1, "WST", [128, 4, 128], BF16)
                BSR = k.sb(es1, "BSR", [1, 512], BF16)
                GVB = k.sb(es1, "GVB", [128, 512], F32)
                GHB = k.sb(es1, "GHB", [128, 512], F32)
                BGB = k.sb(es1, "BGB", [128, 16], F32)
                CW = k.sb(es1, "CW", [128, 8, 3], F32)
                CBI = k.sb(es1, "CBI", [128, 8], F32)
                BADA = k.sb(es1, "BADA", [128, 48], F32)
                CT = k.sb(es1, "CT", [128, 8, 2], F32)
                SCT = k.sb(es1, "SCT", [128, 8, 2], BF16)
                WAB = [AOT, BOT]
                M0B = k.sb(es1, "M0B", [128, 8], F32)
                FLB = k.sb(es1, "FLB", [128, 6], F32)
                MISC = k.sb(es1, "MISC", [128, 16], F32)
                HTH = {id(HT[i]): [k.view(f"HT{i}lo", HT[i].t[:, 0:4, :]), k.view(f"HT{i}hi", HT[i].t[:, 4:8, :])]
                       for i in range(2)}
                KTv = [k.view(f"KT{i}", KT.t[:, :, i * 512:(i + 1) * 512]) for i in range(2)]
                QTv = [k.view(f"QT{i}", QT.t[:, :, i * 512:(i + 1) * 512]) for i in range(2)]
                KTOKv = [k.view(f"KTOKh{i}", KTOK.t[:, i * 4:(i + 1) * 4, :]) for i in range(2)]
                VVv = [k.view(f"VVh{i}", VV.t[:, i * 4:(i + 1) * 4, :, :]) for i in range(2)]
                GTv = [k.view(f"GTh{i}", GT.t[:, i * 4:(i + 1) * 4, :]) for i in range(2)]
                NEGT = k.sb(es1, "NEGT", [128, 4], F32)
                NEGF = k.sb(es1, "NEGF", [128, 6], F32)
                MLOC = k.sb(es1, "MLOC", [128, 3, 8], F32)
                BSEG = k.sb(es1, "BSEG", [128, 3, 8], F32)
                MFO = k.sb(es1, "MFO", [128, 8], F32)
                OFS = k.sb(es1, "OFS", [128, 64], F32)
                RSEG = k.sb(es1, "RSEG", [128, 8], F32)
                FT = k.sb(es1, "FT", [128, 16], F32)
                _botf = BOT.t[:, 4:12, :, :].rearrange("p t c d -> p (t c d)").bitcast(F32)
                _hsf = HS.t[:].rearrange("p t c -> p (t c)")
                SC = [[[None] * 4 for d in range(2)] for p in range(3)]
                SCX = {}
                for _p in range(3):
                    for _d in range(2):
                        for _h in range(4):
                            _n = _p * 8 + _d * 4 + _h
                            if _n < 15:
                                _b = k.view(f"SC{_p}{_d}{_h}", _botf[:, _n * 130:(_n + 1) * 130])
                                SCX[id(_b)] = [BOT]
                            else:
                                _o = 2112 + (_n - 15) * 130
                                _b = k.view(f"SC{_p}{_d}{_h}", _hsf[:, _o:_o + 130])
                                SCX[id(_b)] = None
                            SC[_p][_d][_h] = _b

                with nc.allow_non_contiguous_dma(reason="small param relayout"):
                    for v in range(2):
                        k.dma(SP, CT[:, :, v], cvec[v, :].rearrange("(k p) -> p k", p=128), W=[CT])
                    k.dma(SP, BADA[:], b_ada.rearrange("(c p) -> p c", p=128), W=[BADA])
                    k.dma(SP, G1[:], g_norm1.rearrange("(k p) -> p k", p=128), W=[G1])
                    k.dma(SP, G2[:], g_norm2.rearrange("(k p) -> p k", p=128), W=[G2])
                    for j in range(3):
                        k.dma(SP, CW[:, :, j], conv_w[j, :].rearrange("(k p) -> p k", p=128), W=[CW])
                    k.dma(SP, CBI[:], conv_b.rearrange("(k p) -> p k", p=128), W=[CBI])

                k.dma(SP, GVB[:], g_v.partition_broadcast(128), W=[GVB])
                k.dma(SP, GHB[:], g_h.partition_broadcast(128), W=[GHB])
                k.dma(SP, BGB[:], b_gate.partition_broadcast(128), W=[BGB])
                k.dma(SP, M0B[:], stm.partition_broadcast(128), W=[M0B])
                k.dma(SP, FLB[:], flg.partition_broadcast(128), W=[FLB])
                k.dma(POOL, BSR[:], b_s, W=[BSR])
                for g in range(4):
                    k.dma(SP, PRE[0][:, g * 128:(g + 1) * 128], w_s[g, :, :], W=[PRE[0]])
                k.group([lambda g=g: pe.transpose(PS[7][:, g * 128:(g + 1) * 128], PRE[0][:, g * 128:(g + 1) * 128], IDF())
                         for g in range(4)], R=[PRE[0], CF], W=[PS[7]])
                k.op(DVE, lambda: dve.tensor_copy(out=WST[:], in_=PS[7][:].rearrange("p (g t) -> p g t", t=128)),
                     R=[PS[7]], W=[WST])
                for kk in range(8):
                    k.op(DVE, lambda kk=kk: dve.tensor_scalar(out=OH[0:8, kk, :], in0=CF[0:8, 384:512],
                                                              scalar1=CF[0:8, kk:kk + 1], scalar2=None, op0=ALU.mult),
                         R=[CF], W=[OH])
                k.op(DVE, lambda: dve.tensor_scalar(out=GHB[:], in0=GHB[:], scalar1=0.5, scalar2=None, op0=ALU.mult),
                     R=[GHB], W=[GHB])
                for d in range(2):
                    for h in range(4):
                        k.op(POOL, lambda d=d, h=h: pool.memset(CST[d][h][:], 0.0), W=[CST[d][h]])
                k.op(POOL, lambda: pool.memset(VV[:, :, :, 128:132], 1.0), W=VVv)

                k.op(ACT, lambda: act.activation(out=SCT[:], in_=CT[:], func=AF.Silu), R=[CT], W=[SCT])
                wa_view = w_ada.rearrange("(k p) n -> p k n", p=128)

                def mod_block(cb):
                    slot = WAB[cb % 2]
                    sv = slot[:, 4:12, :, :].rearrange("p t c d -> p t (c d)")
                    k.dma(POOL, sv, wa_view[:, :, cb * 512:(cb + 1) * 512], W=[slot])
                    for i in range(4):
                        ch = cb * 4 + i
                        k.group([lambda kk=kk, i=i, ch=ch: pe.matmul(PS[0][:, ch * 2:ch * 2 + 2],
                                                                      lhsT=sv[:, kk, i * 128:(i + 1) * 128],
                                                                      rhs=SCT[:, kk, :], start=(kk == 0), stop=(kk == 7))
                                 for kk in range(8)], R=[slot, SCT], W=[PS[0]])

                def mod_load(cb):
                    sv = AOT[:, 4:12, :, :].rearrange("p t c d -> p t (c d)")
                    k.dma(POOL, sv, wa_view[:, :, cb * 512:(cb + 1) * 512], W=[AOT])

                def mod_compute(cb):
                    sv = AOT[:, 4:12, :, :].rearrange("p t c d -> p t (c d)")
                    pbm = PS[6]
                    for i in range(4):
                        k.group([lambda kk=kk, i=i: pe.matmul(pbm[:, i * 2:i * 2 + 2], lhsT=sv[:, kk, i * 128:(i + 1) * 128],
                                                              rhs=SCT[:, kk, :], start=(kk == 0), stop=(kk == 7))
                                 for kk in range(8)], R=[AOT, SCT], W=[pbm])
                    for v in range(2):
                        k.op(DVE, lambda v=v: dve.tensor_tensor(
                            out=MODP[:, cb * 4:cb * 4 + 4, v], in0=pbm[:, 0:8].rearrange("p (c v) -> p c v", v=2)[:, :, v],
                            in1=BADA[:, cb * 4:cb * 4 + 4], op=ALU.add), R=[pbm, BADA], W=[MODP])
                    if cb + 1 < 12:
                        mod_load(cb + 1)
                    if cb == 11:
                        for v in range(2):
                            k.op(DVE, lambda v=v: dve.scalar_tensor_tensor(out=S2[:, :, v], in0=MODP[:, 32:40, v], scalar=1.0,
                                                                           in1=G2[:], op0=ALU.add, op1=ALU.mult),
                                 R=[MODP, G2], W=[S2])

                def mod_finish(c0, c1):
                    for v in range(2):
                        k.op(DVE, lambda v=v: dve.tensor_tensor(
                            out=MODP[:, c0:c1, v], in0=PS[0][:, 2 * c0:2 * c1].rearrange("p (c v) -> p c v", v=2)[:, :, v],
                            in1=BADA[:, c0:c1], op=ALU.add), R=[PS[0], BADA], W=[MODP])

                for cb in range(4):
                    mod_block(cb)
                mod_finish(0, 16)
                for v in range(2):
                    k.op(DVE, lambda v=v: dve.scalar_tensor_tensor(out=S1[:, :, v], in0=MODP[:, 8:16, v], scalar=1.0,
                                                                   in1=G1[:], op0=ALU.add, op1=ALU.mult),
                         R=[MODP, G1], W=[S1])

                stage(1)
                win_view = w_in.rearrange("(k p) n -> p k n", p=128)
                for kk in range(8):
                    pass
                WING = {}
                for (g0, g1) in ((OFF_U, OFF_V), (OFF_Q, OFF_K), (OFF_K, OFF_VV), (OFF_VV, OFF_O), (OFF_G, DIN),
                                 (OFF_O, OFF_G), (OFF_V, OFF_Q)):
                    gb = k.view(f"WIN_{g0}", WIN.t[:, :, g0:g1])
                    for c0 in range(g0, g1, 128):
                        WING[c0] = gb
                    k.dma(POOL, WIN[:, :, g0:g1], win_view[:, :, g0:g1], W=[gb])

                ps_rot = {"proj": 0, "small": 0, "tr": 0}

                def bank(kind):
                    if kind == "tr":
                        b = PS[7 - ps_rot["tr"] % 2]
                    elif kind == "proj":
                        b = PS[2 + ps_rot["proj"] % 4]
                    else:
                        b = PS[6 + ps_rot["small"] % 2]
                    ps_rot[kind] += 1
                    return b

                xn_rot = [0]

                def norm_pre(xbuf, x_ap, ssl):
                    k.op(ACT, lambda: act.activation(out=JUNK[:], in_=x_ap, func=AF.Square, accum_out=SS[:, ssl:ssl + 1]),
                         R=[xbuf], W=[JUNK, SS])
                    rstd_of(SS[:, ssl:ssl + 1], RS[:, ssl:ssl + 1], 1, 1.0 / D)
                    xn = XN[xn_rot[0] % 4]
                    xn_rot[0] += 1
                    k.op(DVE, lambda: dve.tensor_scalar(out=xn[:], in0=x_ap, scalar1=RS[:, ssl:ssl + 1], scalar2=None,
                                                        op0=ALU.mult), R=[xbuf, RS], W=[xn])
                    return xn

                def norm_post(xn, Sx, SHoff, v, ht, col0):
                    pbs = [PS[0], PS[1]]
                    pvs = [pbs[0][:].bitcast(BF16), pbs[1][:].bitcast(BF16)]
                    for hb in range(2):
                        k.group([lambda kk=kk, hb=hb: pe.transpose(pvs[hb][:, (kk % 4) * 128:(kk % 4 + 1) * 128],
                                                                  xn[:, kk * 128:(kk + 1) * 128], IDB)
                                 for kk in range(hb * 4, hb * 4 + 4)], R=[xn, CB16], W=[pbs[hb]])
                    for j in range(4):
                        kk = j
                        k.op(DVE, lambda kk=kk, j=j: dve.tensor_scalar(out=ht[:, kk, col0:col0 + 128],
                                                                       in0=pvs[0][:, j * 128:(j + 1) * 128],
                                                                       scalar1=Sx[:, kk, v:v + 1],
                                                                       scalar2=MODP[:, SHoff + kk, v:v + 1],
                                                                       op0=ALU.mult, op1=ALU.add),
                             R=[pbs[0], Sx, MODP], W=[HTH[id(ht)][0]])
                        kk = 4 + j
                        k.op(ACT, lambda kk=kk, j=j: act.activation(out=ht[:, kk, col0:col0 + 128],
                                                                    in_=pvs[1][:, j * 128:(j + 1) * 128], func=AF.Identity,
                                                                    scale=Sx[:, kk, v:v + 1],
                                                                    bias=MODP[:, SHoff + kk, v:v + 1]),
                             R=[pbs[1], Sx, MODP], W=[HTH[id(ht)][1]])

                lagq = []

                def lag_flush():
                    while lagq:
                        lagq.pop(0)()

                def lag_push(fn):
                    lag_flush()
                    lagq.append(fn)

                def proj_fm(ht, ntok, col, evac):
                    pb = bank("proj")
                    k.group([lambda kk=kk: pe.matmul(pb[:, 0:ntok], lhsT=WIN[:, kk, col:col + 128], rhs=ht[:, kk, 0:ntok],
                                                     start=(kk == 0), stop=(kk == 7)) for kk in range(8)],
                            R=[WING[col]] + HTH[id(ht)], W=[pb])
                    lag_push(lambda: evac(pb))

                def proj_tm(ht, tcol, col, ncol, evac, kind="proj"):
                    pb = bank(kind)
                    k.group([lambda kk=kk: pe.matmul(pb[:, 0:ncol], lhsT=ht[:, kk, tcol:tcol + 128],
                                                     rhs=WIN[:, kk, col:col + ncol],
                                                     start=(kk == 0), stop=(kk == 7)) for kk in range(8)],
                            R=[WING[col]] + HTH[id(ht)], W=[pb])
                    lag_push(lambda: evac(pb))

                pre_rot = [0]

                def conv_silu(pb, ch, ntok, seqlen, out_ap, outbuf):
                    pr = PRE[pre_rot[0] % 3]
                    pre_rot[0] += 1
                    k.op(ACT, lambda: act.activation(out=pr[:, 0:ntok], in_=pb[:, 0:ntok], func=AF.Identity,
                                                     scale=CW[:, ch, 1:2], bias=CBI[:, ch:ch + 1]),
                         R=[pb, CW, CBI], W=[pr])
                    pv = pb[:, 0:ntok].rearrange("p (s t) -> p s t", t=seqlen)
                    rv = pr[:, 0:ntok].rearrange("p (s t) -> p s t", t=seqlen)

                    def stage_b():
                        k.op(DVE, lambda: dve.scalar_tensor_tensor(out=rv[:, :, 1:seqlen], in0=pv[:, :, 0:seqlen - 1],
                                                                   scalar=CW[:, ch, 0:1], in1=rv[:, :, 1:seqlen],
                                                                   op0=ALU.mult, op1=ALU.add), R=[pb, CW, pr], W=[pr])
                        k.op(DVE, lambda: dve.scalar_tensor_tensor(out=rv[:, :, 0:seqlen - 1], in0=pv[:, :, 1:seqlen],
                                                                   scalar=CW[:, ch, 2:3], in1=rv[:, :, 0:seqlen - 1],
                                                                   op0=ALU.mult, op1=ALU.add), R=[pb, CW, pr], W=[pr])

                    def stage_c():
                        k.op(ACT, lambda: act.activation(out=out_ap, in_=pr[:, 0:ntok], func=AF.Silu), R=[pr], W=[outbuf])

                    if conv_c:
                        conv_c.pop(0)()
                    if conv_b:
                        fb, fc = conv_b.pop(0)
                        fb()
                        conv_c.append(fc)
                    conv_b.append((stage_b, stage_c))

                conv_b = []
                conv_c = []

                def conv_flush():
                    while conv_b:
                        fb, fc = conv_b.pop(0)
                        fb()
                        conv_c.append(fc)
                    while conv_c:
                        conv_c.pop(0)()

                ktok_rot = [0]

                def k_to_tok_tile(ut):
                    lag_flush()
                    conv_flush()
                    pb = bank("small")
                    pbv = pb[:].bitcast(BF16)
                    k.group([lambda h=h: pe.transpose(pbv[:, h * 128:(h + 1) * 128], KT[:, h, ut * 128:(ut + 1) * 128], IDB)
                             for h in range(4)], R=[KTv[ut // 4], CB16], W=[pb])
                    ktok_rot[0] += 1
                    if ktok_rot[0] % 2 == 0:
                        k.op(DVE, lambda: dve.tensor_copy(out=KTOK[:, ut, :], in_=pbv[:, 0:512]), R=[pb], W=[KTOKv[ut // 4]])
                    else:
                        k.op(ACT, lambda: act.activation(out=KTOK[:, ut, :], in_=pbv[:, 0:512], func=AF.Copy), R=[pb], W=[KTOKv[ut // 4]])

                def gates_evac(pb, ut):
                    k.op(DVE, lambda: dve.tensor_tensor(out=GT[:, ut, :], in0=pb[:, 0:16], in1=BGB[:], op=ALU.add),
                         R=[pb, BGB], W=[GTv[ut // 4]])

                pa_state = {"xc": 0, "n": 0}

                def mk_blk(tiles, v, seqlen, own, u0, after=None):
                    blk = dict(tiles=tiles, v=v, seqlen=seqlen, own=own, u0=u0, after=after, idx=pa_state["n"])
                    blk["ht"] = HT[pa_state["n"] % 2]
                    pa_state["n"] += 1
                    return blk

                def pa_pre(blk, i):
                    t = blk["tiles"][i]
                    xb = XS[pa_state["xc"] % 2]
                    pa_state["xc"] += 1
                    if blk["own"]:
                        src = xp[t * 128:(t + 1) * 128, :] if t < NT_P else xs[(t - NT_P) * 128:(t - NT_P + 1) * 128, :]
                    else:
                        src = xf[t * 128:(t + 1) * 128, :]
                    k.dma(SP, xb[:], src, W=[xb])
                    blk.setdefault("xn", {})[i] = norm_pre(xb, xb[:], (blk["idx"] % 2) * 4 + i)

                def pa_post(blk, i):
                    norm_post(blk["xn"][i], S1, 0, blk["v"], blk["ht"], i * 128)

                def pa_items(blk):
                    tiles, v, seqlen, own, u0, ht = blk["tiles"], blk["v"], blk["seqlen"], blk["own"], blk["u0"], blk["ht"]
                    nt = len(tiles)
                    ntok = nt * 128
                    items = []
                    if own:
                        for c in range(4):
                            items.append(lambda c=c: proj_fm(ht, ntok, OFF_U + c * 128,
                                         lambda pb: k.op(ACT, lambda: act.activation(out=GUT[:, c, 0:ntok], in_=pb[:, 0:ntok],
                                                                                     func=AF.Gelu_apprx_tanh), R=[pb], W=[GUT])))
                        for c in range(4):
                            items.append(lambda c=c: proj_fm(ht, ntok, OFF_Q + c * 128,
                                         lambda pb: conv_silu(pb, c, ntok, seqlen, QT[:, c, u0 * 128:u0 * 128 + ntok], QTv[u0 // 4])))
                    for c in range(4):
                        items.append(lambda c=c: proj_fm(ht, ntok, OFF_K + c * 128,
                                     lambda pb: conv_silu(pb, 4 + c, ntok, seqlen, KT[:, c, u0 * 128:u0 * 128 + ntok], KTv[u0 // 4])))

                    def fm_tail_flush():
                        lag_flush()
                        conv_flush()
                    items.append(fm_tail_flush)

                    def tile_item(i, t):
                        ut = u0 + i
                        if i % 2 == 0:
                            proj_tm(ht, i * 128, OFF_VV, 512,
                                    lambda pb: k.op(ACT, lambda: act.activation(
                                        out=VV[:, ut, :, 0:128], in_=pb[:].rearrange("p (h d) -> p h d", d=128), func=AF.Copy),
                                        R=[pb], W=[VVv[ut // 4]]))
                        else:
                            proj_tm(ht, i * 128, OFF_VV, 512,
                                    lambda pb: k.op(DVE, lambda: dve.tensor_copy(
                                        out=VV[:, ut, :, 0:128], in_=pb[:].rearrange("p (h d) -> p h d", d=128)),
                                        R=[pb], W=[VVv[ut // 4]]))
                        proj_tm(ht, i * 128, OFF_G, 16, lambda pb: gates_evac(pb, ut), kind="small")

                    def tile_item_own(i, t):
                        ut = u0 + i
                        proj_tm(ht, i * 128, OFF_O, 512,
                                lambda pb: k.op(ACT, lambda: act.activation(out=TH[:, ut, :], in_=pb[:], func=AF.Tanh,
                                                                            scale=0.5), R=[pb], W=[TH]))
                        proj_tm(ht, i * 128, OFF_V, 512,
                                lambda pb: k.op(ACT, lambda: act.activation(out=GV[:], in_=pb[:], func=AF.Gelu_apprx_tanh),
                                                R=[pb], W=[GV]))
                        lag_flush()
                        for g in range(4):
                            k.op(ACT, lambda g=g: act.activation(out=JUNK[:, 0:128], in_=GV[:, g * 128:(g + 1) * 128],
                                                                 func=AF.Square, accum_out=SS[:, 8 + g:9 + g]),
                                 R=[GV], W=[JUNK, SS])
                        rstd_of(SS[:, 8:12], RS[:, 8:12], 4, 1.0 / 128)
                        vg = VGS[i % 2]
                        for g in range(4):
                            k.op(DVE, lambda g=g: dve.scalar_tensor_tensor(
                                out=vg[:, g * 128:(g + 1) * 128], in0=GV[:, g * 128:(g + 1) * 128],
                                scalar=RS[:, 8 + g:9 + g], in1=GVB[:, g * 128:(g + 1) * 128],
                                op0=ALU.mult, op1=ALU.mult), R=[GV, RS, GVB], W=[vg])

                    def tile_item_own_b(i, t):
                        vg = VGS[i % 2]
                        pb = bank("proj")
                        fns = []
                        for g in range(4):
                            fns.append(lambda g=g: pe.matmul(pb[:, g * 128:(g + 1) * 128], lhsT=vg[:, g * 128:(g + 1) * 128],
                                                             rhs=WST[:, g, :], start=True, stop=False))
                            fns.append(lambda g=g: pe.matmul(pb[:, g * 128:(g + 1) * 128], lhsT=ONESB[0:1, :],
                                                             rhs=BSR[0:1, g * 128:(g + 1) * 128], start=False, stop=True))
                        k.group(fns, R=[vg, WST, CB16, BSR], W=[pb])
                        k.op(DVE, lambda: dve.tensor_tensor(
                            out=AOT[:, t, :, :], in0=pb[:].rearrange("p (g t) -> p g t", t=128),
                            in1=GUT[:, :, i * 128:(i + 1) * 128], op=ALU.mult), R=[pb, GUT], W=[AOT])

                    def full_flush():
                        lag_flush()
                        conv_flush()
                    if blk.get("pre_a") is not None:
                        items.insert(0, blk["pre_a"])
                    if blk.get("mid_a") is not None:
                        items.append(full_flush)
                        items.append(blk["mid_a"])
                    for i, t in enumerate(tiles):
                        items.append(lambda i=i, t=t: tile_item(i, t))
                        if own:
                            items.append(lambda i=i, t=t: tile_item_own(i, t))
                            if i > 0:
                                items.append(lambda i=i: tile_item_own_b(i - 1, tiles[i - 1]))
                    if own:
                        items.append(lambda: tile_item_own_b(nt - 1, tiles[nt - 1]))
                    if blk.get("mid_b") is not None:
                        items.append(full_flush)
                        items.append(blk["mid_b"])
                    for i, t in enumerate(tiles):
                        items.append(lambda i=i: k_to_tok_tile(u0 + i))
                    return items

                def run_blocks(blks):
                    for n_, b_ in enumerate(blks):
                        b_["ht"] = HT[n_ % 2]
                        b_["idx"] = n_
                    for i in range(4):
                        pa_pre(blks[0], i)
                    for i in range(4):
                        pa_post(blks[0], i)
                    for n, blk in enumerate(blks):
                        items = pa_items(blk)
                        nxt = blks[n + 1] if n + 1 < len(blks) else None
                        L = len(items)
                        marks = {max(0, (L * (q + 1)) // 8): q for q in range(4)}
                        if nxt is not None:
                            for i in range(4):
                                pa_pre(nxt, i)
                        mods = blk.get("mods")
                        extras = list(blk.get("extras") or [])
                        for idx, it in enumerate(items):
                            it()
                            if extras:
                                lag_flush()
                                extras.pop(0)()
                            if nxt is not None and idx in marks:
                                pa_post(nxt, marks[idx])
                            if mods is not None and idx == L // 4:
                                lag_flush()
                                mod_compute(mods[0])
                            if mods is not None and idx == (3 * L) // 4:
                                lag_flush()
                                mod_compute(mods[1])
                        lag_flush()
                        conv_flush()
                        while extras:
                            extras.pop(0)()
                        if blk["after"] is not None:
                            blk["after"]()

                def gates_unit(ntl, seqs, m0_aps, prefix_only=False):
                    n8 = ntl * 8
                    gv4 = GT[:, 0:ntl, :].rearrange("p t (d j h) -> p t d j h", d=2, j=2)
                    SPT3 = SPT[:, 0:n8].rearrange("p (t e) -> p t e", e=8)
                    for d in range(2):
                        k.op(ACT, lambda d=d: act.activation(out=SPT3[:, :, d * 4:(d + 1) * 4], in_=gv4[:, :, d, 1, :],
                                                             func=AF.Exp, scale=-1.0), R=GTv, W=[SPT])
                    k.op(ACT, lambda: act.activation(out=SPT[:, 0:n8], in_=SPT[:, 0:n8], func=AF.Ln, bias=1.0),
                         R=[SPT], W=[SPT])
                    pcb = PS[0]
                    fns = []
                    for t in range(ntl):
                        fns.append(lambda t=t: pe.matmul(pcb[:, t * 8:t * 8 + 4], lhsT=TRIF, rhs=SPT[:, t * 8:t * 8 + 4],
                                                         start=True, stop=True))
                        fns.append(lambda t=t: pe.matmul(pcb[:, t * 8 + 4:t * 8 + 8], lhsT=TRIB, rhs=SPT[:, t * 8 + 4:t * 8 + 8],
                                                         start=True, stop=True))
                    k.group(fns, R=[CF, SPT], W=[pcb])
                    k.op(DVE, lambda: dve.tensor_copy(out=CBT[:, 0:n8], in_=pcb[:, 0:n8]), R=[pcb], W=[CBT])
                    for d in range(2):
                        k.op(DVE, lambda d=d: dve.tensor_tensor(
                            out=UT[:, 0:n8].rearrange("p (t e) -> p t e", e=8)[:, :, d * 4:(d + 1) * 4],
                            in0=CBT[:, 0:n8].rearrange("p (t e) -> p t e", e=8)[:, :, d * 4:(d + 1) * 4],
                            in1=gv4[:, :, d, 0, :], op=ALU.add), R=[CBT] + GTv, W=[UT])
                    ptr = PS[1]
                    k.group([lambda: pe.transpose(ptr[0:n8, 0:128], UT[:, 0:n8], IDF())], R=[UT, CF], W=[ptr])
                    k.op(DVE, lambda: dve.tensor_reduce(out=UMX[0:n8, :], in_=ptr[0:n8, 0:128], axis=AX.X, op=ALU.max),
                         R=[ptr], W=[UMX])
                    k.op(DVE, lambda: dve.tensor_scalar(out=DG[0:n8, 0:n8], in0=CF[0:n8, 0:n8], scalar1=UMX[0:n8, 0:1],
                                                        scalar2=None, op0=ALU.mult), R=[CF, UMX], W=[DG])
                    pbb = PS[1]
                    k.group([lambda: pe.matmul(pbb[:, 128:128 + n8], lhsT=ONESF[0:n8, :], rhs=DG[0:n8, 0:n8],
                                               start=True, stop=True)], R=[CF, DG], W=[pbb])
                    k.op(DVE, lambda: dve.tensor_copy(out=UMB[:, 0:n8], in_=pbb[:, 128:128 + n8]), R=[pbb], W=[UMB])
                    k.group([lambda: pe.matmul(pbb[:, 256:256 + n8], lhsT=ONESF,
                                               rhs=SPT[:, 0:n8], start=True, stop=True)],
                            R=[CF, SPT], W=[pbb])
                    k.op(DVE, lambda: dve.tensor_copy(out=BLB[:, 0:n8], in_=pbb[:, 256:256 + n8]), R=[pbb], W=[BLB])
                    if prefix_only:
                        return
                    for si, seq in enumerate(seqs):
                        for d in range(2):
                            order = seq if d == 0 else seq[::-1]
                            m0 = m0_aps[si][d]
                            if m0 is None:
                                k.op(DVE, lambda d=d: dve.memset(MCUR[:, d * 4:(d + 1) * 4], 0.0), W=[MCUR])
                            else:
                                k.op(DVE, lambda d=d, m0=m0: dve.tensor_copy(out=MCUR[:, d * 4:(d + 1) * 4], in_=m0[0]),
                                     R=[m0[1]], W=[MCUR])
                            for t in order:
                                sl = slice(t * 8 + d * 4, t * 8 + d * 4 + 4)
                                k.op(DVE, lambda sl=sl, d=d: dve.tensor_copy(out=MPV[:, sl], in_=MCUR[:, d * 4:(d + 1) * 4]),
                                     R=[MCUR], W=[MPV])
                                k.op(DVE, lambda sl=sl, d=d: dve.tensor_tensor(out=RB[:, sl], in0=MCUR[:, d * 4:(d + 1) * 4],
                                                                               in1=UMB[:, sl], op=ALU.max),
                                     R=[MCUR, UMB], W=[RB])
                                k.op(DVE, lambda sl=sl, d=d: dve.tensor_tensor(out=MCUR[:, d * 4:(d + 1) * 4], in0=RB[:, sl],
                                                                               in1=BLB[:, sl], op=ALU.subtract),
                                     R=[RB, BLB], W=[MCUR])
                            seq_end(si, d)
                    k.op(DVE, lambda: dve.tensor_tensor(out=EE[:, 0:n8], in0=UT[:, 0:n8], in1=RB[:, 0:n8], op=ALU.subtract),
                         R=[UT, RB], W=[EE])
                    k.op(ACT, lambda: act.activation(out=EE[:, 0:n8], in_=EE[:, 0:n8], func=AF.Exp, bias=MISC[:, 0:1]),
                         R=[EE, MISC], W=[EE])
                    k.op(DVE, lambda: dve.tensor_tensor(out=THR[:, 0:n8], in0=CBT[:, 0:n8], in1=RB[:, 0:n8], op=ALU.subtract),
                         R=[CBT, RB], W=[THR])
                    k.op(ACT, lambda: act.activation(out=THR[:, 0:n8], in_=THR[:, 0:n8], func=AF.Exp), R=[THR], W=[THR])
                    k.op(DVE, lambda: dve.tensor_tensor(out=DCY[:, 0:n8], in0=MPV[:, 0:n8], in1=RB[:, 0:n8], op=ALU.subtract),
                         R=[MPV, RB], W=[DCY])
                    k.op(ACT, lambda: act.activation(out=DCY[:, 0:n8], in_=DCY[:, 0:n8], func=AF.Exp), R=[DCY], W=[DCY])

                seq_end_cb = [None]

                def seq_end(si, d):
                    if seq_end_cb[0] is not None:
                        seq_end_cb[0](si, d)

                k.op(DVE, lambda: dve.memset(MISC[:, 0:1], LNS), W=[MISC])

                NCH = 8
                HSV = [k.view(f"HSV{t}", HS.t[:, t, :]) for t in range(NT_S)]
                DDV = [k.view(f"DDV{i}", DD.t[:, i * 4:(i + 1) * 4]) for i in range(NCH)]
                hs_written = {}

                def mlstm_wave(chains, banks=None, slot0=0):
                    n = len(chains)
                    info = []
                    for i, (ut, d, h) in enumerate(chains):
                        j = slot0 + i
                        info.append(dict(ut=ut, d=d, h=h, col=ut * 8 + d * 4 + h, pb=(banks[i] if banks else PS[i]), vp=VP[j],
                                         cb=CBF[j], st=ST[j], dd=DDV[j], cst=CST[d][h], qs=slice(ut * 128, (ut + 1) * 128)))
                    for i, c in enumerate(info):
                        k.op(ACT, lambda c=c: act.activation(out=c["vp"][:, 0:130], in_=VV[:, c["ut"], c["h"], 0:130],
                                                             func=AF.Identity, scale=EE[:, c["col"]:c["col"] + 1]),
                             R=[VVv[c["ut"] // 4], EE], W=[c["vp"]])
                        if False:
                            pass
                        else:
                            k.op(ACT, lambda c=c: act.activation(out=c["cb"][:, 0:130], in_=c["cst"][:, 0:130],
                                                                 func=AF.Identity, scale=DCY[:, c["col"]:c["col"] + 1]),
                                 R=[c["cst"], DCY], W=[c["cb"]])
                    for c in info:
                        k.group([lambda c=c: pe.matmul(c["pb"][:, 0:128], lhsT=KT[:, c["h"], c["qs"]], rhs=QT[:, c["h"], c["qs"]],
                                                       start=True, stop=True)], R=[KTv[c["ut"] // 4], QTv[c["ut"] // 4]], W=[c["pb"]])
                    for c in info:
                        k.op(DVE, lambda c=c: dve.tensor_tensor(out=c["st"][:], in0=c["pb"][:, 0:128], in1=MASK[c["d"]],
                                                                op=ALU.mult), R=[c["pb"], CB16], W=[c["st"]])
                    for c in info:
                        k.group([lambda c=c: pe.matmul(c["pb"][:, 128:258], lhsT=c["st"][:], rhs=c["vp"][:, 0:130],
                                                       start=True, stop=False),
                                 lambda c=c: pe.matmul(c["pb"][:, 128:258], lhsT=QT[:, c["h"], c["qs"]], rhs=c["cb"][:, 0:130],
                                                       start=False, stop=True),
                                 lambda c=c: pe.matmul(c["pb"][:, 260:390], lhsT=KTOK[:, c["ut"], c["h"] * 128:(c["h"] + 1) * 128],
                                                       rhs=c["vp"][:, 0:130], start=True, stop=True)],
                                R=[c["st"], c["vp"], QTv[c["ut"] // 4], c["cb"], KTOKv[c["ut"] // 4]], W=[c["pb"]])
                    for c in info:
                        dd = c["dd"]
                        k.op(DVE, lambda c=c, dd=dd: dve.tensor_scalar(out=dd[:, 2:3], in0=c["pb"][:, 256:257], scalar1=-1.0,
                                                                       scalar2=None, op0=ALU.mult), R=[c["pb"]], W=[dd])
                        k.op(DVE, lambda c=c, dd=dd: dve.scalar_tensor_tensor(out=dd[:, 0:1], in0=c["pb"][:, 256:257],
                                                                              scalar=THR[:, c["col"]:c["col"] + 1],
                                                                              in1=dd[:, 2:3], op0=ALU.max, op1=ALU.max),
                             R=[c["pb"], THR, dd], W=[dd])
                        k.op(DVE, lambda dd=dd: dve.reciprocal(out=dd[:, 1:2], in_=dd[:, 0:1]), R=[dd], W=[dd])
                    for c in info:
                        dd = c["dd"]
                        hsb = HSV[c["ut"]]
                        hsl = slice(c["h"] * 128, (c["h"] + 1) * 128)
                        key = (c["ut"], c["h"])
                        if key not in hs_written:
                            hs_written[key] = True
                            k.op(ACT, lambda c=c, dd=dd, hsb=hsb, hsl=hsl: act.activation(out=hsb[:, hsl], in_=c["pb"][:, 128:256],
                                                                                         func=AF.Identity, scale=dd[:, 1:2]),
                                 R=[c["pb"], dd], W=[hsb])
                        else:
                            k.op(DVE, lambda c=c, dd=dd, hsb=hsb, hsl=hsl: dve.scalar_tensor_tensor(
                                out=hsb[:, hsl], in0=c["pb"][:, 128:256], scalar=dd[:, 1:2], in1=hsb[:, hsl],
                                op0=ALU.mult, op1=ALU.add), R=[c["pb"], dd, hsb], W=[hsb])
                    for c in info:
                        k.op(DVE, lambda c=c: dve.scalar_tensor_tensor(out=c["cst"][:, 0:130], in0=c["cst"][:, 0:130],
                                                                       scalar=DCY[:, c["col"]:c["col"] + 1],
                                                                       in1=c["pb"][:, 260:390], op0=ALU.mult, op1=ALU.add),
                             R=[c["cst"], DCY, c["pb"]], W=[c["cst"]])

                def mlstm_post_unit(uts, gts):
                    ntl = len(uts)
                    for ut in uts:
                        for h in range(4):
                            k.op(ACT, lambda h=h, ut=ut: act.activation(out=JUNK[:, 0:128], in_=HSV[ut][:, h * 128:(h + 1) * 128],
                                                                        func=AF.Square,
                                                                        accum_out=SS[:, 16 + ut * 4 + h:17 + ut * 4 + h]),
                                 R=[HSV[ut]], W=[JUNK, SS])
                        rstd_of(SS[:, 16 + 4 * ut:20 + 4 * ut], RS[:, 16 + 4 * ut:20 + 4 * ut], 4, 1.0 / 128)
                    for ut, gt in zip(uts, gts):
                        for h in range(4):
                            k.op(DVE, lambda h=h, ut=ut: dve.scalar_tensor_tensor(
                                out=HSV[ut][:, h * 128:(h + 1) * 128], in0=HSV[ut][:, h * 128:(h + 1) * 128],
                                scalar=RS[:, 16 + ut * 4 + h:17 + ut * 4 + h], in1=GHB[:, h * 128:(h + 1) * 128],
                                op0=ALU.mult, op1=ALU.mult), R=[HSV[ut], RS, GHB], W=[HSV[ut]])
                        xn = XN[xn_rot[0] % 2]
                        xn_rot[0] += 1
                        k.op(POOL, lambda ut=ut, xn=xn: pool.scalar_tensor_tensor(out=xn[:, 0:512], in0=TH[:, ut, :], scalar=1.0,
                                                                                   in1=HSV[ut][:], op0=ALU.add, op1=ALU.mult),
                             R=[TH, HSV[ut]], W=[xn]) if False else k.op(
                            DVE, lambda ut=ut, xn=xn: dve.scalar_tensor_tensor(out=xn[:, 0:512], in0=TH[:, ut, :], scalar=1.0,
                                                                               in1=HSV[ut][:], op0=ALU.add, op1=ALU.mult),
                            R=[TH, HSV[ut]], W=[xn])
                        pb = bank("tr")
                        pbv = pb[:].bitcast(BF16)
                        k.group([lambda c=c, xn=xn, pbv=pbv: pe.transpose(pbv[:, c * 128:(c + 1) * 128],
                                                                          xn[:, c * 128:(c + 1) * 128], IDB)
                                 for c in range(4)], R=[xn, CB16], W=[pb])
                        k.op(ACT, lambda gt=gt, pbv=pbv: act.activation(out=BOT[:, gt, :, :],
                                                                        in_=pbv[:, 0:512].rearrange("p (c t) -> p c t", t=128),
                                                                        func=AF.Copy), R=[pb], W=[BOT])

                prompt_blk = mk_blk([0, 1, 2, 3], 0, 256, True, 0)

                def prompt_seq_end(si, d):
                    k.dma(SP, nm[si:si + 1, d * 4:(d + 1) * 4], MCUR[0:1, d * 4:(d + 1) * 4], R=[MCUR], final=True)

                def prompt_gates():
                    seq_end_cb[0] = prompt_seq_end
                    gates_unit(4, [[0, 1], [2, 3]], [[None, None], [None, None]])
                    seq_end_cb[0] = None

                HW_BANKS = [PS[0], PS[1], PS[6], PS[7]]
                prompt_extras = [prompt_gates]
                for si, seq in enumerate([[0, 1], [2, 3]]):
                    def zero_states():
                        for d in range(2):
                            for h in range(4):
                                k.op(POOL, lambda d=d, h=h: pool.memset(CST[d][h][:], 0.0), W=[CST[d][h]])
                    prompt_extras.append(zero_states)
                    for w in range(2):
                        prompt_extras.append(lambda seq=seq, w=w: mlstm_wave([(seq[w], 0, h) for h in range(4)],
                                                                            banks=HW_BANKS, slot0=0))
                        prompt_extras.append(lambda seq=seq, w=w: mlstm_wave([(seq[1 - w], 1, h) for h in range(4)],
                                                                            banks=HW_BANKS, slot0=4))

                    def store_states(si=si):
                        with nc.allow_non_contiguous_dma(reason="state column"):
                            for d in range(2):
                                for h in range(4):
                                    k.dma(SP, nC[si, d, h, :, :], CST[d][h][:, 0:128], R=[CST[d][h]], final=True)
                                    k.dma(SP, nn[si, d, h, :].rearrange("(p o) -> p o", o=1), CST[d][h][:, 128:129],
                                          R=[CST[d][h]], final=True)
                    prompt_extras.append(store_states)
                prompt_extras.append(lambda: mlstm_post_unit([0, 1, 2, 3], [0, 1, 2, 3]))

                mod_load(4)

                stage(5)
                k.op(DVE, lambda: dve.memset(NEGT[:], NEG), W=[NEGT])
                k.op(DVE, lambda: dve.tensor_scalar(out=NEGF[:], in0=FLB[:], scalar1=-1.0, scalar2=-NEG,
                                                    op0=ALU.add, op1=ALU.mult), R=[FLB], W=[NEGF])
                def summary_pre(p):
                    gates_unit(8, None, None, prefix_only=True)
                    summary_a0(p)

                def summary_a0(p):
                    B3 = BLB[:, 0:64].rearrange("p (t e) -> p t e", e=8)
                    O3 = OFS[:, 0:64].rearrange("p (t e) -> p t e", e=8)
                    k.op(DVE, lambda: dve.memset(OFS[:], 0.0), W=[OFS])
                    for t in range(1, 8):
                        k.op(DVE, lambda t=t: dve.tensor_tensor(out=O3[:, t, 0:4], in0=O3[:, t - 1, 0:4], in1=B3[:, t - 1, 0:4],
                                                                op=ALU.add), R=[OFS, BLB], W=[OFS])
                    for t in range(6, -1, -1):
                        k.op(DVE, lambda t=t: dve.tensor_tensor(out=O3[:, t, 4:8], in0=O3[:, t + 1, 4:8], in1=B3[:, t + 1, 4:8],
                                                                op=ALU.add), R=[OFS, BLB], W=[OFS])
                    k.op(DVE, lambda p=p: dve.tensor_reduce(out=BSEG[:, p - 1, :],
                                                            in_=BLB[:, 0:64].rearrange("p (t e) -> p e t", e=8),
                                                            axis=AX.X, op=ALU.add), R=[BLB], W=[BSEG])
                    k.op(DVE, lambda: dve.tensor_tensor(out=RB[:, 0:64], in0=UMB[:, 0:64], in1=OFS[:, 0:64], op=ALU.add),
                         R=[UMB, OFS], W=[RB])
                    k.op(DVE, lambda: dve.tensor_reduce(out=RSEG[:], in_=RB[:, 0:64].rearrange("p (t e) -> p e t", e=8),
                                                        axis=AX.X, op=ALU.max), R=[RB], W=[RSEG])
                    k.op(DVE, lambda p=p: dve.tensor_tensor(out=MLOC[:, p - 1, :], in0=RSEG[:], in1=BSEG[:, p - 1, :],
                                                            op=ALU.subtract), R=[RSEG, BSEG], W=[MLOC])
                    k.op(DVE, lambda: dve.tensor_tensor(out=EE[:, 0:64], in0=UT[:, 0:64], in1=OFS[:, 0:64], op=ALU.add),
                         R=[UT, OFS], W=[EE])
                    for t in range(8):
                        k.op(DVE, lambda t=t: dve.tensor_tensor(out=EE[:, t * 8:t * 8 + 8], in0=EE[:, t * 8:t * 8 + 8],
                                                                in1=RSEG[:], op=ALU.subtract), R=[EE, RSEG], W=[EE])
                    k.op(ACT, lambda: act.activation(out=EE[:, 0:64], in_=EE[:, 0:64], func=AF.Exp, bias=MISC[:, 0:1]),
                         R=[EE, MISC], W=[EE])
                    summary_vpb(0)

                def summary_a(p):
                    summary_mm(p, 0)
                    summary_vpb(1)

                VPB = HS.t[:].rearrange("p t c -> p (t c)").bitcast(BF16)[:, 0:8 * 4 * 132].rearrange(
                    "p (t h c) -> p t h c", t=8, h=4)

                def summary_vpb(d):
                    for t in range(8):
                        k.op(DVE, lambda t=t, d=d: dve.tensor_tensor(
                            out=VPB[:, t, :, 0:130], in0=VV[:, t, :, 0:130],
                            in1=EE[:, t * 8 + d * 4:t * 8 + d * 4 + 4].unsqueeze(2).to_broadcast([128, 4, 130]),
                            op=ALU.mult), R=VVv + [EE], W=HSV)

                def summary_mm(p, d):
                    pbs = [PS[2 + d * 2], PS[3 + d * 2]]
                    for h in range(4):
                        pb = pbs[h // 2]
                        c0 = (h % 2) * 132
                        k.group([lambda t=t, h=h, pb=pb, c0=c0: pe.matmul(pb[:, c0:c0 + 130],
                                                                         lhsT=KTOK[:, t, h * 128:(h + 1) * 128],
                                                                         rhs=VPB[:, t, h, 0:130],
                                                                         start=(t == 0), stop=(t == 7)) for t in range(8)],
                                R=KTOKv + HSV, W=[pb])
                        scb = SC[p - 1][d][h]
                        k.op(ACT, lambda scb=scb, pb=pb, c0=c0: act.activation(out=scb[:], in_=pb[:, c0:c0 + 130],
                                                                               func=AF.Copy),
                             R=[pb], W=[scb] + (SCX[id(scb)] or HSV))

                def summary_b(p):
                    summary_mm(p, 1)

                blks = []
                for p in (1, 2, 3):
                    base = (p - 1) * 8
                    ba = mk_blk([base + i for i in range(4)], 1, 64, False, 0)
                    bb = mk_blk([base + 4 + i for i in range(4)], 1, 64, False, 4)
                    blks += [bb, ba] if p == 1 else [ba, bb]
                blks.append(mk_blk([4, 5, 6, 7], 1, 64, True, 0))
                blks.append(mk_blk([8, 9, 10, 11], 1, 64, True, 4))
                for p in (1, 2, 3):
                    blks[2 * p]["pre_a"] = (lambda p=p: summary_pre(p))
                    blks[2 * p]["mid_a"] = (lambda p=p: summary_a(p))
                    blks[2 * p]["mid_b"] = (lambda p=p: summary_b(p))
                for j in range(4):
                    blks[j]["mods"] = [4 + 2 * j, 5 + 2 * j]
                def fold_states():
                    for d in range(2):
                        for h in range(4):
                            k.dma(SP, CST[d][h][:, 0:128], stC[d, h, :, :], W=[CST[d][h]])
                            with nc.allow_non_contiguous_dma(reason="state column"):
                                k.dma(SP, CST[d][h][:, 128:129], stn[d, h, :].rearrange("(p o) -> p o", o=1), W=[CST[d][h]])
                    stage(5.6)
                    k.op(DVE, lambda: dve.tensor_copy(out=MFO[:], in_=M0B[:]), R=[M0B], W=[MFO])
                    for d in range(2):
                        for p in ((1, 2, 3) if d == 0 else (3, 2, 1)):
                            fi = d * 3 + p - 1
                            dsl = slice(d * 4, (d + 1) * 4)
                            k.op(DVE, lambda: dve.scalar_tensor_tensor(out=FT[:, 0:4], in0=BSEG[:, p - 1, dsl],
                                                                       scalar=FLB[:, fi:fi + 1], in1=MFO[:, dsl],
                                                                       op0=ALU.mult, op1=ALU.subtract), R=[BSEG, FLB, MFO], W=[FT])
                            k.op(DVE, lambda: dve.tensor_scalar(out=FT[:, 0:4], in0=FT[:, 0:4], scalar1=-1.0, scalar2=None,
                                                                op0=ALU.mult), R=[FT], W=[FT])
                            k.op(DVE, lambda: dve.tensor_scalar(out=FT[:, 4:8], in0=MLOC[:, p - 1, dsl], scalar1=FLB[:, fi:fi + 1],
                                                                scalar2=NEGF[:, fi:fi + 1], op0=ALU.mult, op1=ALU.add),
                                 R=[MLOC, FLB, NEGF], W=[FT])
                            k.op(DVE, lambda: dve.tensor_tensor(out=MFO[:, dsl], in0=FT[:, 0:4], in1=FT[:, 4:8], op=ALU.max),
                                 R=[FT], W=[MFO])
                            for q in range(2):
                                k.op(DVE, lambda q=q: dve.tensor_tensor(out=FT[:, 8 + q * 4:12 + q * 4], in0=FT[:, q * 4:q * 4 + 4],
                                                                        in1=MFO[:, dsl], op=ALU.subtract), R=[FT, MFO], W=[FT])
                            k.op(ACT, lambda: act.activation(out=FT[:, 8:16], in_=FT[:, 8:16], func=AF.Exp), R=[FT], W=[FT])
                            for h in range(4):
                                k.op(DVE, lambda h=h: dve.tensor_scalar(out=CST[d][h][:, 0:130], in0=CST[d][h][:, 0:130],
                                                                        scalar1=FT[:, 8 + h:9 + h], scalar2=None, op0=ALU.mult),
                                     R=[CST[d][h], FT], W=[CST[d][h]])
                                k.op(DVE, lambda h=h: dve.scalar_tensor_tensor(out=CST[d][h][:, 0:130], in0=SC[p - 1][d][h][:],
                                                                               scalar=FT[:, 12 + h:13 + h], in1=CST[d][h][:, 0:130],
                                                                               op0=ALU.mult, op1=ALU.add),
                                     R=[SC[p - 1][d][h], FT, CST[d][h]] + (SCX[id(SC[p - 1][d][h])] or HSV), W=[CST[d][h]])

                blks[7]['mid_b'] = fold_states
                blks[0]["extras"] = prompt_extras
                run_blocks([prompt_blk] + blks)
                stage(5.5)
                stage(5.7)
                gates_unit(8, [list(range(8))], [[(MFO[:, 0:4], MFO), (MFO[:, 4:8], MFO)]])
                hs_written.clear()
                for w in range(8):
                    mlstm_wave([(w, 0, h) for h in range(4)] + [(7 - w, 1, h) for h in range(4)])
                mlstm_post_unit(list(range(8)), [4 + u for u in range(8)])

                stage(6)
                era1_tokens = k.all_tokens(k.bufs)

            GAB.w.update(era1_tokens)
            GAB.r.update(era1_tokens)

            def gate_bcast(off):
                for v in range(2):
                    pt = PS[0]
                    k.group([lambda: pe.transpose(pt[0:8, 0:128], MODP[:, off:off + 8, v], IDF())], R=[MODP, CF], W=[pt])
                    k.op(DVE, lambda: dve.tensor_copy(out=UTT[0:8, :], in_=pt[0:8, 0:128]), R=[pt], W=[UTT])
                    for hf in range(2):
                        pbk = PS[1 + hf]
                        fns = []
                        for c in range(4):
                            kk = hf * 4 + c
                            fns.append(lambda kk=kk, c=c: pe.matmul(pbk[:, c * 128:(c + 1) * 128],
                                                                    lhsT=OH[0:8, kk, :],
                                                                    rhs=UTT[0:8, :], start=True, stop=True))
                        k.group(fns, R=[OH, UTT], W=[pbk])
                        k.op(DVE, lambda hf=hf, v=v, pbk=pbk: dve.tensor_copy(out=GAB[:, v, hf * 512:(hf + 1) * 512],
                                                                             in_=pbk[:]), R=[pbk], W=[GAB])

            esx = ExitStack()
            with esx:
                def nbx(es, name, shape, dt, toks):
                    b = k.sb(es, name, shape, dt)
                    b.w.update(toks)
                    b.r.update(toks)
                    return b
                X = [nbx(esx, f"X{t}", [128, D], F32, era1_tokens) for t in range(NT)]
                esc = ExitStack()
                with esc:
                    WOUT = nbx(esc, "WOUT", [128, 8, D], BF16, era1_tokens)
                    TMPCS = [nbx(esc, f"TMPC{i}", [128, 512], F32, era1_tokens) for i in range(2)]
                    wout_view = w_out.rearrange("(k p) n -> p k n", p=128)
                    for kk in range(8):
                        k.dma(POOL, WOUT[:, kk, :], wout_view[:, kk, :], W=[WOUT])
                    for t in range(NT):
                        src = xp[t * 128:(t + 1) * 128, :] if t < NT_P else xs[(t - NT_P) * 128:(t - NT_P + 1) * 128, :]
                        k.dma(POOL, X[t][:], src, W=[X[t]])
                    gate_bcast(16)
                    crot = [0]
                    for t in range(NT):
                        v = 0 if t < NT_P else 1
                        for hf in range(2):
                            pb = PS[crot[0] % 8]
                            crot[0] += 1
                            fns = []
                            for kc in range(8):
                                src = AOT if kc < 4 else BOT
                                fns.append(lambda kc=kc, src=src: pe.matmul(pb[:], lhsT=src[:, t, kc % 4, :],
                                                                            rhs=WOUT[:, kc, hf * 512:(hf + 1) * 512],
                                                                            start=(kc == 0), stop=(kc == 7)))
                            k.group(fns, R=[AOT, BOT, WOUT], W=[pb])
                            TMPC = TMPCS[crot[0] % 2]
                            k.op(DVE, lambda: dve.tensor_tensor(out=TMPC[:], in0=pb[:], in1=GAB[:, v, hf * 512:(hf + 1) * 512],
                                                                op=ALU.mult), R=[pb, GAB], W=[TMPC])
                            eng, hh = (DVE, dve) if hf == 0 else (POOL, pool)
                            k.op(eng, lambda hh=hh: hh.tensor_tensor(out=X[t][:, hf * 512:(hf + 1) * 512],
                                                                     in0=X[t][:, hf * 512:(hf + 1) * 512], in1=TMPC[:],
                                                                     op=ALU.add),
                                 R=[X[t], TMPC], W=[X[t]])
                    gate_bcast(40)
                    era_tokens = k.all_tokens(k.bufs)

                stage(7)
                es2 = ExitStack()
                es2.__enter__()

                def nb(name, shape, dt):
                    return nbx(es2, name, shape, dt, era_tokens)
                MB = 6 * 128
                W2 = nb("W2", [128, NFF, D], BF16)
                GTF = nb("GTF", [128, NFF, MB], BF16)
                GFB = nb("GFB", [128, D], F32)
                k.dma(SP, GFB[:], g_final.partition_broadcast(128), W=[GFB])
                aflat = AOT.t[:].rearrange("p t c d -> p (t c d)")
                bflat = BOT.t[:].rearrange("p t c d -> p (t c d)")
                H2 = k.view("H2", aflat.rearrange("p (k n) -> p k n", n=MB), era_tokens)
                W13 = [k.view(f"W13_{i}", bflat[:, i * 2048:(i + 1) * 2048].rearrange("p (w k n) -> p w k n", w=2, k=8),
                              era_tokens) for i in range(3)]
                SA = [nb(f"SA{i}", [128, 384], BF16) for i in range(2)]
                TM2 = nb("TM2", [128, 512], F32)
                OUT = [nb(f"OUT{i}", [128, D], F32) for i in range(2)]
                w1v = w1.rearrange("(k p) n -> p k n", p=128)
                w3v = w3.rearrange("(k p) n -> p k n", p=128)
                w2v = w2.rearrange("(f p) n -> p f n", p=128)
                orot = [0]
                for mb in range(2):
                    tiles = list(range(mb * 6, mb * 6 + 6))
                    for i, t in enumerate(tiles):
                        v = 0 if t < NT_P else 1
                        ssl = 24 + i
                        k.op(ACT, lambda t=t, ssl=ssl: act.activation(out=JUNK[:], in_=X[t][:], func=AF.Square,
                                                                      accum_out=SS[:, ssl:ssl + 1]), R=[X[t]], W=[JUNK, SS])
                        rstd_of(SS[:, ssl:ssl + 1], RS[:, ssl:ssl + 1], 1, 1.0 / D)
                        xn = XN[i % 4]
                        k.op(DVE, lambda t=t, ssl=ssl, xn=xn: dve.tensor_scalar(out=xn[:], in0=X[t][:], scalar1=RS[:, ssl:ssl + 1],
                                                                               scalar2=None, op0=ALU.mult),
                             R=[X[t], RS], W=[xn])
                        pb = PS[i % 2]
                        pbv = pb[:].bitcast(BF16)
                        k.group([lambda kk=kk, xn=xn, pbv=pbv: pe.transpose(pbv[:, kk * 128:(kk + 1) * 128],
                                                                           xn[:, kk * 128:(kk + 1) * 128], IDB)
                                 for kk in range(8)], R=[xn, CB16], W=[pb])
                        for kk in range(8):
                            k.op(DVE, lambda kk=kk, i=i, v=v, pbv=pbv: dve.tensor_scalar(
                                out=H2[:, kk, i * 128:(i + 1) * 128], in0=pbv[:, kk * 128:(kk + 1) * 128],
                                scalar1=S2[:, kk, v:v + 1], scalar2=MODP[:, 24 + kk, v:v + 1], op0=ALU.mult, op1=ALU.add),
                                R=[pb, S2, MODP], W=[H2])
                    for f in range(NFF):
                        slot = W13[f % 3]
                        k.dma(POOL, slot[:, 0, :, :], w1v[:, :, f * 128:(f + 1) * 128], W=[slot])
                        k.dma(POOL, slot[:, 1, :, :], w3v[:, :, f * 128:(f + 1) * 128], W=[slot])
                        if mb == 0:
                            k.dma(POOL, W2[:, f, :], w2v[:, f, :], W=[W2])
                        for hf in range(2):
                            pa = PS[2 + (f * 4 + hf * 2) % 6]
                            pbb = PS[2 + (f * 4 + hf * 2 + 1) % 6]
                            for (pp, wi) in ((pa, 0), (pbb, 1)):
                                k.group([lambda kk=kk, pp=pp, wi=wi: pe.matmul(pp[:, 0:384], lhsT=slot[:, wi, kk, :],
                                                                               rhs=H2[:, kk, hf * 384:(hf + 1) * 384],
                                                                               start=(kk == 0), stop=(kk == 7))
                                         for kk in range(8)], R=[slot, H2], W=[pp])
                            sa = SA[hf]
                            k.op(ACT, lambda pa=pa, sa=sa: act.activation(out=sa[:], in_=pa[:, 0:384], func=AF.Silu),
                                 R=[pa], W=[sa])
                            k.op(DVE, lambda pbb=pbb, sa=sa, f=f, hf=hf: dve.tensor_tensor(
                                out=GTF[:, f, hf * 384:(hf + 1) * 384], in0=pbb[:, 0:384], in1=sa[:], op=ALU.mult),
                                R=[pbb, sa], W=[GTF])
                    for i, t in enumerate(tiles):
                        v = 0 if t < NT_P else 1
                        for hf in range(2):
                            pb = PS[(i * 2 + hf) % 8]
                            k.group([lambda f=f, pb=pb: pe.matmul(pb[:], lhsT=GTF[:, f, i * 128:(i + 1) * 128],
                                                                  rhs=W2[:, f, hf * 512:(hf + 1) * 512],
                                                                  start=(f == 0), stop=(f == NFF - 1)) for f in range(NFF)],
                                    R=[GTF, W2], W=[pb])
                            k.op(DVE, lambda pb=pb, v=v, hf=hf: dve.tensor_tensor(out=TM2[:], in0=pb[:],
                                                                                 in1=GAB[:, v, hf * 512:(hf + 1) * 512],
                                                                                 op=ALU.mult), R=[pb, GAB], W=[TM2])
                            k.op(POOL, lambda t=t, hf=hf: pool.tensor_tensor(out=X[t][:, hf * 512:(hf + 1) * 512],
                                                                             in0=X[t][:, hf * 512:(hf + 1) * 512],
                                                                             in1=TM2[:], op=ALU.add),
                                 R=[X[t], TM2], W=[X[t]])
                        ssl = 32 + i
                        k.op(ACT, lambda t=t, ssl=ssl: act.activation(out=JUNK[:], in_=X[t][:], func=AF.Square,
                                                                      accum_out=SS[:, ssl:ssl + 1]), R=[X[t]], W=[JUNK, SS])
                        rstd_of(SS[:, ssl:ssl + 1], RS[:, ssl:ssl + 1], 1, 1.0 / D)
                        ob = OUT[orot[0] % 2]
                        orot[0] += 1
                        k.op(DVE, lambda t=t, ssl=ssl, ob=ob: dve.scalar_tensor_tensor(
                            out=ob[:], in0=X[t][:], scalar=RS[:, ssl:ssl + 1], in1=GFB[:], op0=ALU.mult, op1=ALU.mult),
                            R=[X[t], RS, GFB], W=[ob])
                        dst = yp[t * 128:(t + 1) * 128, :] if t < NT_P else ys[(t - NT_P) * 128:(t - NT_P + 1) * 128, :]
                        k.dma(SP, dst, ob[:], R=[ob], final=True)

                es2.__exit__(None, None, None)
        except StageExit:
            pass
        k.finish()
    return nc


_NC_CACHE = {}


def make_consts():
    c = np.zeros((128, 512), np.float32)
    c[:, 0:128] = np.eye(128, dtype=np.float32)
    s = np.arange(128)[:, None]
    j = np.arange(128)[None, :]
    c[:, 128:256] = (s <= j).astype(np.float32)
    c[:, 256:384] = (s >= j).astype(np.float32)
    c[:, 384:512] = 1.0
    return c


def kernel(x_prompt, x_sample, state_C, state_n, state_m, c, c_ctx, w_ada, b_ada, g_norm1,
           w_in, b_gate, w_s, b_s, g_v, conv_w, conv_b, g_h, w_out, g_norm2, w1, w3, w2, g_final):
    f = lambda a: np.ascontiguousarray(np.asarray(a, dtype=np.float32))
    x_prompt, x_sample = f(x_prompt), f(x_sample)
    state_C, state_n, state_m, c, c_ctx = f(state_C), f(state_n), f(state_m), f(c), f(c_ctx)
    if "nc" not in _NC_CACHE:
        _NC_CACHE["nc"] = build_nc()
    nc = _NC_CACHE["nc"]
    consts = make_consts()
    shared = {
        "consts": consts, "w_ada": f(w_ada)[0], "b_ada": f(b_ada)[0], "g_norm1": f(g_norm1)[0], "w_in": f(w_in)[0],
        "b_gate": f(b_gate)[0].reshape(1, 16), "w_s": f(w_s)[0], "b_s": f(b_s)[0].reshape(1, 512),
        "g_v": f(g_v)[0].reshape(1, 512), "conv_w": f(conv_w)[0], "conv_b": f(conv_b)[0],
        "g_h": f(g_h)[0].reshape(1, 512), "w_out": f(w_out)[0], "g_norm2": f(g_norm2)[0], "w1": f(w1)[0],
        "w3": f(w3)[0], "w2": f(w2)[0], "g_final": f(g_final).reshape(1, D),
    }
    in_maps = []
    for core in range(8):
        b, j = core // 4, core % 4
        m = dict(shared)
        m["xp"] = x_prompt[2 * core:2 * core + 2].reshape(512, D)
        m["xs"] = x_sample[b, j * 1024:(j + 1) * 1024]
        m["xf"] = np.concatenate([x_sample[b, ((j + p) % 4) * 1024:((j + p) % 4 + 1) * 1024] for p in (1, 2, 3)], 0)
        m["cvec"] = np.stack([c_ctx, c[b]], 0)
        m["stC"] = state_C[b, 0]
        m["stn"] = state_n[b, 0]
        m["stm"] = state_m[b, 0].reshape(1, 8)
        fl = np.zeros((1, 6), np.float32)
        for p in (1, 2, 3):
            fl[0, p - 1] = 1.0 if p >= 4 - j else 0.0
            fl[0, 3 + p - 1] = 1.0 if p <= 3 - j else 0.0
        m["flg"] = fl
        in_maps.append({kk: np.ascontiguousarray(vv) for kk, vv in m.items()})
    res = run_bass_kernel_spmd(nc, in_maps, core_ids=list(range(8)))
    B = x_prompt.shape[0]
    y_prompt = np.zeros((B, 256, D), np.float32)
    y_sample = np.zeros((2, 4096, D), np.float32)
    new_C = np.zeros((B, 1, 2, 4, 128, 128), np.float32)
    new_n = np.zeros((B, 1, 2, 4, 128), np.float32)
    new_m = np.zeros((B, 1, 2, 4), np.float32)
    for core in range(8):
        r = res.results[core]
        b, j = core // 4, core % 4
        y_prompt[2 * core:2 * core + 2] = r["yp"].reshape(2, 256, D)
        y_sample[b, j * 1024:(j + 1) * 1024] = r["ys"]
        new_C[2 * core:2 * core + 2, 0] = r["nC"]
        new_n[2 * core:2 * core + 2, 0] = r["nn"]
        new_m[2 * core:2 * core + 2, 0] = r["nm"].reshape(2, 2, 4)
    return (y_prompt, y_sample, new_C, new_n, new_m)
```

```python
import numpy as np
from contextlib import ExitStack
import concourse.bass as bass
import concourse.mybir as mybir
from concourse.bass_utils import run_bass_kernel_spmd

F32 = mybir.dt.float32
BF16 = mybir.dt.bfloat16
AF = mybir.ActivationFunctionType
ALU = mybir.AluOpType
AX = mybir.AxisListType

D = 1024
DIN = 3088
DFF = 2816
NFF = DFF // 128
OFF_U, OFF_V, OFF_Q, OFF_K, OFF_VV, OFF_O, OFF_G = 0, 512, 1024, 1536, 2048, 2560, 3072
EPS = 1e-6
NEG = -1e30
LNS = float(-0.5 * np.log(128.0))
NT_P = 4
NT_S = 8
NT = NT_P + NT_S
NFOR = 24
import os
STAGE = float(os.environ.get("KSTAGE", "99"))


class StageExit(Exception):
    pass


DEAD = [False]
STRICT = False


def stage(n):
    if STAGE <= n:
        DEAD[0] = True


class Eng:
    def __init__(self, h, name):
        self.h = h
        self.name = name
        self.sem = None
        self.n = 0
        self.waited = {}


class Buf:
    def __init__(self, name, t):
        self.name = name
        self.t = t
        self.w = {}
        self.r = {}
        self.sem = None
        self.nd = 0
        self.psum = False

    def __getitem__(self, idx):
        return self.t[idx]


class K:
    def __init__(self, nc, es):
        self.nc = nc
        self.es = es
        self.PE = Eng(nc.tensor, "pe")
        self.ACT = Eng(nc.scalar, "act")
        self.DVE = Eng(nc.vector, "dve")
        self.POOL = Eng(nc.gpsimd, "pool")
        self.SP = Eng(nc.sync, "sp")
        self.engs = [self.PE, self.ACT, self.DVE, self.POOL, self.SP]
        for e in self.engs:
            e.sem = es.enter_context(nc.semaphore("s_" + e.name))
        self.nsem = 5
        self.final = []
        self.bufs = []

    def sb(self, es, name, shape, dt):
        b = Buf(name, es.enter_context(self.nc.sbuf_tensor(name, shape, dt)))
        self.bufs.append(b)
        return b

    def ps(self, es, name, shape, dt):
        b = Buf(name, es.enter_context(self.nc.psum_tensor(name, shape, dt)))
        b.psum = True
        self.bufs.append(b)
        return b

    def _need(self, e, toks):
        for key, (sem, val) in toks:
            if e.waited.get(id(sem), 0) >= val:
                continue
            e.h.wait_ge(sem, val)
            e.waited[id(sem)] = val

    def _deps(self, e, R, W):
        toks = []
        for b in R:
            for key, tok in b.w.items():
                if key == e.name and e is self.PE:
                    continue
                toks.append((key, tok))
            if b.psum:
                for key, tok in b.r.items():
                    if key != e.name:
                        toks.append((key, tok))
        for b in W:
            for key, tok in b.w.items():
                if key == e.name and (e is self.PE or not STRICT):
                    continue
                toks.append((key, tok))
            for key, tok in b.r.items():
                if key == e.name and (e is self.PE or not STRICT):
                    continue
                toks.append((key, tok))
        return toks

    def op(self, e, fn, R=(), W=()):
        if DEAD[0]:
            return None
        pend = []
        for key, (sem, val) in self._deps(e, R, W):
            if e.waited.get(id(sem), 0) >= val:
                continue
            e.waited[id(sem)] = val
            pend = [p for p in pend if p[0] is not sem] + [(sem, val)]
        for sem, val in pend[:-1]:
            e.h.wait_ge(sem, val)
        inst = fn()
        if pend:
            inst._wait_ge(pend[-1][0], pend[-1][1])
        inst.then_inc(e.sem, 1)
        e.n += 1
        tok = (e.sem, e.n)
        for b in R:
            b.r[e.name] = tok
        for b in W:
            b.w[e.name] = tok
        return tok

    def group(self, fns, R=(), W=()):
        e = self.PE
        if DEAD[0]:
            return None
        pend = []
        for key, (sem, val) in self._deps(e, R, W):
            if e.waited.get(id(sem), 0) >= val:
                continue
            e.waited[id(sem)] = val
            pend = [p for p in pend if p[0] is not sem] + [(sem, val)]
        for sem, val in pend[:-1]:
            e.h.wait_ge(sem, val)
        inst = None
        for n_, fn in enumerate(fns):
            inst = fn()
            if n_ == 0 and pend:
                inst._wait_ge(pend[-1][0], pend[-1][1])
        inst.then_inc(e.sem, 1)
        e.n += 1
        tok = (e.sem, e.n)
        for b in R:
            b.r[e.name] = tok
        for b in W:
            b.w[e.name] = tok
        return tok

    def dma(self, q, out, in_, R=(), W=(), final=False, **kw):
        owner = W[0] if len(W) else R[0]
        if DEAD[0]:
            return None
        if owner.sem is None:
            owner.sem = self.es.enter_context(self.nc.semaphore("d_" + owner.name))
            self.nsem += 1
        toks = []
        okey = "dma:" + owner.name
        for b in R:
            toks += list(b.w.items())
        for b in W:
            toks += [kv for kv in b.w.items() if kv[0] != okey] + list(b.r.items())
        pend = []
        for key_, (sem_, val_) in toks:
            if q.waited.get(id(sem_), 0) >= val_:
                continue
            q.waited[id(sem_)] = val_
            pend = [p for p in pend if p[0] is not sem_] + [(sem_, val_)]
        for sem_, val_ in pend[:-1]:
            q.h.wait_ge(sem_, val_)
        inst = q.h.dma_start(out=out, in_=in_, **kw)
        if pend:
            inst._wait_ge(pend[-1][0], pend[-1][1])
        inst.then_inc(owner.sem, 16)
        owner.nd += 1
        tok = (owner.sem, 16 * owner.nd)
        key = "dma:" + owner.name
        for b in R:
            b.r[key] = tok
        for b in W:
            b.w[key] = tok
        if final:
            self.final.append((key, tok))
        return tok

    def view(self, name, ap, toks=None):
        b = Buf(name, ap)
        if toks:
            b.w.update(toks)
            b.r.update(toks)
        self.bufs.append(b)
        return b

    def all_tokens(self, bufs):
        toks = {}
        for e in self.engs:
            if e.n:
                toks[e.name] = (e.sem, e.n)
        for b in bufs:
            for d in (b.w, b.r):
                for key, tok in d.items():
                    if key.startswith("dma:"):
                        toks[key] = tok
        return toks

    def finish(self):
        self._need(self.SP, self.final)
        toks = [(e.name, (e.sem, e.n)) for e in self.engs if e.n and e is not self.SP]
        self._need(self.SP, toks)


def build_nc(dbg_names=()):
    nc = bass.Bass("TRN2", target_bir_lowering=False)
    DEAD[0] = False

    def din(name, shape):
        return nc.dram_tensor(name, list(shape), F32, kind="ExternalInput").ap()

    def dout(name, shape):
        return nc.dram_tensor(name, list(shape), F32, kind="ExternalOutput").ap()

    xp = din("xp", [NT_P * 128, D])
    xs = din("xs", [NT_S * 128, D])
    xf = din("xf", [NFOR * 128, D])
    cvec = din("cvec", [2, D])
    stC = din("stC", [2, 4, 128, 128])
    stn = din("stn", [2, 4, 128])
    stm = din("stm", [1, 8])
    flg = din("flg", [1, 6])
    consts = din("consts", [128, 512])
    w_ada = din("w_ada", [D, 6 * D])
    b_ada = din("b_ada", [6 * D])
    g_norm1 = din("g_norm1", [D])
    w_in = din("w_in", [D, DIN])
    b_gate = din("b_gate", [1, 16])
    w_s = din("w_s", [4, 128, 128])
    b_s = din("b_s", [1, 512])
    g_v = din("g_v", [1, 512])
    conv_w = din("conv_w", [3, 1024])
    conv_b = din("conv_b", [1024])
    g_h = din("g_h", [1, 512])
    w_out = din("w_out", [D, D])
    g_norm2 = din("g_norm2", [D])
    w1 = din("w1", [D, DFF])
    w3 = din("w3", [D, DFF])
    w2 = din("w2", [DFF, D])
    g_final = din("g_final", [1, D])

    yp = dout("yp", [NT_P * 128, D])
    ys = dout("ys", [NT_S * 128, D])
    nC = dout("nC", [2, 2, 4, 128, 128])
    nn = dout("nn", [2, 2, 4, 128])
    nm = dout("nm", [2, 8])
    dbg = {}

    es0 = ExitStack()
    with es0:
        k = K(nc, es0)
        PE, ACT, DVE, POOL, SP = k.PE, k.ACT, k.DVE, k.POOL, k.SP
        pe, act, dve, pool, sp = nc.tensor, nc.scalar, nc.vector, nc.gpsimd, nc.sync

        AOT = k.sb(es0, "AOT", [128, NT, 4, 128], BF16)
        BOT = k.sb(es0, "BOT", [128, NT, 4, 128], BF16)
        CF = k.sb(es0, "CF", [128, 512], F32)
        CB16 = k.sb(es0, "CB16", [128, 512], BF16)
        MODP = k.sb(es0, "MODP", [128, 48, 2], F32)
        S1 = k.sb(es0, "S1", [128, 8, 2], F32)
        S2 = k.sb(es0, "S2", [128, 8, 2], F32)
        G1 = k.sb(es0, "G1", [128, 8], F32)
        G2 = k.sb(es0, "G2", [128, 8], F32)
        GAB = k.sb(es0, "GAB", [128, 2, D], F32)
        SS = k.sb(es0, "SS", [128, 64], F32)
        RS = k.sb(es0, "RS", [128, 64], F32)
        MHALF = k.sb(es0, "MHALF", [128, 32], F32)
        OH = k.sb(es0, "OH", [8, 8, 128], F32)
        UTT = k.sb(es0, "UTT", [64, 128], F32)
        JUNK = k.sb(es0, "JUNK", [128, D], BF16)
        XN = [k.sb(es0, f"XN{i}", [128, D], BF16) for i in range(4)]
        PS = [k.ps(es0, f"PS{i}", [128, 512], F32) for i in range(8)]

        IDF = lambda n=128: CF[0:n, 0:n]
        TRIF = CF[:, 128:256]
        TRIB = CF[:, 256:384]
        ONESF = CF[:, 384:512]
        IDB = CB16[:, 0:128]
        MASK = [CB16[:, 128:256], CB16[:, 256:384]]
        ONESB = CB16[:, 384:512]

        k.dma(SP, CF[:], consts, W=[CF])
        k.dma(POOL, CB16[:], consts, W=[CB16])
        k.op(DVE, lambda: dve.memset(MHALF[:], -0.5), W=[MHALF])

        def rstd_of(ss_ap, out_ap, n, inv):
            k.op(POOL, lambda: pool.tensor_scalar(out=out_ap, in0=ss_ap, scalar1=inv, scalar2=EPS,
                                                  op0=ALU.mult, op1=ALU.add), R=[SS], W=[RS])
            k.op(POOL, lambda: pool.tensor_tensor(out=out_ap, in0=out_ap, in1=MHALF[:, 0:n], op=ALU.pow),
                 R=[RS, MHALF], W=[RS])

        try:
            es1 = ExitStack()
            with es1:
                WIN = k.sb(es1, "WIN", [128, 8, DIN], BF16)
                HT = [k.sb(es1, f"HT{i}", [128, 8, 512], BF16) for i in range(2)]
                XS = [k.view(f"XS{i}", GAB.t[:, i, :]) for i in range(2)]
                QT = k.sb(es1, "QT", [128, 4, NT_S * 128], BF16)
                KT = k.sb(es1, "KT", [128, 4, NT_S * 128], BF16)
                KTOK = k.sb(es1, "KTOK", [128, NT_S, 512], BF16)
                VV = k.sb(es1, "VV", [128, NT_S, 4, 132], BF16)
                TH = k.sb(es1, "TH", [128, NT_S, 512], BF16)
                HS = k.sb(es1, "HS", [128, NT_S, 512], F32)
                GUT = k.sb(es1, "GUT", [128, 4, 512], BF16)
                GV = k.sb(es1, "GV", [128, 512], F32)
                VGS = [k.sb(es1, f"VG{i}", [128, 512], BF16) for i in range(2)]
                PRE = [k.sb(es1, f"PRE{i}", [128, 512], F32) for i in range(3)]
                GT = k.sb(es1, "GT", [128, NT_S, 16], F32)
                SPT = k.sb(es1, "SPT", [128, NT_S * 8], F32)
                UT = k.sb(es1, "UT", [128, NT_S * 8], F32)
                CBT = k.sb(es1, "CBT", [128, NT_S * 8], F32)
                EE = k.sb(es1, "EE", [128, NT_S * 8], F32)
                THR = k.sb(es1, "THR", [128, NT_S * 8], F32)
                UMB = k.sb(es1, "UMB", [128, NT_S * 8], F32)
                BLB = k.sb(es1, "BLB", [128, NT_S * 8], F32)
                RB = k.sb(es1, "RB", [128, NT_S * 8], F32)
                MPV = k.sb(es1, "MPV", [128, NT_S * 8], F32)
                DCY = k.sb(es1, "DCY", [128, NT_S * 8], F32)
                MCUR = k.sb(es1, "MCUR", [128, 8], F32)
                UMX = k.sb(es1, "UMX", [64, 1], F32)
                DG = k.sb(es1, "DG", [64, 64], F32)
                CST = [[k.sb(es1, f"CST{d}{h}", [128, 132], F32) for h in range(4)] for d in range(2)]
                CBF = [k.sb(es1, f"CBF{i}", [128, 132], BF16) for i in range(8)]
                VP = [k.sb(es1, f"VP{i}", [128, 132], BF16) for i in range(8)]
                ST = [k.sb(es1, f"ST{i}", [128, 128], BF16) for i in range(8)]
                DD = k.sb(es1, "DD", [128, 32], F32)
                WST = k.sb(es1, "WST", [128, 4, 128], BF16)
                BSR = k.sb(es1, "BSR", [1, 512], BF16)
                GVB = k.sb(es1, "GVB", [128, 512], F32)
                GHB = k.sb(es1, "GHB", [128, 512], F32)
                BGB = k.sb(es1, "BGB", [128, 16], F32)
                CW = k.sb(es1, "CW", [128, 8, 3], F32)
                CBI = k.sb(es1, "CBI", [128, 8], F32)
                BADA = k.sb(es1, "BADA", [128, 48], F32)
                CT = k.sb(es1, "CT", [128, 8, 2], F32)
                SCT = k.sb(es1, "SCT", [128, 8, 2], BF16)
                WAB = [AOT, BOT]
                M0B = k.sb(es1, "M0B", [128, 8], F32)
                FLB = k.sb(es1, "FLB", [128, 6], F32)
                MISC = k.sb(es1, "MISC", [128, 16], F32)
                HTH = {id(HT[i]): [k.view(f"HT{i}lo", HT[i].t[:, 0:4, :]), k.view(f"HT{i}hi", HT[i].t[:, 4:8, :])]
                       for i in range(2)}
                KTv = [k.view(f"KT{i}", KT.t[:, :, i * 512:(i + 1) * 512]) for i in range(2)]
                QTv = [k.view(f"QT{i}", QT.t[:, :, i * 512:(i + 1) * 512]) for i in range(2)]
                KTOKv = [k.view(f"KTOKh{i}", KTOK.t[:, i * 4:(i + 1) * 4, :]) for i in range(2)]
                VVv = [k.view(f"VVh{i}", VV.t[:, i * 4:(i + 1) * 4, :, :]) for i in range(2)]
                GTv = [k.view(f"GTh{i}", GT.t[:, i * 4:(i + 1) * 4, :]) for i in range(2)]
                NEGT = k.sb(es1, "NEGT", [128, 4], F32)
                NEGF = k.sb(es1, "NEGF", [128, 6], F32)
                MLOC = k.sb(es1, "MLOC", [128, 3, 8], F32)
                BSEG = k.sb(es1, "BSEG", [128, 3, 8], F32)
                MFO = k.sb(es1, "MFO", [128, 8], F32)
                OFS = k.sb(es1, "OFS", [128, 64], F32)
                RSEG = k.sb(es1, "RSEG", [128, 8], F32)
                FT = k.sb(es1, "FT", [128, 16], F32)
                _botf = BOT.t[:, 4:12, :, :].rearrange("p t c d -> p (t c d)").bitcast(F32)
                _hsf = HS.t[:].rearrange("p t c -> p (t c)")
                SC = [[[None] * 4 for d in range(2)] for p in range(3)]
                SCX = {}
                for _p in range(3):
                    for _d in range(2):
                        for _h in range(4):
                            _n = _p * 8 + _d * 4 + _h
                            if _n < 15:
                                _b = k.view(f"SC{_p}{_d}{_h}", _botf[:, _n * 130:(_n + 1) * 130])
                                SCX[id(_b)] = [BOT]
                            else:
                                _o = 2112 + (_n - 15) * 130
                                _b = k.view(f"SC{_p}{_d}{_h}", _hsf[:, _o:_o + 130])
                                SCX[id(_b)] = None
                            SC[_p][_d][_h] = _b

                with nc.allow_non_contiguous_dma(reason="small param relayout"):
                    for v in range(2):
                        k.dma(SP, CT[:, :, v], cvec[v, :].rearrange("(k p) -> p k", p=128), W=[CT])
                    k.dma(SP, BADA[:], b_ada.rearrange("(c p) -> p c", p=128), W=[BADA])
                    k.dma(SP, G1[:], g_norm1.rearrange("(k p) -> p k", p=128), W=[G1])
                    k.dma(SP, G2[:], g_norm2.rearrange("(k p) -> p k", p=128), W=[G2])
                    for j in range(3):
                        k.dma(SP, CW[:, :, j], conv_w[j, :].rearrange("(k p) -> p k", p=128), W=[CW])
                    k.dma(SP, CBI[:], conv_b.rearrange("(k p) -> p k", p=128), W=[CBI])

                k.dma(SP, GVB[:], g_v.partition_broadcast(128), W=[GVB])
                k.dma(SP, GHB[:], g_h.partition_broadcast(128), W=[GHB])
                k.dma(SP, BGB[:], b_gate.partition_broadcast(128), W=[BGB])
                k.dma(SP, M0B[:], stm.partition_broadcast(128), W=[M0B])
                k.dma(SP, FLB[:], flg.partition_broadcast(128), W=[FLB])
                k.dma(POOL, BSR[:], b_s, W=[BSR])
                for g in range(4):
                    k.dma(SP, PRE[0][:, g * 128:(g + 1) * 128], w_s[g, :, :], W=[PRE[0]])
                k.group([lambda g=g: pe.transpose(PS[7][:, g * 128:(g + 1) * 128], PRE[0][:, g * 128:(g + 1) * 128], IDF())
                         for g in range(4)], R=[PRE[0], CF], W=[PS[7]])
                k.op(DVE, lambda: dve.tensor_copy(out=WST[:], in_=PS[7][:].rearrange("p (g t) -> p g t", t=128)),
                     R=[PS[7]], W=[WST])
                for kk in range(8):
                    k.op(DVE, lambda kk=kk: dve.tensor_scalar(out=OH[0:8, kk, :], in0=CF[0:8, 384:512],
                                                              scalar1=CF[0:8, kk:kk + 1], scalar2=None, op0=ALU.mult),
                         R=[CF], W=[OH])
                k.op(DVE, lambda: dve.tensor_scalar(out=GHB[:], in0=GHB[:], scalar1=0.5, scalar2=None, op0=ALU.mult),
                     R=[GHB], W=[GHB])
                for d in range(2):
                    for h in range(4):
                        k.op(POOL, lambda d=d, h=h: pool.memset(CST[d][h][:], 0.0), W=[CST[d][h]])
                k.op(POOL, lambda: pool.memset(VV[:, :, :, 128:132], 1.0), W=VVv)

                k.op(ACT, lambda: act.activation(out=SCT[:], in_=CT[:], func=AF.Silu), R=[CT], W=[SCT])
                wa_view = w_ada.rearrange("(k p) n -> p k n", p=128)

                def mod_block(cb):
                    slot = WAB[cb % 2]
                    sv = slot[:, 4:12, :, :].rearrange("p t c d -> p t (c d)")
                    k.dma(POOL, sv, wa_view[:, :, cb * 512:(cb + 1) * 512], W=[slot])
                    for i in range(4):
                        ch = cb * 4 + i
                        k.group([lambda kk=kk, i=i, ch=ch: pe.matmul(PS[0][:, ch * 2:ch * 2 + 2],
                                                                      lhsT=sv[:, kk, i * 128:(i + 1) * 128],
                                                                      rhs=SCT[:, kk, :], start=(kk == 0), stop=(kk == 7))
                                 for kk in range(8)], R=[slot, SCT], W=[PS[0]])

                def mod_load(cb):
                    sv = AOT[:, 4:12, :, :].rearrange("p t c d -> p t (c d)")
                    k.dma(POOL, sv, wa_view[:, :, cb * 512:(cb + 1) * 512], W=[AOT])

                def mod_compute(cb):
                    sv = AOT[:, 4:12, :, :].rearrange("p t c d -> p t (c d)")
                    pbm = PS[6]
                    for i in range(4):
                        k.group([lambda kk=kk, i=i: pe.matmul(pbm[:, i * 2:i * 2 + 2], lhsT=sv[:, kk, i * 128:(i + 1) * 128],
                                                              rhs=SCT[:, kk, :], start=(kk == 0), stop=(kk == 7))
                                 for kk in range(8)], R=[AOT, SCT], W=[pbm])
                    for v in range(2):
                        k.op(DVE, lambda v=v: dve.tensor_tensor(
                            out=MODP[:, cb * 4:cb * 4 + 4, v], in0=pbm[:, 0:8].rearrange("p (c v) -> p c v", v=2)[:, :, v],
                            in1=BADA[:, cb * 4:cb * 4 + 4], op=ALU.add), R=[pbm, BADA], W=[MODP])
                    if cb + 1 < 12:
                        mod_load(cb + 1)
                    if cb == 11:
                        for v in range(2):
                            k.op(DVE, lambda v=v: dve.scalar_tensor_tensor(out=S2[:, :, v], in0=MODP[:, 32:40, v], scalar=1.0,
                                                                           in1=G2[:], op0=ALU.add, op1=ALU.mult),
                                 R=[MODP, G2], W=[S2])

                def mod_finish(c0, c1):
                    for v in range(2):
                        k.op(DVE, lambda v=v: dve.tensor_tensor(
                            out=MODP[:, c0:c1, v], in0=PS[0][:, 2 * c0:2 * c1].rearrange("p (c v) -> p c v", v=2)[:, :, v],
                            in1=BADA[:, c0:c1], op=ALU.add), R=[PS[0], BADA], W=[MODP])

                for cb in range(4):
                    mod_block(cb)
                mod_finish(0, 16)
                for v in range(2):
                    k.op(DVE, lambda v=v: dve.scalar_tensor_tensor(out=S1[:, :, v], in0=MODP[:, 8:16, v], scalar=1.0,
                                                                   in1=G1[:], op0=ALU.add, op1=ALU.mult),
                         R=[MODP, G1], W=[S1])

                stage(1)
                win_view = w_in.rearrange("(k p) n -> p k n", p=128)
                for kk in range(8):
                    pass
                WING = {}
                for (g0, g1) in ((OFF_U, OFF_V), (OFF_Q, OFF_K), (OFF_K, OFF_VV), (OFF_VV, OFF_O), (OFF_G, DIN),
                                 (OFF_O, OFF_G), (OFF_V, OFF_Q)):
                    gb = k.view(f"WIN_{g0}", WIN.t[:, :, g0:g1])
                    for c0 in range(g0, g1, 128):
                        WING[c0] = gb
                    k.dma(POOL, WIN[:, :, g0:g1], win_view[:, :, g0:g1], W=[gb])

                ps_rot = {"proj": 0, "small": 0, "tr": 0}

                def bank(kind):
                    if kind == "tr":
                        b = PS[7 - ps_rot["tr"] % 2]
                    elif kind == "proj":
                        b = PS[2 + ps_rot["proj"] % 4]
                    else:
                        b = PS[6 + ps_rot["small"] % 2]
                    ps_rot[kind] += 1
                    return b

                xn_rot = [0]

                def norm_pre(xbuf, x_ap, ssl):
                    k.op(ACT, lambda: act.activation(out=JUNK[:], in_=x_ap, func=AF.Square, accum_out=SS[:, ssl:ssl + 1]),
                         R=[xbuf], W=[JUNK, SS])
                    rstd_of(SS[:, ssl:ssl + 1], RS[:, ssl:ssl + 1], 1, 1.0 / D)
                    xn = XN[xn_rot[0] % 4]
                    xn_rot[0] += 1
                    k.op(DVE, lambda: dve.tensor_scalar(out=xn[:], in0=x_ap, scalar1=RS[:, ssl:ssl + 1], scalar2=None,
                                                        op0=ALU.mult), R=[xbuf, RS], W=[xn])
                    return xn

                def norm_post(xn, Sx, SHoff, v, ht, col0):
                    pbs = [PS[0], PS[1]]
                    pvs = [pbs[0][:].bitcast(BF16), pbs[1][:].bitcast(BF16)]
                    for hb in range(2):
                        k.group([lambda kk=kk, hb=hb: pe.transpose(pvs[hb][:, (kk % 4) * 128:(kk % 4 + 1) * 128],
                                                                  xn[:, kk * 128:(kk + 1) * 128], IDB)
                                 for kk in range(hb * 4, hb * 4 + 4)], R=[xn, CB16], W=[pbs[hb]])
                    for j in range(4):
                        kk = j
                        k.op(DVE, lambda kk=kk, j=j: dve.tensor_scalar(out=ht[:, kk, col0:col0 + 128],
                                                                       in0=pvs[0][:, j * 128:(j + 1) * 128],
                                                                       scalar1=Sx[:, kk, v:v + 1],
                                                                       scalar2=MODP[:, SHoff + kk, v:v + 1],
                                                                       op0=ALU.mult, op1=ALU.add),
                             R=[pbs[0], Sx, MODP], W=[HTH[id(ht)][0]])
                        kk = 4 + j
                        k.op(ACT, lambda kk=kk, j=j: act.activation(out=ht[:, kk, col0:col0 + 128],
                                                                    in_=pvs[1][:, j * 128:(j + 1) * 128], func=AF.Identity,
                                                                    scale=Sx[:, kk, v:v + 1],
                                                                    bias=MODP[:, SHoff + kk, v:v + 1]),
                             R=[pbs[1], Sx, MODP], W=[HTH[id(ht)][1]])

                lagq = []

                def lag_flush():
                    while lagq:
                        lagq.pop(0)()

                def lag_push(fn):
                    lag_flush()
                    lagq.append(fn)

                def proj_fm(ht, ntok, col, evac):
                    pb = bank("proj")
                    k.group([lambda kk=kk: pe.matmul(pb[:, 0:ntok], lhsT=WIN[:, kk, col:col + 128], rhs=ht[:, kk, 0:ntok],
                                                     start=(kk == 0), stop=(kk == 7)) for kk in range(8)],
                            R=[WING[col]] + HTH[id(ht)], W=[pb])
                    lag_push(lambda: evac(pb))

                def proj_tm(ht, tcol, col, ncol, evac, kind="proj"):
                    pb = bank(kind)
                    k.group([lambda kk=kk: pe.matmul(pb[:, 0:ncol], lhsT=ht[:, kk, tcol:tcol + 128],
                                                     rhs=WIN[:, kk, col:col + ncol],
                                                     start=(kk == 0), stop=(kk == 7)) for kk in range(8)],
                            R=[WING[col]] + HTH[id(ht)], W=[pb])
                    lag_push(lambda: evac(pb))

                pre_rot = [0]

                def conv_silu(pb, ch, ntok, seqlen, out_ap, outbuf):
                    pr = PRE[pre_rot[0] % 3]
                    pre_rot[0] += 1
                    k.op(ACT, lambda: act.activation(out=pr[:, 0:ntok], in_=pb[:, 0:ntok], func=AF.Identity,
                                                     scale=CW[:, ch, 1:2], bias=CBI[:, ch:ch + 1]),
                         R=[pb, CW, CBI], W=[pr])
                    pv = pb[:, 0:ntok].rearrange("p (s t) -> p s t", t=seqlen)
                    rv = pr[:, 0:ntok].rearrange("p (s t) -> p s t", t=seqlen)

                    def stage_b():
                        k.op(DVE, lambda: dve.scalar_tensor_tensor(out=rv[:, :, 1:seqlen], in0=pv[:, :, 0:seqlen - 1],
                                                                   scalar=CW[:, ch, 0:1], in1=rv[:, :, 1:seqlen],
                                                                   op0=ALU.mult, op1=ALU.add), R=[pb, CW, pr], W=[pr])
                        k.op(DVE, lambda: dve.scalar_tensor_tensor(out=rv[:, :, 0:seqlen - 1], in0=pv[:, :, 1:seqlen],
                                                                   scalar=CW[:, ch, 2:3], in1=rv[:, :, 0:seqlen - 1],
                                                                   op0=ALU.mult, op1=ALU.add), R=[pb, CW, pr], W=[pr])

                    def stage_c():
                        k.op(ACT, lambda: act.activation(out=out_ap, in_=pr[:, 0:ntok], func=AF.Silu), R=[pr], W=[outbuf])

                    if conv_c:
                        conv_c.pop(0)()
                    if conv_b:
                        fb, fc = conv_b.pop(0)
                        fb()
                        conv_c.append(fc)
                    conv_b.append((stage_b, stage_c))

                conv_b = []
                conv_c = []

                def conv_flush():
                    while conv_b:
                        fb, fc = conv_b.pop(0)
                        fb()
                        conv_c.append(fc)
                    while conv_c:
                        conv_c.pop(0)()

                ktok_rot = [0]

                def k_to_tok_tile(ut):
                    lag_flush()
                    conv_flush()
                    pb = bank("small")
                    pbv = pb[:].bitcast(BF16)
                    k.group([lambda h=h: pe.transpose(pbv[:, h * 128:(h + 1) * 128], KT[:, h, ut * 128:(ut + 1) * 128], IDB)
                             for h in range(4)], R=[KTv[ut // 4], CB16], W=[pb])
                    ktok_rot[0] += 1
                    if ktok_rot[0] % 2 == 0:
                        k.op(DVE, lambda: dve.tensor_copy(out=KTOK[:, ut, :], in_=pbv[:, 0:512]), R=[pb], W=[KTOKv[ut // 4]])
                    else:
                        k.op(ACT, lambda: act.activation(out=KTOK[:, ut, :], in_=pbv[:, 0:512], func=AF.Copy), R=[pb], W=[KTOKv[ut // 4]])

                def gates_evac(pb, ut):
                    k.op(DVE, lambda: dve.tensor_tensor(out=GT[:, ut, :], in0=pb[:, 0:16], in1=BGB[:], op=ALU.add),
                         R=[pb, BGB], W=[GTv[ut // 4]])

                pa_state = {"xc": 0, "n": 0}

                def mk_blk(tiles, v, seqlen, own, u0, after=None):
                    blk = dict(tiles=tiles, v=v, seqlen=seqlen, own=own, u0=u0, after=after, idx=pa_state["n"])
                    blk["ht"] = HT[pa_state["n"] % 2]
                    pa_state["n"] += 1
                    return blk

                def pa_pre(blk, i):
                    t = blk["tiles"][i]
                    xb = XS[pa_state["xc"] % 2]
                    pa_state["xc"] += 1
                    if blk["own"]:
                        src = xp[t * 128:(t + 1) * 128, :] if t < NT_P else xs[(t - NT_P) * 128:(t - NT_P + 1) * 128, :]
                    else:
                        src = xf[t * 128:(t + 1) * 128, :]
                    k.dma(SP, xb[:], src, W=[xb])
                    blk.setdefault("xn", {})[i] = norm_pre(xb, xb[:], (blk["idx"] % 2) * 4 + i)

                def pa_post(blk, i):
                    norm_post(blk["xn"][i], S1, 0, blk["v"], blk["ht"], i * 128)

                def pa_items(blk):
                    tiles, v, seqlen, own, u0, ht = blk["tiles"], blk["v"], blk["seqlen"], blk["own"], blk["u0"], blk["ht"]
                    nt = len(tiles)
                    ntok = nt * 128
                    items = []
                    if own:
                        for c in range(4):
                            items.append(lambda c=c: proj_fm(ht, ntok, OFF_U + c * 128,
                                         lambda pb: k.op(ACT, lambda: act.activation(out=GUT[:, c, 0:ntok], in_=pb[:, 0:ntok],
                                                                                     func=AF.Gelu_apprx_tanh), R=[pb], W=[GUT])))
                        for c in range(4):
                            items.append(lambda c=c: proj_fm(ht, ntok, OFF_Q + c * 128,
                                         lambda pb: conv_silu(pb, c, ntok, seqlen, QT[:, c, u0 * 128:u0 * 128 + ntok], QTv[u0 // 4])))
                    for c in range(4):
                        items.append(lambda c=c: proj_fm(ht, ntok, OFF_K + c * 128,
                                     lambda pb: conv_silu(pb, 4 + c, ntok, seqlen, KT[:, c, u0 * 128:u0 * 128 + ntok], KTv[u0 // 4])))

                    def fm_tail_flush():
                        lag_flush()
                        conv_flush()
                    items.append(fm_tail_flush)

                    def tile_item(i, t):
                        ut = u0 + i
                        if i % 2 == 0:
                            proj_tm(ht, i * 128, OFF_VV, 512,
                                    lambda pb: k.op(ACT, lambda: act.activation(
                                        out=VV[:, ut, :, 0:128], in_=pb[:].rearrange("p (h d) -> p h d", d=128), func=AF.Copy),
                                        R=[pb], W=[VVv[ut // 4]]))
                        else:
                            proj_tm(ht, i * 128, OFF_VV, 512,
                                    lambda pb: k.op(DVE, lambda: dve.tensor_copy(
                                        out=VV[:, ut, :, 0:128], in_=pb[:].rearrange("p (h d) -> p h d", d=128)),
                                        R=[pb], W=[VVv[ut // 4]]))
                        proj_tm(ht, i * 128, OFF_G, 16, lambda pb: gates_evac(pb, ut), kind="small")

                    def tile_item_own(i, t):
                        ut = u0 + i
                        proj_tm(ht, i * 128, OFF_O, 512,
                                lambda pb: k.op(ACT, lambda: act.activation(out=TH[:, ut, :], in_=pb[:], func=AF.Tanh,
                                                                            scale=0.5), R=[pb], W=[TH]))
                        proj_tm(ht, i * 128, OFF_V, 512,
                                lambda pb: k.op(ACT, lambda: act.activation(out=GV[:], in_=pb[:], func=AF.Gelu_apprx_tanh),
                                                R=[pb], W=[GV]))
                        lag_flush()
                        for g in range(4):
                            k.op(ACT, lambda g=g: act.activation(out=JUNK[:, 0:128], in_=GV[:, g * 128:(g + 1) * 128],
                                                                 func=AF.Square, accum_out=SS[:, 8 + g:9 + g]),
                                 R=[GV], W=[JUNK, SS])
                        rstd_of(SS[:, 8:12], RS[:, 8:12], 4, 1.0 / 128)
                        vg = VGS[i % 2]
                        for g in range(4):
                            k.op(DVE, lambda g=g: dve.scalar_tensor_tensor(
                                out=vg[:, g * 128:(g + 1) * 128], in0=GV[:, g * 128:(g + 1) * 128],
                                scalar=RS[:, 8 + g:9 + g], in1=GVB[:, g * 128:(g + 1) * 128],
                                op0=ALU.mult, op1=ALU.mult), R=[GV, RS, GVB], W=[vg])

                    def tile_item_own_b(i, t):
                        vg = VGS[i % 2]
                        pb = bank("proj")
                        fns = []
                        for g in range(4):
                            fns.append(lambda g=g: pe.matmul(pb[:, g * 128:(g + 1) * 128], lhsT=vg[:, g * 128:(g + 1) * 128],
                                                             rhs=WST[:, g, :], start=True, stop=False))
                            fns.append(lambda g=g: pe.matmul(pb[:, g * 128:(g + 1) * 128], lhsT=ONESB[0:1, :],
                                                             rhs=BSR[0:1, g * 128:(g + 1) * 128], start=False, stop=True))
                        k.group(fns, R=[vg, WST, CB16, BSR], W=[pb])
                        k.op(DVE, lambda: dve.tensor_tensor(
                            out=AOT[:, t, :, :], in0=pb[:].rearrange("p (g t) -> p g t", t=128),
                            in1=GUT[:, :, i * 128:(i + 1) * 128], op=ALU.mult), R=[pb, GUT], W=[AOT])

                    def full_flush():
                        lag_flush()
                        conv_flush()
                    if blk.get("pre_a") is not None:
                        items.insert(0, blk["pre_a"])
                    if blk.get("mid_a") is not None:
                        items.append(full_flush)
                        items.append(blk["mid_a"])
                    for i, t in enumerate(tiles):
                        items.append(lambda i=i, t=t: tile_item(i, t))
                        if own:
                            items.append(lambda i=i, t=t: tile_item_own(i, t))
                            if i > 0:
                                items.append(lambda i=i: tile_item_own_b(i - 1, tiles[i - 1]))
                    if own:
                        items.append(lambda: tile_item_own_b(nt - 1, tiles[nt - 1]))
                    if blk.get("mid_b") is not None:
                        items.append(full_flush)
                        items.append(blk["mid_b"])
                    for i, t in enumerate(tiles):
                        items.append(lambda i=i: k_to_tok_tile(u0 + i))
                    return items

                def run_blocks(blks):
                    for n_, b_ in enumerate(blks):
                        b_["ht"] = HT[n_ % 2]
                        b_["idx"] = n_
                    for i in range(4):
                        pa_pre(blks[0], i)
                    for i in range(4):
                        pa_post(blks[0], i)
                    for n, blk in enumerate(blks):
                        items = pa_items(blk)
                        nxt = blks[n + 1] if n + 1 < len(blks) else None
                        L = len(items)
                        marks = {max(0, (L * (q + 1)) // 8): q for q in range(4)}
                        if nxt is not None:
                            for i in range(4):
                                pa_pre(nxt, i)
                        mods = blk.get("mods")
                        extras = list(blk.get("extras") or [])
                        for idx, it in enumerate(items):
                            it()
                            if extras:
                                lag_flush()
                                extras.pop(0)()
                            if nxt is not None and idx in marks:
                                pa_post(nxt, marks[idx])
                            if mods is not None and idx == L // 4:
                                lag_flush()
                                mod_compute(mods[0])
                            if mods is not None and idx == (3 * L) // 4:
                                lag_flush()
                                mod_compute(mods[1])
                        lag_flush()
                        conv_flush()
                        while extras:
                            extras.pop(0)()
                        if blk["after"] is not None:
                            blk["after"]()

                def gates_unit(ntl, seqs, m0_aps, prefix_only=False):
                    n8 = ntl * 8
                    gv4 = GT[:, 0:ntl, :].rearrange("p t (d j h) -> p t d j h", d=2, j=2)
                    SPT3 = SPT[:, 0:n8].rearrange("p (t e) -> p t e", e=8)
                    for d in range(2):
                        k.op(ACT, lambda d=d: act.activation(out=SPT3[:, :, d * 4:(d + 1) * 4], in_=gv4[:, :, d, 1, :],
                                                             func=AF.Exp, scale=-1.0), R=GTv, W=[SPT])
                    k.op(ACT, lambda: act.activation(out=SPT[:, 0:n8], in_=SPT[:, 0:n8], func=AF.Ln, bias=1.0),
                         R=[SPT], W=[SPT])
                    pcb = PS[0]
                    fns = []
                    for t in range(ntl):
                        fns.append(lambda t=t: pe.matmul(pcb[:, t * 8:t * 8 + 4], lhsT=TRIF, rhs=SPT[:, t * 8:t * 8 + 4],
                                                         start=True, stop=True))
                        fns.append(lambda t=t: pe.matmul(pcb[:, t * 8 + 4:t * 8 + 8], lhsT=TRIB, rhs=SPT[:, t * 8 + 4:t * 8 + 8],
                                                         start=True, stop=True))
                    k.group(fns, R=[CF, SPT], W=[pcb])
                    k.op(DVE, lambda: dve.tensor_copy(out=CBT[:, 0:n8], in_=pcb[:, 0:n8]), R=[pcb], W=[CBT])
                    for d in range(2):
                        k.op(DVE, lambda d=d: dve.tensor_tensor(
                            out=UT[:, 0:n8].rearrange("p (t e) -> p t e", e=8)[:, :, d * 4:(d + 1) * 4],
                            in0=CBT[:, 0:n8].rearrange("p (t e) -> p t e", e=8)[:, :, d * 4:(d + 1) * 4],
                            in1=gv4[:, :, d, 0, :], op=ALU.add), R=[CBT] + GTv, W=[UT])
                    ptr = PS[1]
                    k.group([lambda: pe.transpose(ptr[0:n8, 0:128], UT[:, 0:n8], IDF())], R=[UT, CF], W=[ptr])
                    k.op(DVE, lambda: dve.tensor_reduce(out=UMX[0:n8, :], in_=ptr[0:n8, 0:128], axis=AX.X, op=ALU.max),
                         R=[ptr], W=[UMX])
                    k.op(DVE, lambda: dve.tensor_scalar(out=DG[0:n8, 0:n8], in0=CF[0:n8, 0:n8], scalar1=UMX[0:n8, 0:1],
                                                        scalar2=None, op0=ALU.mult), R=[CF, UMX], W=[DG])
                    pbb = PS[1]
                    k.group([lambda: pe.matmul(pbb[:, 128:128 + n8], lhsT=ONESF[0:n8, :], rhs=DG[0:n8, 0:n8],
                                               start=True, stop=True)], R=[CF, DG], W=[pbb])
                    k.op(DVE, lambda: dve.tensor_copy(out=UMB[:, 0:n8], in_=pbb[:, 128:128 + n8]), R=[pbb], W=[UMB])
                    k.group([lambda: pe.matmul(pbb[:, 256:256 + n8], lhsT=ONESF,
                                               rhs=SPT[:, 0:n8], start=True, stop=True)],
                            R=[CF, SPT], W=[pbb])
                    k.op(DVE, lambda: dve.tensor_copy(out=BLB[:, 0:n8], in_=pbb[:, 256:256 + n8]), R=[pbb], W=[BLB])
                    if prefix_only:
                        return
                    for si, seq in enumerate(seqs):
                        for d in range(2):
                            order = seq if d == 0 else seq[::-1]
                            m0 = m0_aps[si][d]
                            if m0 is None:
                                k.op(DVE, lambda d=d: dve.memset(MCUR[:, d * 4:(d + 1) * 4], 0.0), W=[MCUR])
                            else:
                                k.op(DVE, lambda d=d, m0=m0: dve.tensor_copy(out=MCUR[:, d * 4:(d + 1) * 4], in_=m0[0]),
                                     R=[m0[1]], W=[MCUR])
                            for t in order:
                                sl = slice(t * 8 + d * 4, t * 8 + d * 4 + 4)
                                k.op(DVE, lambda sl=sl, d=d: dve.tensor_copy(out=MPV[:, sl], in_=MCUR[:, d * 4:(d + 1) * 4]),
                                     R=[MCUR], W=[MPV])
                                k.op(DVE, lambda sl=sl, d=d: dve.tensor_tensor(out=RB[:, sl], in0=MCUR[:, d * 4:(d + 1) * 4],
                                                                               in1=UMB[:, sl], op=ALU.max),
                                     R=[MCUR, UMB], W=[RB])
                                k.op(DVE, lambda sl=sl, d=d: dve.tensor_tensor(out=MCUR[:, d * 4:(d + 1) * 4], in0=RB[:, sl],
                                                                               in1=BLB[:, sl], op=ALU.subtract),
                                     R=[RB, BLB], W=[MCUR])
                            seq_end(si, d)
                    k.op(DVE, lambda: dve.tensor_tensor(out=EE[:, 0:n8], in0=UT[:, 0:n8], in1=RB[:, 0:n8], op=ALU.subtract),
                         R=[UT, RB], W=[EE])
                    k.op(ACT, lambda: act.activation(out=EE[:, 0:n8], in_=EE[:, 0:n8], func=AF.Exp, bias=MISC[:, 0:1]),
                         R=[EE, MISC], W=[EE])
                    k.op(DVE, lambda: dve.tensor_tensor(out=THR[:, 0:n8], in0=CBT[:, 0:n8], in1=RB[:, 0:n8], op=ALU.subtract),
                         R=[CBT, RB], W=[THR])
                    k.op(ACT, lambda: act.activation(out=THR[:, 0:n8], in_=THR[:, 0:n8], func=AF.Exp), R=[THR], W=[THR])
                    k.op(DVE, lambda: dve.tensor_tensor(out=DCY[:, 0:n8], in0=MPV[:, 0:n8], in1=RB[:, 0:n8], op=ALU.subtract),
                         R=[MPV, RB], W=[DCY])
                    k.op(ACT, lambda: act.activation(out=DCY[:, 0:n8], in_=DCY[:, 0:n8], func=AF.Exp), R=[DCY], W=[DCY])

                seq_end_cb = [None]

                def seq_end(si, d):
                    if seq_end_cb[0] is not None:
                        seq_end_cb[0](si, d)

                k.op(DVE, lambda: dve.memset(MISC[:, 0:1], LNS), W=[MISC])

                NCH = 8
                HSV = [k.view(f"HSV{t}", HS.t[:, t, :]) for t in range(NT_S)]
                DDV = [k.view(f"DDV{i}", DD.t[:, i * 4:(i + 1) * 4]) for i in range(NCH)]
                hs_written = {}

                def mlstm_wave(chains, banks=None, slot0=0):
                    n = len(chains)
                    info = []
                    for i, (ut, d, h) in enumerate(chains):
                        j = slot0 + i
                        info.append(dict(ut=ut, d=d, h=h, col=ut * 8 + d * 4 + h, pb=(banks[i] if banks else PS[i]), vp=VP[j],
                                         cb=CBF[j], st=ST[j], dd=DDV[j], cst=CST[d][h], qs=slice(ut * 128, (ut + 1) * 128)))
                    for i, c in enumerate(info):
                        k.op(ACT, lambda c=c: act.activation(out=c["vp"][:, 0:130], in_=VV[:, c["ut"], c["h"], 0:130],
                                                             func=AF.Identity, scale=EE[:, c["col"]:c["col"] + 1]),
                             R=[VVv[c["ut"] // 4], EE], W=[c["vp"]])
                        if False:
                            pass
                        else:
                            k.op(ACT, lambda c=c: act.activation(out=c["cb"][:, 0:130], in_=c["cst"][:, 0:130],
                                                                 func=AF.Identity, scale=DCY[:, c["col"]:c["col"] + 1]),
                                 R=[c["cst"], DCY], W=[c["cb"]])
                    for c in info:
                        k.group([lambda c=c: pe.matmul(c["pb"][:, 0:128], lhsT=KT[:, c["h"], c["qs"]], rhs=QT[:, c["h"], c["qs"]],
                                                       start=True, stop=True)], R=[KTv[c["ut"] // 4], QTv[c["ut"] // 4]], W=[c["pb"]])
                    for c in info:
                        k.op(DVE, lambda c=c: dve.tensor_tensor(out=c["st"][:], in0=c["pb"][:, 0:128], in1=MASK[c["d"]],
                                                                op=ALU.mult), R=[c["pb"], CB16], W=[c["st"]])
                    for c in info:
                        k.group([lambda c=c: pe.matmul(c["pb"][:, 128:258], lhsT=c["st"][:], rhs=c["vp"][:, 0:130],
                                                       start=True, stop=False),
                                 lambda c=c: pe.matmul(c["pb"][:, 128:258], lhsT=QT[:, c["h"], c["qs"]], rhs=c["cb"][:, 0:130],
                                                       start=False, stop=True),
                                 lambda c=c: pe.matmul(c["pb"][:, 260:390], lhsT=KTOK[:, c["ut"], c["h"] * 128:(c["h"] + 1) * 128],
                                                       rhs=c["vp"][:, 0:130], start=True, stop=True)],
                                R=[c["st"], c["vp"], QTv[c["ut"] // 4], c["cb"], KTOKv[c["ut"] // 4]], W=[c["pb"]])
                    for c in info:
                        dd = c["dd"]
                        k.op(DVE, lambda c=c, dd=dd: dve.tensor_scalar(out=dd[:, 2:3], in0=c["pb"][:, 256:257], scalar1=-1.0,
                                                                       scalar2=None, op0=ALU.mult), R=[c["pb"]], W=[dd])
                        k.op(DVE, lambda c=c, dd=dd: dve.scalar_tensor_tensor(out=dd[:, 0:1], in0=c["pb"][:, 256:257],
                                                                              scalar=THR[:, c["col"]:c["col"] + 1],
                                                                              in1=dd[:, 2:3], op0=ALU.max, op1=ALU.max),
                             R=[c["pb"], THR, dd], W=[dd])
                        k.op(DVE, lambda dd=dd: dve.reciprocal(out=dd[:, 1:2], in_=dd[:, 0:1]), R=[dd], W=[dd])
                    for c in info:
                        dd = c["dd"]
                        hsb = HSV[c["ut"]]
                        hsl = slice(c["h"] * 128, (c["h"] + 1) * 128)
                        key = (c["ut"], c["h"])
                        if key not in hs_written:
                            hs_written[key] = True
                            k.op(ACT, lambda c=c, dd=dd, hsb=hsb, hsl=hsl: act.activation(out=hsb[:, hsl], in_=c["pb"][:, 128:256],
                                                                                         func=AF.Identity, scale=dd[:, 1:2]),
                                 R=[c["pb"], dd], W=[hsb])
                        else:
                            k.op(DVE, lambda c=c, dd=dd, hsb=hsb, hsl=hsl: dve.scalar_tensor_tensor(
                                out=hsb[:, hsl], in0=c["pb"][:, 128:256], scalar=dd[:, 1:2], in1=hsb[:, hsl],
                                op0=ALU.mult, op1=ALU.add), R=[c["pb"], dd, hsb], W=[hsb])
                    for c in info:
                        k.op(DVE, lambda c=c: dve.scalar_tensor_tensor(out=c["cst"][:, 0:130], in0=c["cst"][:, 0:130],
                                                                       scalar=DCY[:, c["col"]:c["col"] + 1],
                                                                       in1=c["pb"][:, 260:390], op0=ALU.mult, op1=ALU.add),
                             R=[c["cst"], DCY, c["pb"]], W=[c["cst"]])

                def mlstm_post_unit(uts, gts):
                    ntl = len(uts)
                    for ut in uts:
                        for h in range(4):
                            k.op(ACT, lambda h=h, ut=ut: act.activation(out=JUNK[:, 0:128], in_=HSV[ut][:, h * 128:(h + 1) * 128],
                                                                        func=AF.Square,
                                                                        accum_out=SS[:, 16 + ut * 4 + h:17 + ut * 4 + h]),
                                 R=[HSV[ut]], W=[JUNK, SS])
                        rstd_of(SS[:, 16 + 4 * ut:20 + 4 * ut], RS[:, 16 + 4 * ut:20 + 4 * ut], 4, 1.0 / 128)
                    for ut, gt in zip(uts, gts):
                        for h in range(4):
                            k.op(DVE, lambda h=h, ut=ut: dve.scalar_tensor_tensor(
                                out=HSV[ut][:, h * 128:(h + 1) * 128], in0=HSV[ut][:, h * 128:(h + 1) * 128],
                                scalar=RS[:, 16 + ut * 4 + h:17 + ut * 4 + h], in1=GHB[:, h * 128:(h + 1) * 128],
                                op0=ALU.mult, op1=ALU.mult), R=[HSV[ut], RS, GHB], W=[HSV[ut]])
                        xn = XN[xn_rot[0] % 2]
                        xn_rot[0] += 1
                        k.op(POOL, lambda ut=ut, xn=xn: pool.scalar_tensor_tensor(out=xn[:, 0:512], in0=TH[:, ut, :], scalar=1.0,
                                                                                   in1=HSV[ut][:], op0=ALU.add, op1=ALU.mult),
                             R=[TH, HSV[ut]], W=[xn]) if False else k.op(
                            DVE, lambda ut=ut, xn=xn: dve.scalar_tensor_tensor(out=xn[:, 0:512], in0=TH[:, ut, :], scalar=1.0,
                                                                               in1=HSV[ut][:], op0=ALU.add, op1=ALU.mult),
                            R=[TH, HSV[ut]], W=[xn])
                        pb = bank("tr")
                        pbv = pb[:].bitcast(BF16)
                        k.group([lambda c=c, xn=xn, pbv=pbv: pe.transpose(pbv[:, c * 128:(c + 1) * 128],
                                                                          xn[:, c * 128:(c + 1) * 128], IDB)
                                 for c in range(4)], R=[xn, CB16], W=[pb])
                        k.op(ACT, lambda gt=gt, pbv=pbv: act.activation(out=BOT[:, gt, :, :],
                                                                        in_=pbv[:, 0:512].rearrange("p (c t) -> p c t", t=128),
                                                                        func=AF.Copy), R=[pb], W=[BOT])

                prompt_blk = mk_blk([0, 1, 2, 3], 0, 256, True, 0)

                def prompt_seq_end(si, d):
                    k.dma(SP, nm[si:si + 1, d * 4:(d + 1) * 4], MCUR[0:1, d * 4:(d + 1) * 4], R=[MCUR], final=True)

                def prompt_gates():
                    seq_end_cb[0] = prompt_seq_end
                    gates_unit(4, [[0, 1], [2, 3]], [[None, None], [None, None]])
                    seq_end_cb[0] = None

                HW_BANKS = [PS[0], PS[1], PS[6], PS[7]]
                prompt_extras = [prompt_gates]
                for si, seq in enumerate([[0, 1], [2, 3]]):
                    def zero_states():
                        for d in range(2):
                            for h in range(4):
                                k.op(POOL, lambda d=d, h=h: pool.memset(CST[d][h][:], 0.0), W=[CST[d][h]])
                    prompt_extras.append(zero_states)
                    for w in range(2):
                        prompt_extras.append(lambda seq=seq, w=w: mlstm_wave([(seq[w], 0, h) for h in range(4)],
                                                                            banks=HW_BANKS, slot0=0))
                        prompt_extras.append(lambda seq=seq, w=w: mlstm_wave([(seq[1 - w], 1, h) for h in range(4)],
                                                                            banks=HW_BANKS, slot0=4))

                    def store_states(si=si):
                        with nc.allow_non_contiguous_dma(reason="state column"):
                            for d in range(2):
                                for h in range(4):
                                    k.dma(SP, nC[si, d, h, :, :], CST[d][h][:, 0:128], R=[CST[d][h]], final=True)
                                    k.dma(SP, nn[si, d, h, :].rearrange("(p o) -> p o", o=1), CST[d][h][:, 128:129],
                                          R=[CST[d][h]], final=True)
                    prompt_extras.append(store_states)
                prompt_extras.append(lambda: mlstm_post_unit([0, 1, 2, 3], [0, 1, 2, 3]))

                mod_load(4)

                stage(5)
                k.op(DVE, lambda: dve.memset(NEGT[:], NEG), W=[NEGT])
                k.op(DVE, lambda: dve.tensor_scalar(out=NEGF[:], in0=FLB[:], scalar1=-1.0, scalar2=-NEG,
                                                    op0=ALU.add, op1=ALU.mult), R=[FLB], W=[NEGF])
                def summary_pre(p):
                    gates_unit(8, None, None, prefix_only=True)
                    summary_a0(p)

                def summary_a0(p):
                    B3 = BLB[:, 0:64].rearrange("p (t e) -> p t e", e=8)
                    O3 = OFS[:, 0:64].rearrange("p (t e) -> p t e", e=8)
                    k.op(DVE, lambda: dve.memset(OFS[:], 0.0), W=[OFS])
                    for t in range(1, 8):
                        k.op(DVE, lambda t=t: dve.tensor_tensor(out=O3[:, t, 0:4], in0=O3[:, t - 1, 0:4], in1=B3[:, t - 1, 0:4],
                                                                op=ALU.add), R=[OFS, BLB], W=[OFS])
                    for t in range(6, -1, -1):
                        k.op(DVE, lambda t=t: dve.tensor_tensor(out=O3[:, t, 4:8], in0=O3[:, t + 1, 4:8], in1=B3[:, t + 1, 4:8],
                                                                op=ALU.add), R=[OFS, BLB], W=[OFS])
                    k.op(DVE, lambda p=p: dve.tensor_reduce(out=BSEG[:, p - 1, :],
                                                            in_=BLB[:, 0:64].rearrange("p (t e) -> p e t", e=8),
                                                            axis=AX.X, op=ALU.add), R=[BLB], W=[BSEG])
                    k.op(DVE, lambda: dve.tensor_tensor(out=RB[:, 0:64], in0=UMB[:, 0:64], in1=OFS[:, 0:64], op=ALU.add),
                         R=[UMB, OFS], W=[RB])
                    k.op(DVE, lambda: dve.tensor_reduce(out=RSEG[:], in_=RB[:, 0:64].rearrange("p (t e) -> p e t", e=8),
                                                        axis=AX.X, op=ALU.max), R=[RB], W=[RSEG])
                    k.op(DVE, lambda p=p: dve.tensor_tensor(out=MLOC[:, p - 1, :], in0=RSEG[:], in1=BSEG[:, p - 1, :],
                                                            op=ALU.subtract), R=[RSEG, BSEG], W=[MLOC])
                    k.op(DVE, lambda: dve.tensor_tensor(out=EE[:, 0:64], in0=UT[:, 0:64], in1=OFS[:, 0:64], op=ALU.add),
                         R=[UT, OFS], W=[EE])
                    for t in range(8):
                        k.op(DVE, lambda t=t: dve.tensor_tensor(out=EE[:, t * 8:t * 8 + 8], in0=EE[:, t * 8:t * 8 + 8],
                                                                in1=RSEG[:], op=ALU.subtract), R=[EE, RSEG], W=[EE])
                    k.op(ACT, lambda: act.activation(out=EE[:, 0:64], in_=EE[:, 0:64], func=AF.Exp, bias=MISC[:, 0:1]),
                         R=[EE, MISC], W=[EE])
                    summary_vpb(0)

                def summary_a(p):
                    summary_mm(p, 0)
                    summary_vpb(1)

                VPB = HS.t[:].rearrange("p t c -> p (t c)").bitcast(BF16)[:, 0:8 * 4 * 132].rearrange(
                    "p (t h c) -> p t h c", t=8, h=4)

                def summary_vpb(d):
                    for t in range(8):
                        k.op(DVE, lambda t=t, d=d: dve.tensor_tensor(
                            out=VPB[:, t, :, 0:130], in0=VV[:, t, :, 0:130],
                            in1=EE[:, t * 8 + d * 4:t * 8 + d * 4 + 4].unsqueeze(2).to_broadcast([128, 4, 130]),
                            op=ALU.mult), R=VVv + [EE], W=HSV)

                def summary_mm(p, d):
                    pbs = [PS[2 + d * 2], PS[3 + d * 2]]
                    for h in range(4):
                        pb = pbs[h // 2]
                        c0 = (h % 2) * 132
                        k.group([lambda t=t, h=h, pb=pb, c0=c0: pe.matmul(pb[:, c0:c0 + 130],
                                                                         lhsT=KTOK[:, t, h * 128:(h + 1) * 128],
                                                                         rhs=VPB[:, t, h, 0:130],
                                                                         start=(t == 0), stop=(t == 7)) for t in range(8)],
                                R=KTOKv + HSV, W=[pb])
                        scb = SC[p - 1][d][h]
                        k.op(ACT, lambda scb=scb, pb=pb, c0=c0: act.activation(out=scb[:], in_=pb[:, c0:c0 + 130],
                                                                               func=AF.Copy),
                             R=[pb], W=[scb] + (SCX[id(scb)] or HSV))

                def summary_b(p):
                    summary_mm(p, 1)

                blks = []
                for p in (1, 2, 3):
                    base = (p - 1) * 8
                    ba = mk_blk([base + i for i in range(4)], 1, 64, False, 0)
                    bb = mk_blk([base + 4 + i for i in range(4)], 1, 64, False, 4)
                    blks += [bb, ba] if p == 1 else [ba, bb]
                blks.append(mk_blk([4, 5, 6, 7], 1, 64, True, 0))
                blks.append(mk_blk([8, 9, 10, 11], 1, 64, True, 4))
                for p in (1, 2, 3):
                    blks[2 * p]["pre_a"] = (lambda p=p: summary_pre(p))
                    blks[2 * p]["mid_a"] = (lambda p=p: summary_a(p))
                    blks[2 * p]["mid_b"] = (lambda p=p: summary_b(p))
                for j in range(4):
                    blks[j]["mods"] = [4 + 2 * j, 5 + 2 * j]
                def fold_states():
                    for d in range(2):
                        for h in range(4):
                            k.dma(SP, CST[d][h][:, 0:128], stC[d, h, :, :], W=[CST[d][h]])
                            with nc.allow_non_contiguous_dma(reason="state column"):
                                k.dma(SP, CST[d][h][:, 128:129], stn[d, h, :].rearrange("(p o) -> p o", o=1), W=[CST[d][h]])
                    stage(5.6)
                    k.op(DVE, lambda: dve.tensor_copy(out=MFO[:], in_=M0B[:]), R=[M0B], W=[MFO])
                    for d in range(2):
                        for p in ((1, 2, 3) if d == 0 else (3, 2, 1)):
                            fi = d * 3 + p - 1
                            dsl = slice(d * 4, (d + 1) * 4)
                            k.op(DVE, lambda: dve.scalar_tensor_tensor(out=FT[:, 0:4], in0=BSEG[:, p - 1, dsl],
                                                                       scalar=FLB[:, fi:fi + 1], in1=MFO[:, dsl],
                                                                       op0=ALU.mult, op1=ALU.subtract), R=[BSEG, FLB, MFO], W=[FT])
                            k.op(DVE, lambda: dve.tensor_scalar(out=FT[:, 0:4], in0=FT[:, 0:4], scalar1=-1.0, scalar2=None,
                                                                op0=ALU.mult), R=[FT], W=[FT])
                            k.op(DVE, lambda: dve.tensor_scalar(out=FT[:, 4:8], in0=MLOC[:, p - 1, dsl], scalar1=FLB[:, fi:fi + 1],
                                                                scalar2=NEGF[:, fi:fi + 1], op0=ALU.mult, op1=ALU.add),
                                 R=[MLOC, FLB, NEGF], W=[FT])
                            k.op(DVE, lambda: dve.tensor_tensor(out=MFO[:, dsl], in0=FT[:, 0:4], in1=FT[:, 4:8], op=ALU.max),
                                 R=[FT], W=[MFO])
                            for q in range(2):
                                k.op(DVE, lambda q=q: dve.tensor_tensor(out=FT[:, 8 + q * 4:12 + q * 4], in0=FT[:, q * 4:q * 4 + 4],
                                                                        in1=MFO[:, dsl], op=ALU.subtract), R=[FT, MFO], W=[FT])
                            k.op(ACT, lambda: act.activation(out=FT[:, 8:16], in_=FT[:, 8:16], func=AF.Exp), R=[FT], W=[FT])
                            for h in range(4):
                                k.op(DVE, lambda h=h: dve.tensor_scalar(out=CST[d][h][:, 0:130], in0=CST[d][h][:, 0:130],
                                                                        scalar1=FT[:, 8 + h:9 + h], scalar2=None, op0=ALU.mult),
                                     R=[CST[d][h], FT], W=[CST[d][h]])
                                k.op(DVE, lambda h=h: dve.scalar_tensor_tensor(out=CST[d][h][:, 0:130], in0=SC[p - 1][d][h][:],
                                                                               scalar=FT[:, 12 + h:13 + h], in1=CST[d][h][:, 0:130],
                                                                               op0=ALU.mult, op1=ALU.add),
                                     R=[SC[p - 1][d][h], FT, CST[d][h]] + (SCX[id(SC[p - 1][d][h])] or HSV), W=[CST[d][h]])

                blks[7]['mid_b'] = fold_states
                blks[0]["extras"] = prompt_extras
                run_blocks([prompt_blk] + blks)
                stage(5.5)
                stage(5.7)
                gates_unit(8, [list(range(8))], [[(MFO[:, 0:4], MFO), (MFO[:, 4:8], MFO)]])
                hs_written.clear()
                for w in range(8):
                    mlstm_wave([(w, 0, h) for h in range(4)] + [(7 - w, 1, h) for h in range(4)])
                mlstm_post_unit(list(range(8)), [4 + u for u in range(8)])

                stage(6)
                era1_tokens = k.all_tokens(k.bufs)

            GAB.w.update(era1_tokens)
            GAB.r.update(era1_tokens)

            def gate_bcast(off):
                for v in range(2):
                    pt = PS[0]
                    k.group([lambda: pe.transpose(pt[0:8, 0:128], MODP[:, off:off + 8, v], IDF())], R=[MODP, CF], W=[pt])
                    k.op(DVE, lambda: dve.tensor_copy(out=UTT[0:8, :], in_=pt[0:8, 0:128]), R=[pt], W=[UTT])
                    for hf in range(2):
                        pbk = PS[1 + hf]
                        fns = []
                        for c in range(4):
                            kk = hf * 4 + c
                            fns.append(lambda kk=kk, c=c: pe.matmul(pbk[:, c * 128:(c + 1) * 128],
                                                                    lhsT=OH[0:8, kk, :],
                                                                    rhs=UTT[0:8, :], start=True, stop=True))
                        k.group(fns, R=[OH, UTT], W=[pbk])
                        k.op(DVE, lambda hf=hf, v=v, pbk=pbk: dve.tensor_copy(out=GAB[:, v, hf * 512:(hf + 1) * 512],
                                                                             in_=pbk[:]), R=[pbk], W=[GAB])

            esx = ExitStack()
            with esx:
                def nbx(es, name, shape, dt, toks):
                    b = k.sb(es, name, shape, dt)
                    b.w.update(toks)
                    b.r.update(toks)
                    return b
                X = [nbx(esx, f"X{t}", [128, D], F32, era1_tokens) for t in range(NT)]
                esc = ExitStack()
                with esc:
                    WOUT = nbx(esc, "WOUT", [128, 8, D], BF16, era1_tokens)
                    TMPCS = [nbx(esc, f"TMPC{i}", [128, 512], F32, era1_tokens) for i in range(2)]
                    wout_view = w_out.rearrange("(k p) n -> p k n", p=128)
                    for kk in range(8):
                        k.dma(POOL, WOUT[:, kk, :], wout_view[:, kk, :], W=[WOUT])
                    for t in range(NT):
                        src = xp[t * 128:(t + 1) * 128, :] if t < NT_P else xs[(t - NT_P) * 128:(t - NT_P + 1) * 128, :]
                        k.dma(POOL, X[t][:], src, W=[X[t]])
                    gate_bcast(16)
                    crot = [0]
                    for t in range(NT):
                        v = 0 if t < NT_P else 1
                        for hf in range(2):
                            pb = PS[crot[0] % 8]
                            crot[0] += 1
                            fns = []
                            for kc in range(8):
                                src = AOT if kc < 4 else BOT
                                fns.append(lambda kc=kc, src=src: pe.matmul(pb[:], lhsT=src[:, t, kc % 4, :],
                                                                            rhs=WOUT[:, kc, hf * 512:(hf + 1) * 512],
                                                                            start=(kc == 0), stop=(kc == 7)))
                            k.group(fns, R=[AOT, BOT, WOUT], W=[pb])
                            TMPC = TMPCS[crot[0] % 2]
                            k.op(DVE, lambda: dve.tensor_tensor(out=TMPC[:], in0=pb[:], in1=GAB[:, v, hf * 512:(hf + 1) * 512],
                                                                op=ALU.mult), R=[pb, GAB], W=[TMPC])
                            eng, hh = (DVE, dve) if hf == 0 else (POOL, pool)
                            k.op(eng, lambda hh=hh: hh.tensor_tensor(out=X[t][:, hf * 512:(hf + 1) * 512],
                                                                     in0=X[t][:, hf * 512:(hf + 1) * 512], in1=TMPC[:],
                                                                     op=ALU.add),
                                 R=[X[t], TMPC], W=[X[t]])
                    gate_bcast(40)
                    era_tokens = k.all_tokens(k.bufs)

                stage(7)
                es2 = ExitStack()
                es2.__enter__()

                def nb(name, shape, dt):
                    return nbx(es2, name, shape, dt, era_tokens)
                MB = 6 * 128
                W2 = nb("W2", [128, NFF, D], BF16)
                GTF = nb("GTF", [128, NFF, MB], BF16)
                GFB = nb("GFB", [128, D], F32)
                k.dma(SP, GFB[:], g_final.partition_broadcast(128), W=[GFB])
                aflat = AOT.t[:].rearrange("p t c d -> p (t c d)")
                bflat = BOT.t[:].rearrange("p t c d -> p (t c d)")
                H2 = k.view("H2", aflat.rearrange("p (k n) -> p k n", n=MB), era_tokens)
                W13 = [k.view(f"W13_{i}", bflat[:, i * 2048:(i + 1) * 2048].rearrange("p (w k n) -> p w k n", w=2, k=8),
                              era_tokens) for i in range(3)]
                SA = [nb(f"SA{i}", [128, 384], BF16) for i in range(2)]
                TM2 = nb("TM2", [128, 512], F32)
                OUT = [nb(f"OUT{i}", [128, D], F32) for i in range(2)]
                w1v = w1.rearrange("(k p) n -> p k n", p=128)
                w3v = w3.rearrange("(k p) n -> p k n", p=128)
                w2v = w2.rearrange("(f p) n -> p f n", p=128)
                orot = [0]
                for mb in range(2):
                    tiles = list(range(mb * 6, mb * 6 + 6))
                    for i, t in enumerate(tiles):
                        v = 0 if t < NT_P else 1
                        ssl = 24 + i
                        k.op(ACT, lambda t=t, ssl=ssl: act.activation(out=JUNK[:], in_=X[t][:], func=AF.Square,
                                                                      accum_out=SS[:, ssl:ssl + 1]), R=[X[t]], W=[JUNK, SS])
                        rstd_of(SS[:, ssl:ssl + 1], RS[:, ssl:ssl + 1], 1, 1.0 / D)
                        xn = XN[i % 4]
                        k.op(DVE, lambda t=t, ssl=ssl, xn=xn: dve.tensor_scalar(out=xn[:], in0=X[t][:], scalar1=RS[:, ssl:ssl + 1],
                                                                               scalar2=None, op0=ALU.mult),
                             R=[X[t], RS], W=[xn])
                        pb = PS[i % 2]
                        pbv = pb[:].bitcast(BF16)
                        k.group([lambda kk=kk, xn=xn, pbv=pbv: pe.transpose(pbv[:, kk * 128:(kk + 1) * 128],
                                                                           xn[:, kk * 128:(kk + 1) * 128], IDB)
                                 for kk in range(8)], R=[xn, CB16], W=[pb])
                        for kk in range(8):
                            k.op(DVE, lambda kk=kk, i=i, v=v, pbv=pbv: dve.tensor_scalar(
                                out=H2[:, kk, i * 128:(i + 1) * 128], in0=pbv[:, kk * 128:(kk + 1) * 128],
                                scalar1=S2[:, kk, v:v + 1], scalar2=MODP[:, 24 + kk, v:v + 1], op0=ALU.mult, op1=ALU.add),
                                R=[pb, S2, MODP], W=[H2])
                    for f in range(NFF):
                        slot = W13[f % 3]
                        k.dma(POOL, slot[:, 0, :, :], w1v[:, :, f * 128:(f + 1) * 128], W=[slot])
                        k.dma(POOL, slot[:, 1, :, :], w3v[:, :, f * 128:(f + 1) * 128], W=[slot])
                        if mb == 0:
                            k.dma(POOL, W2[:, f, :], w2v[:, f, :], W=[W2])
                        for hf in range(2):
                            pa = PS[2 + (f * 4 + hf * 2) % 6]
                            pbb = PS[2 + (f * 4 + hf * 2 + 1) % 6]
                            for (pp, wi) in ((pa, 0), (pbb, 1)):
                                k.group([lambda kk=kk, pp=pp, wi=wi: pe.matmul(pp[:, 0:384], lhsT=slot[:, wi, kk, :],
                                                                               rhs=H2[:, kk, hf * 384:(hf + 1) * 384],
                                                                               start=(kk == 0), stop=(kk == 7))
                                         for kk in range(8)], R=[slot, H2], W=[pp])
                            sa = SA[hf]
                            k.op(ACT, lambda pa=pa, sa=sa: act.activation(out=sa[:], in_=pa[:, 0:384], func=AF.Silu),
                                 R=[pa], W=[sa])
                            k.op(DVE, lambda pbb=pbb, sa=sa, f=f, hf=hf: dve.tensor_tensor(
                                out=GTF[:, f, hf * 384:(hf + 1) * 384], in0=pbb[:, 0:384], in1=sa[:], op=ALU.mult),
                                R=[pbb, sa], W=[GTF])
                    for i, t in enumerate(tiles):
                        v = 0 if t < NT_P else 1
                        for hf in range(2):
                            pb = PS[(i * 2 + hf) % 8]
                            k.group([lambda f=f, pb=pb: pe.matmul(pb[:], lhsT=GTF[:, f, i * 128:(i + 1) * 128],
                                                                  rhs=W2[:, f, hf * 512:(hf + 1) * 512],
                                                                  start=(f == 0), stop=(f == NFF - 1)) for f in range(NFF)],
                                    R=[GTF, W2], W=[pb])
                            k.op(DVE, lambda pb=pb, v=v, hf=hf: dve.tensor_tensor(out=TM2[:], in0=pb[:],
                                                                                 in1=GAB[:, v, hf * 512:(hf + 1) * 512],
                                                                                 op=ALU.mult), R=[pb, GAB], W=[TM2])
                            k.op(POOL, lambda t=t, hf=hf: pool.tensor_tensor(out=X[t][:, hf * 512:(hf + 1) * 512],
                                                                             in0=X[t][:, hf * 512:(hf + 1) * 512],
                                                                             in1=TM2[:], op=ALU.add),
                                 R=[X[t], TM2], W=[X[t]])
                        ssl = 32 + i
                        k.op(ACT, lambda t=t, ssl=ssl: act.activation(out=JUNK[:], in_=X[t][:], func=AF.Square,
                                                                      accum_out=SS[:, ssl:ssl + 1]), R=[X[t]], W=[JUNK, SS])
                        rstd_of(SS[:, ssl:ssl + 1], RS[:, ssl:ssl + 1], 1, 1.0 / D)
                        ob = OUT[orot[0] % 2]
                        orot[0] += 1
                        k.op(DVE, lambda t=t, ssl=ssl, ob=ob: dve.scalar_tensor_tensor(
                            out=ob[:], in0=X[t][:], scalar=RS[:, ssl:ssl + 1], in1=GFB[:], op0=ALU.mult, op1=ALU.mult),
                            R=[X[t], RS, GFB], W=[ob])
                        dst = yp[t * 128:(t + 1) * 128, :] if t < NT_P else ys[(t - NT_P) * 128:(t - NT_P + 1) * 128, :]
                        k.dma(SP, dst, ob[:], R=[ob], final=True)

                es2.__exit__(None, None, None)
        except StageExit:
            pass
        k.finish()
    return nc


_NC_CACHE = {}


def make_consts():
    c = np.zeros((128, 512), np.float32)
    c[:, 0:128] = np.eye(128, dtype=np.float32)
    s = np.arange(128)[:, None]
    j = np.arange(128)[None, :]
    c[:, 128:256] = (s <= j).astype(np.float32)
    c[:, 256:384] = (s >= j).astype(np.float32)
    c[:, 384:512] = 1.0
    return c


def kernel(x_prompt, x_sample, state_C, state_n, state_m, c, c_ctx, w_ada, b_ada, g_norm1,
           w_in, b_gate, w_s, b_s, g_v, conv_w, conv_b, g_h, w_out, g_norm2, w1, w3, w2, g_final):
    f = lambda a: np.ascontiguousarray(np.asarray(a, dtype=np.float32))
    x_prompt, x_sample = f(x_prompt), f(x_sample)
    state_C, state_n, state_m, c, c_ctx = f(state_C), f(state_n), f(state_m), f(c), f(c_ctx)
    if "nc" not in _NC_CACHE:
        _NC_CACHE["nc"] = build_nc()
    nc = _NC_CACHE["nc"]
    consts = make_consts()
    shared = {
        "consts": consts, "w_ada": f(w_ada)[0], "b_ada": f(b_ada)[0], "g_norm1": f(g_norm1)[0], "w_in": f(w_in)[0],
        "b_gate": f(b_gate)[0].reshape(1, 16), "w_s": f(w_s)[0], "b_s": f(b_s)[0].reshape(1, 512),
        "g_v": f(g_v)[0].reshape(1, 512), "conv_w": f(conv_w)[0], "conv_b": f(conv_b)[0],
        "g_h": f(g_h)[0].reshape(1, 512), "w_out": f(w_out)[0], "g_norm2": f(g_norm2)[0], "w1": f(w1)[0],
        "w3": f(w3)[0], "w2": f(w2)[0], "g_final": f(g_final).reshape(1, D),
    }
    in_maps = []
    for core in range(8):
        b, j = core // 4, core % 4
        m = dict(shared)
        m["xp"] = x_prompt[2 * core:2 * core + 2].reshape(512, D)
        m["xs"] = x_sample[b, j * 1024:(j + 1) * 1024]
        m["xf"] = np.concatenate([x_sample[b, ((j + p) % 4) * 1024:((j + p) % 4 + 1) * 1024] for p in (1, 2, 3)], 0)
        m["cvec"] = np.stack([c_ctx, c[b]], 0)
        m["stC"] = state_C[b, 0]
        m["stn"] = state_n[b, 0]
        m["stm"] = state_m[b, 0].reshape(1, 8)
        fl = np.zeros((1, 6), np.float32)
        for p in (1, 2, 3):
            fl[0, p - 1] = 1.0 if p >= 4 - j else 0.0
            fl[0, 3 + p - 1] = 1.0 if p <= 3 - j else 0.0
        m["flg"] = fl
        in_maps.append({kk: np.ascontiguousarray(vv) for kk, vv in m.items()})
    res = run_bass_kernel_spmd(nc, in_maps, core_ids=list(range(8)))
    B = x_prompt.shape[0]
    y_prompt = np.zeros((B, 256, D), np.float32)
    y_sample = np.zeros((2, 4096, D), np.float32)
    new_C = np.zeros((B, 1, 2, 4, 128, 128), np.float32)
    new_n = np.zeros((B, 1, 2, 4, 128), np.float32)
    new_m = np.zeros((B, 1, 2, 4), np.float32)
    for core in range(8):
        r = res.results[core]
        b, j = core // 4, core % 4
        y_prompt[2 * core:2 * core + 2] = r["yp"].reshape(2, 256, D)
        y_sample[b, j * 1024:(j + 1) * 1024] = r["ys"]
        new_C[2 * core:2 * core + 2, 0] = r["nC"]
        new_n[2 * core:2 * core + 2, 0] = r["nn"]
        new_m[2 * core:2 * core + 2, 0] = r["nm"].reshape(2, 2, 4)
    return (y_prompt, y_sample, new_C, new_n, new_m)
```
